# Optimizing a Trainium2 kernel written in Bass

```python
import math
import jax, jax.numpy as jnp
from jax import lax
import numpy as np

D_MODEL = 1024
BATCH = 4
SEQ = 8192
DEPTH = 2
DEC_BATCH = 32
DEC_SEQ = 64
PAST_LEN = 2048

CHUNK = 64
N_MIXERS = 2
N_SGU_LAYERS = (DEPTH + 1) // 2
N_GDN_LAYERS = DEPTH // 2
SGU_CHUNK = 128
D_SGU = 2 * D_MODEL
SGU_GROUPS = 8
SGU_GROUP_DIM = D_SGU // SGU_GROUPS
GDN_HEAD_DIM = 128
GDN_K_HEADS = D_MODEL // 128
GDN_V_HEADS = 2 * GDN_K_HEADS
GDN_D_QK = GDN_K_HEADS * GDN_HEAD_DIM
GDN_D_V = GDN_V_HEADS * GDN_HEAD_DIM
GDN_QKV = 2 * GDN_D_QK + GDN_D_V
GDN_IN = GDN_QKV + GDN_D_V + 2 * GDN_V_HEADS
CONV_K = 4
GDN_CHUNK = CHUNK
D_FF = ((8 * D_MODEL // 3 + 255) // 256) * 256
EPS = 1e-6

kernel_name = "hybrid_stream_sgu_gdn_step"


def rmsnorm(x, w):
    xf = x.astype(jnp.float32)
    return (xf * lax.rsqrt(jnp.mean(xf * xf, axis=-1, keepdims=True) + EPS) * w).astype(x.dtype)


def layernorm(x, g, b):
    xf = x.astype(jnp.float32)
    mu = jnp.mean(xf, axis=-1, keepdims=True)
    var = jnp.mean(jnp.square(xf - mu), axis=-1, keepdims=True)
    return ((xf - mu) * lax.rsqrt(var + 1e-5) * g + b).astype(x.dtype)


def l2norm(x):
    xf = x.astype(jnp.float32)
    return xf * lax.rsqrt(jnp.sum(xf * xf, axis=-1, keepdims=True) + EPS)


def swiglu(h, w_gate, w_up, w_down):
    return (jax.nn.silu(h @ w_gate) * (h @ w_up)) @ w_down


def sgu_mixer(h, w_in, ln_g, ln_b, w_s, b_s, w_out):
    bsz, t_len, _ = h.shape
    uv = jax.nn.gelu(h @ w_in)
    u, v = uv[..., :D_SGU], uv[..., D_SGU:]
    v = layernorm(v, ln_g, ln_b)
    blk = min(SGU_CHUNK, t_len)
    n_blk = t_len // blk
    causal = jnp.tril(jnp.ones((blk, blk), dtype=bool))
    ws = jnp.where(causal, w_s[:, :blk, :blk], 0.0).astype(v.dtype)
    vg = v.reshape(bsz, n_blk, blk, SGU_GROUPS, SGU_GROUP_DIM)
    mixed = jnp.einsum('gts,bnsgc->bntgc', ws, vg) + b_s[:, :blk].T[None, None, :, :, None]
    out = (u * mixed.reshape(bsz, t_len, D_SGU)) @ w_out
    return out, v


def causal_conv(x, buf, w):
    t_len = x.shape[1]
    xp = jnp.concatenate([buf.astype(x.dtype), x], axis=1)
    y = xp[:, 0:t_len] * w[0]
    for i in range(1, CONV_K):
        y = y + xp[:, i:i + t_len] * w[i]
    return y, xp[:, -(CONV_K - 1):]


def gdn_core(q, k, v, beta, log_a, s0):
    bsz, t_len, n_h, _ = q.shape
    dv = v.shape[-1]
    blk = min(GDN_CHUNK, t_len)
    n_blk = t_len // blk
    f32 = jnp.float32

    def to_blocks(a):
        return a.astype(f32).reshape(bsz, n_blk, blk, n_h, -1).transpose(1, 0, 3, 2, 4)

    qb, kb, vb = to_blocks(q), to_blocks(k), to_blocks(v)
    bb = to_blocks(beta[..., None])[..., 0]
    gcum = jnp.cumsum(to_blocks(log_a[..., None])[..., 0], axis=-1)
    incl = jnp.tril(jnp.ones((blk, blk), dtype=bool))
    strict = jnp.tril(jnp.ones((blk, blk), dtype=bool), -1)
    diff = gcum[..., :, None] - gcum[..., None, :]
    decay = jnp.where(incl, jnp.exp(jnp.where(incl, diff, 0.0)), 0.0)
    a_mat = jnp.where(strict, bb[..., :, None] * jnp.einsum('nbhtk,nbhsk->nbhts', kb, kb) * decay, 0.0)
    gam = jnp.exp(gcum)
    rhs = jnp.concatenate([bb[..., None] * vb, (bb * gam)[..., None] * kb], axis=-1)
    sol = lax.linalg.triangular_solve(a_mat + jnp.eye(blk, dtype=f32), rhs,
                                      left_side=True, lower=True, unit_diagonal=True)
    u_b, w_b = sol[..., :dv], sol[..., dv:]
    p_mat = jnp.einsum('nbhtk,nbhsk->nbhts', qb, kb) * decay
    q_dec = gam[..., None] * qb
    k_dec = jnp.exp(gcum[..., -1:] - gcum)[..., None] * kb
    g_last = jnp.exp(gcum[..., -1])

    def step(s, xs):
        u, w, p, qd, kd, gl = xs
        delta = u - jnp.einsum('bhlk,bhkv->bhlv', w, s)
        o = jnp.einsum('bhlk,bhkv->bhlv', qd, s) + jnp.einsum('bhts,bhsv->bhtv', p, delta)
        s = gl[..., None, None] * s + jnp.einsum('bhlk,bhlv->bhkv', kd, delta)
        return s, o

    s_final, o = lax.scan(step, s0.astype(f32), (u_b, w_b, p_mat, q_dec, k_dec, g_last))
    o = o.transpose(1, 0, 3, 2, 4).reshape(bsz, t_len, n_h, dv)
    return o, s_final


def gdn_mixer(h, conv_buf, s0, w_in, w_conv, a_log, dt_bias, w_onorm, w_out):
    bsz, t_len, _ = h.shape
    proj = h @ w_in
    qkv = proj[..., :GDN_QKV]
    z = proj[..., GDN_QKV:GDN_QKV + GDN_D_V]
    b_logit = proj[..., GDN_QKV + GDN_D_V:GDN_QKV + GDN_D_V + GDN_V_HEADS]
    a_logit = proj[..., GDN_QKV + GDN_D_V + GDN_V_HEADS:]
    qkv, new_buf = causal_conv(qkv, conv_buf, w_conv)
    qkv = jax.nn.silu(qkv)
    q = l2norm(qkv[..., :GDN_D_QK].reshape(bsz, t_len, GDN_K_HEADS, GDN_HEAD_DIM)) * (GDN_HEAD_DIM ** -0.5)
    k = l2norm(qkv[..., GDN_D_QK:2 * GDN_D_QK].reshape(bsz, t_len, GDN_K_HEADS, GDN_HEAD_DIM))
    rep = GDN_V_HEADS // GDN_K_HEADS
    q = jnp.repeat(q, rep, axis=2)
    k = jnp.repeat(k, rep, axis=2)
    v = qkv[..., 2 * GDN_D_QK:].reshape(bsz, t_len, GDN_V_HEADS, GDN_HEAD_DIM)
    beta = jax.nn.sigmoid(b_logit.astype(jnp.float32))
    log_a = -jnp.exp(a_log.astype(jnp.float32)) * jax.nn.softplus(
        a_logit.astype(jnp.float32) + dt_bias.astype(jnp.float32))
    o, s_new = gdn_core(q, k, v, beta, log_a, s0)
    o = rmsnorm(o.astype(h.dtype), w_onorm) * jax.nn.silu(z.reshape(bsz, t_len, GDN_V_HEADS, GDN_HEAD_DIM))
    return o.reshape(bsz, t_len, GDN_D_V) @ w_out, new_buf, s_new


def setup_inputs(seed: int = 0) -> dict:
    key = jax.random.key(seed)
    ks = jax.random.split(key, 24)
    f32 = jnp.float32
    nrm = lambda k, shape, scale: jax.random.normal(k, shape, f32) * scale
    dt = jnp.exp(jax.random.uniform(ks[16], (N_GDN_LAYERS, GDN_V_HEADS), f32,
                                    minval=math.log(1e-3), maxval=math.log(1e-1)))
    return {
        "x_prompt": nrm(ks[0], (BATCH, SEQ, D_MODEL), 1.0),
        "x_sample": nrm(ks[1], (DEC_BATCH, DEC_SEQ, D_MODEL), 1.0),
        "state_gdn": nrm(ks[2], (N_GDN_LAYERS, DEC_BATCH, GDN_V_HEADS, GDN_HEAD_DIM, GDN_HEAD_DIM), 0.1),
        "state_conv": nrm(ks[3], (N_GDN_LAYERS, DEC_BATCH, CONV_K - 1, GDN_QKV), 1.0),
        "norm_mix": 1.0 + nrm(ks[4], (DEPTH, D_MODEL), 0.02),
        "norm_ffn": 1.0 + nrm(ks[5], (DEPTH, D_MODEL), 0.02),
        "norm_final": 1.0 + nrm(ks[6], (D_MODEL,), 0.02),
        "sgu_w_in": nrm(ks[7], (N_SGU_LAYERS, D_MODEL, 2 * D_SGU), D_MODEL ** -0.5),
        "sgu_ln_g": 1.0 + nrm(ks[8], (N_SGU_LAYERS, D_SGU), 0.02),
        "sgu_ln_b": nrm(ks[9], (N_SGU_LAYERS, D_SGU), 0.02),
        "sgu_w_s": nrm(ks[10], (N_SGU_LAYERS, SGU_GROUPS, SGU_CHUNK, SGU_CHUNK), SGU_CHUNK ** -0.5),
        "sgu_b_s": 1.0 + nrm(ks[11], (N_SGU_LAYERS, SGU_GROUPS, SGU_CHUNK), 0.02),
        "sgu_w_out": nrm(ks[12], (N_SGU_LAYERS, D_SGU, D_MODEL), D_SGU ** -0.5),
        "gdn_w_in": nrm(ks[13], (N_GDN_LAYERS, D_MODEL, GDN_IN), D_MODEL ** -0.5),
        "gdn_w_conv": nrm(ks[14], (N_GDN_LAYERS, CONV_K, GDN_QKV), CONV_K ** -0.5),
        "gdn_a_log": jnp.log(jax.random.uniform(ks[15], (N_GDN_LAYERS, GDN_V_HEADS), f32, minval=1.0, maxval=16.0)),
        "gdn_dt_bias": dt + jnp.log(-jnp.expm1(-dt)),
        "gdn_w_onorm": 1.0 + nrm(ks[17], (N_GDN_LAYERS, GDN_HEAD_DIM), 0.02),
        "gdn_w_out": nrm(ks[18], (N_GDN_LAYERS, GDN_D_V, D_MODEL), GDN_D_V ** -0.5),
        "ffn_w_gate": nrm(ks[19], (DEPTH, D_MODEL, D_FF), D_MODEL ** -0.5),
        "ffn_w_up": nrm(ks[20], (DEPTH, D_MODEL, D_FF), D_MODEL ** -0.5),
        "ffn_w_down": nrm(ks[21], (DEPTH, D_FF, D_MODEL), D_FF ** -0.5),
    }


def reference(x_prompt, x_sample, state_gdn, state_conv, norm_mix, norm_ffn, norm_final,
              sgu_w_in, sgu_ln_g, sgu_ln_b, sgu_w_s, sgu_b_s, sgu_w_out,
              gdn_w_in, gdn_w_conv, gdn_a_log, gdn_dt_bias, gdn_w_onorm, gdn_w_out,
              ffn_w_gate, ffn_w_up, ffn_w_down):
    xp, xs = x_prompt, x_sample
    gdn_s_p, gdn_c_p, gdn_s_s, gdn_c_s, sgu_v_s = [], [], [], [], []
    for i in range(DEPTH):
        j = i // N_MIXERS
        hp = rmsnorm(xp, norm_mix[i])
        hs = rmsnorm(xs, norm_mix[i])
        if i % N_MIXERS == 0:
            sgu_args = (sgu_w_in[j], sgu_ln_g[j], sgu_ln_b[j], sgu_w_s[j], sgu_b_s[j], sgu_w_out[j])
            mp, _ = sgu_mixer(hp, *sgu_args)
            ms, v_rows = sgu_mixer(hs, *sgu_args)
            sgu_v_s.append(v_rows)
        else:
            gdn_args = (gdn_w_in[j], gdn_w_conv[j], gdn_a_log[j], gdn_dt_bias[j], gdn_w_onorm[j], gdn_w_out[j])
            zero_buf = jnp.zeros((xp.shape[0], CONV_K - 1, GDN_QKV), xp.dtype)
            zero_s = jnp.zeros((xp.shape[0], GDN_V_HEADS, GDN_HEAD_DIM, GDN_HEAD_DIM), jnp.float32)
            mp, cp, sp = gdn_mixer(hp, zero_buf, zero_s, *gdn_args)
            ms, cs, ss = gdn_mixer(hs, state_conv[j], state_gdn[j], *gdn_args)
            gdn_s_p.append(sp)
            gdn_c_p.append(cp)
            gdn_s_s.append(ss)
            gdn_c_s.append(cs)
        xp = xp + mp
        xs = xs + ms
        xp = xp + swiglu(rmsnorm(xp, norm_ffn[i]), ffn_w_gate[i], ffn_w_up[i], ffn_w_down[i])
        xs = xs + swiglu(rmsnorm(xs, norm_ffn[i]), ffn_w_gate[i], ffn_w_up[i], ffn_w_down[i])
    y_prompt = rmsnorm(xp, norm_final)
    y_sample = rmsnorm(xs, norm_final)
    new_state_gdn_prompt = jnp.stack(gdn_s_p)
    new_state_conv_prompt = jnp.stack(gdn_c_p)
    new_state_gdn_sample = jnp.stack(gdn_s_s)
    new_state_conv_sample = jnp.stack(gdn_c_s)
    new_state_sgu_v_sample = jnp.stack(sgu_v_s)
    return (y_prompt, y_sample, new_state_gdn_prompt, new_state_conv_prompt,
            new_state_gdn_sample, new_state_conv_sample, new_state_sgu_v_sample)
```

```python
import numpy as np
import concourse.bass as bass
import concourse.mybir as mybir
from concourse.bass_utils import run_bass_kernel_spmd

F32 = mybir.dt.float32
BF16 = mybir.dt.bfloat16
AF = mybir.ActivationFunctionType
ALU = mybir.AluOpType
AX = mybir.AxisListType

COMPUTE = ("pe", "act", "dve", "pool")


class T:
    __slots__ = ("h", "name", "last_w", "readers", "dma_sem", "kind")

    def __init__(self, h, name, kind="sb"):
        self.h = h
        self.name = name
        self.kind = kind
        self.last_w = {}
        self.readers = {}
        self.dma_sem = None

    def __getitem__(self, k):
        return self.h[k]


class Op:
    __slots__ = ("eng", "fn", "deps", "idx", "signal", "sigval", "is_dma", "dsem", "dval")

    def __init__(self, eng, fn):
        self.eng = eng
        self.fn = fn
        self.deps = []
        self.signal = False
        self.sigval = None
        self.is_dma = False
        self.dsem = None
        self.dval = None


class _Rec:
    def __getattr__(self, name):
        def f(*a, **k):
            self.call = (name, a, k)
            return self
        return f


def _replay(name, a, k):
    return lambda e: getattr(e, name)(*a, **k)


class Prog:
    def __init__(self, nc):
        self.nc = nc
        self.q = {e: [] for e in ("pe", "act", "dve", "pool", "sp")}
        self.tiles = []
        self._sems = []

    def sb(self, name, shape, dtype):
        t = T(self.nc.alloc_sbuf_tensor(name, list(shape), dtype), name)
        self.tiles.append(t)
        return t

    def ps(self, name, shape, dtype=F32):
        t = T(self.nc.alloc_psum_tensor(name, list(shape), dtype), name, "ps")
        self.tiles.append(t)
        return t

    def dram(self, name, shape, dtype, kind="Internal"):
        t = T(self.nc.dram_tensor(name, list(shape), dtype, kind=kind), name, "dram")
        self.tiles.append(t)
        return t

    def ext(self, h, name):
        t = T(h, name, "dram")
        self.tiles.append(t)
        return t

    def new_sem(self, name):
        cm = self.nc.semaphore(name)
        s = cm.__enter__()
        self._sems.append(cm)
        return s

    @staticmethod
    def _conf(k1, k2):
        return k1 is None or k2 is None or k1 == k2

    @staticmethod
    def _rk(x):
        return (x, None) if isinstance(x, T) else x

    def _add_deps(self, op, reads, writes, extra=()):
        deps = list(extra)
        for (t, k) in reads:
            for kk, w in t.last_w.items():
                if self._conf(k, kk):
                    deps.append(w)
        for (t, k) in writes:
            for kk, w in t.last_w.items():
                if self._conf(k, kk):
                    deps.append(w)
            for kk, rs in t.readers.items():
                if self._conf(k, kk):
                    deps.extend(rs)
        best = {}
        for d in deps:
            if d is op:
                continue
            if d.is_dma:
                key = ("dma", id(d.dsem))
                if key not in best or best[key].dval < d.dval:
                    best[key] = d
            else:
                if d.eng == op.eng and op.eng == "pe":
                    continue
                key = d.eng
                if key not in best or best[key].idx < d.idx:
                    best[key] = d
        for d in best.values():
            op.deps.append(d)
            if not d.is_dma:
                d.signal = True
        for (t, k) in reads:
            t.readers.setdefault(k, []).append(op)
        for (t, k) in writes:
            if k is None:
                t.last_w = {None: op}
                t.readers = {}
            else:
                t.last_w[k] = op
                t.readers[k] = []

    def op(self, eng, fn, reads=(), writes=(), extra=()):
        rec = _Rec()
        fn(rec)
        o = Op(eng, _replay(*rec.call))
        o.idx = len(self.q[eng])
        self._add_deps(o, [self._rk(r) for r in reads], [self._rk(w) for w in writes], extra)
        self.q[eng].append(o)
        return o

    def dma(self, out_ap, in_ap, reads=(), writes=(), sem_of=None, extra=(), eng="sp"):
        reads = [self._rk(r) for r in reads]
        writes = [self._rk(w) for w in writes]
        if sem_of is None:
            cands = [x for x in list(writes) + list(reads) if x[0].kind == "sb"]
            sem_of = cands[0] if cands else (list(writes) + list(reads))[0]
        st, sk = self._rk(sem_of)
        if st.dma_sem is None:
            st.dma_sem = {}
        if sk not in st.dma_sem:
            st.dma_sem[sk] = [self.new_sem("d%d" % len(self._sems)), 0]
        ent = st.dma_sem[sk]
        ent[1] += 16
        o = Op(eng, lambda e: e.dma_start(out=out_ap, in_=in_ap))
        o.is_dma = True
        o.dsem = ent[0]
        o.dval = ent[1]
        o.idx = len(self.q[eng])
        self._add_deps(o, reads, writes, extra)
        self.q[eng].append(o)
        return o

    def emit(self):
        nc = self.nc
        sems = {e: self.new_sem("s_" + e) for e in COMPUTE}
        for e in COMPUTE:
            c = 0
            for o in self.q[e]:
                if (not o.is_dma) and o.signal:
                    c += 1
                    o.sigval = c
        engmap = {"pe": "tensor", "act": "scalar", "dve": "vector", "pool": "gpsimd", "sp": "sync"}
        stats = {"waits": 0, "instr": 0}

        def emit_queue(ename, engine):
            known = {}
            for o in self.q[ename]:
                for d in o.deps:
                    if d.is_dma:
                        key, val, sem = ("dma", id(d.dsem)), d.dval, d.dsem
                    else:
                        key, val, sem = d.eng, d.sigval, sems[d.eng]
                    if known.get(key, 0) >= val:
                        continue
                    engine.wait_ge(sem, val)
                    stats["waits"] += 1
                    known[key] = val
                ins = o.fn(engine)
                stats["instr"] += 1
                if o.is_dma:
                    ins.then_inc(o.dsem, 16)
                elif o.signal:
                    ins.then_inc(sems[ename], 1)
            if ename == "sp":
                for t in self.tiles:
                    if t.dma_sem:
                        for ent in t.dma_sem.values():
                            engine.wait_ge(ent[0], ent[1])

        with nc.Block() as block:
            for ename in ["sp", "pool", "act", "dve", "pe"]:
                getattr(block, engmap[ename])(lambda engine, _n=ename: emit_queue(_n, engine))
        return stats


D = 1024
DSGU = 2048
DFF = 2816
GIN = 6176
EPS = 1e-6
NEG = -30000.0
WSLOT = 4096


DBG = []
NGROUPS = 2


def build_program(n_ptiles, NT, NS, n_state=0):
    assert NT % 128 == 0 and NS % 2 == 0
    nc = bass.Bass("TRN2", target_bir_lowering=False)
    P = Prog(nc)
    NP = n_ptiles * NT
    NSAMP = NS * 64
    NTMAX = max(NT, NSAMP) if NS else NT
    assert NSAMP <= NT or n_ptiles == 0
    NBMAX = NTMAX // 128

    def din(name, shape):
        return P.ext(nc.dram_tensor(name, list(shape), F32, kind="ExternalInput"), name)

    def dout(name, shape):
        return P.ext(nc.dram_tensor(name, list(shape), F32, kind="ExternalOutput"), name)

    xp = din("xp", [max(NP, 1), D])
    xq = din("xq", [max(n_state * NT, 1), D])
    flag = din("flag", [1])
    xs = din("xs", [max(NSAMP, 1), D])
    sg = din("sg", [max(NS, 1), 16, 128, 128])
    sc = din("sc", [max(NS, 1), 3, 4096])
    norm_mix = din("norm_mix", [2, D])
    norm_ffn = din("norm_ffn", [2, D])
    norm_final = din("norm_final", [D])
    sgu_w_in = din("sgu_w_in", [D, 4096])
    sgu_ln_g = din("sgu_ln_g", [DSGU])
    sgu_ln_b = din("sgu_ln_b", [DSGU])
    sgu_w_s = din("sgu_w_s", [8, 128, 128])
    sgu_b_s = din("sgu_b_s", [8, 128])
    sgu_w_out = din("sgu_w_out", [DSGU, D])
    gdn_w_in = din("gdn_w_in", [D, GIN])
    gdn_w_conv = din("gdn_w_conv", [4, 4096])
    gdn_a_log = din("gdn_a_log", [16])
    gdn_dt_bias = din("gdn_dt_bias", [16])
    gdn_w_onorm = din("gdn_w_onorm", [128])
    gdn_w_out = din("gdn_w_out", [DSGU, D])
    ffn_w_gate = din("ffn_w_gate", [2, D, DFF])
    ffn_w_up = din("ffn_w_up", [2, D, DFF])
    ffn_w_down = din("ffn_w_down", [2, DFF, D])

    yp = dout("yp", [max(NP, 1), D])
    ys = dout("ys", [max(NSAMP, 1), D])
    sgp = dout("sgp", [16, 128, 128])
    scp = dout("scp", [3, 4096])
    sgs = dout("sgs", [max(NS, 1), 16, 128, 128])
    scs = dout("scs", [max(NS, 1), 3, 4096])
    svs = dout("svs", [max(NSAMP, 1), DSGU])

    xT = P.sb("xT", [128, 8, NTMAX], F32)
    hT = P.sb("hT", [128, 8, NTMAX], BF16)
    A1 = P.sb("A1", [128, 22, NTMAX], BF16)
    kqT = P.sb("kqT", [128, 16, NTMAX], BF16)
    wsl = [P.sb("wsl%d" % i, [128, WSLOT], BF16) for i in range(4)]
    vtk = P.sb("vtk", [128, NBMAX, 2048], F32)
    vnb = P.sb("vnb", [128, 2048], BF16)
    vtok = P.sb("vtok", [128, NBMAX, 2048], BF16)
    ktok = P.sb("ktok", [128, NBMAX, 1024], BF16)
    grep = P.sb("grep", [128, 2048], F32)
    brep = P.sb("brep", [128, 2048], F32)
    wfin = P.sb("wfin", [128, 1024], F32)
    identF = P.sb("identF", [128, 128], F32)
    identB = P.sb("identB", [128, 128], BF16)
    onesF = P.sb("onesF", [128, 128], F32)
    onesB = P.sb("onesB", [128, 128], BF16)
    tril = P.sb("tril", [128, 128], F32)
    incl = P.sb("incl", [128, 128], F32)
    strict = P.sb("strict", [128, 128], F32)
    negm = P.sb("negm", [128, 128], BF16)
    selA = P.sb("selA", [128, 128], F32)
    selB = P.sb("selB", [128, 128], F32)
    blk1 = P.sb("blk1", [128, 128], F32)
    wsT1 = P.sb("wsT", [128, 8, 128], BF16)
    bsrow1 = P.sb("bsrow", [1, 1024], F32)
    wsT = [wsT1, wsT1]
    bsrow = [bsrow1, bsrow1]
    cw = P.sb("cw", [128, 4, 32], F32)
    nsc = P.sb("nsc", [128, 40], F32)
    alr = P.sb("alr", [128, 16], F32)
    dtr = P.sb("dtr", [128, 16], F32)
    hal = P.sb("hal", [128, 32, 4, 3], F32)
    stg = P.sb("stg", [128, 128], F32)
    flg = P.sb("flg", [128, 1], F32)
    xc = P.sb("xc", [128, 4, NTMAX + 12], F32)
    cacc = P.sb("cacc", [128, 4, NTMAX], F32)
    vTg = P.sb("vTg", [128, 4, NTMAX], BF16)
    Sf = [P.sb("Sf%d" % i, [128, 16, 128], F32) for i in range(2)]
    Sb = [P.sb("Sb%d" % i, [128, 16, 128], BF16) for i in range(2)]
    gts_ = [P.sb("gts%d" % i, [128, 32], F32) for i in range(NBMAX)]
    gsm_ = [P.sb("gsm%d" % i, [128, 12, 16], F32) for i in range(NBMAX)]
    gsm_unused = None
    Gtri = P.sb("Gtri", [128, 8, 128], F32)
    E1 = P.sb("E1", [128, 8, 128], F32)
    Es = P.sb("Es", [128, 8, 128], F32)
    PT = P.sb("PT", [128, 8, 128], BF16)
    Xa = [P.sb("Xa%d" % i, [128, 8, 128], BF16) for i in range(2)]
    Xt = [P.sb("Xt%d" % i, [128, 8, 128], BF16) for i in range(2)]
    Pm = [P.sb("Pm%d" % i, [128, 8, 128], BF16) for i in range(2)]
    P32 = P.sb("P32", [128, 8, 128], F32)
    kg = P.sb("kg", [128, 8, 128], BF16)
    kd = P.sb("kd", [128, 8, 128], BF16)
    nWT = P.sb("nWT", [128, 8, 128], BF16)
    qdT = P.sb("qdT", [128, 8, 128], BF16)
    dlt = P.sb("dlt", [128, 8, 128], BF16)
    onb = P.sb("onb", [128, 8, 128], BF16)
    oss = P.sb("oss", [128, 3, 8], F32)
    bst = P.sb("bst", [128, 4, 6], F32)
    bmv = P.sb("bmv", [128, 4], F32)

    pp = [P.ps("pp%d" % i, [128, 1024], F32) for i in range(4)]
    ppc = [0]

    def nps():
        t = pp[ppc[0] % 4]
        ppc[0] += 1
        return t

    class PH:
        def __init__(self, t, half):
            self.t, self.off, self.key = t, half * 512, (t, ("h", half))

        def ap(self, a, b, rows=slice(None)):
            return self.t.h[rows, self.off + a:self.off + b]

        def apb(self, a, b):
            return self.t.h[:, :].bitcast(BF16)[:, 2 * self.off + a:2 * self.off + b]

    p1c = [0]

    def nps1():
        i = p1c[0]
        p1c[0] += 1
        return PH(pp[(i // 2) % 4], i % 2)

    def pbf(t):
        return t.h[:, :].bitcast(BF16)

    wsc = [0]

    def nws():
        t = wsl[wsc[0] % 4]
        wsc[0] += 1
        return t

    dbg_done = set()

    def dbg(name, t, ap, shape, dtype=F32):
        if name not in DBG or name in dbg_done:
            return
        dbg_done.add(name)
        o = P.ext(nc.dram_tensor("dbg_" + name, list(shape), dtype, kind="ExternalOutput"), "dbg_" + name)
        P.dma(o.h.ap(), ap, reads=[t], writes=[o])

    def pool(fn, reads=(), writes=()):
        return P.op("pool", fn, reads, writes)

    def dve(fn, reads=(), writes=()):
        return P.op("dve", fn, reads, writes)

    def act(fn, reads=(), writes=()):
        return P.op("act", fn, reads, writes)

    def pe(fn, reads=(), writes=()):
        return P.op("pe", fn, reads, writes)

    pool(lambda e: e.memset(onesF[:], 1.0), writes=[onesF])
    pool(lambda e: e.memset(onesB[:], 1.0), writes=[onesB])
    pool(lambda e: e.memset(identF[:], 0.0), writes=[identF])
    pool(lambda e: e.affine_select(out=identF[:], in_=onesF[:], pattern=[[-1, 128]], base=0, channel_multiplier=1,
                                   compare_op=ALU.is_equal, fill=0.0), reads=[onesF], writes=[identF])
    pool(lambda e: e.tensor_copy(out=identB[:], in_=identF[:]), reads=[identF], writes=[identB])
    pool(lambda e: e.memset(tril[:], 0.0), writes=[tril])
    pool(lambda e: e.affine_select(out=tril[:], in_=onesF[:], pattern=[[1, 128]], base=0, channel_multiplier=-1,
                                   compare_op=ALU.is_ge, fill=0.0), reads=[onesF], writes=[tril])
    pool(lambda e: e.tensor_copy(out=incl[:], in_=tril[:]), reads=[tril], writes=[incl])
    pool(lambda e: e.memset(incl[0:64, 64:128], 0.0), writes=[incl])
    pool(lambda e: e.memset(strict[:], 0.0), writes=[strict])
    pool(lambda e: e.affine_select(out=strict[:], in_=onesF[:], pattern=[[1, 128]], base=0, channel_multiplier=-1,
                                   compare_op=ALU.is_gt, fill=0.0), reads=[onesF], writes=[strict])
    pool(lambda e: e.memset(strict[0:64, 64:128], 0.0), writes=[strict])
    pool(lambda e: e.tensor_scalar(out=negm[:], in0=incl[:], scalar1=-1.0, scalar2=-NEG, op0=ALU.add, op1=ALU.mult),
         reads=[incl], writes=[negm])
    pool(lambda e: e.memset(selA[:], 0.0), writes=[selA])
    pool(lambda e: e.memset(selA[0:64, :], 1.0), writes=[selA])
    pool(lambda e: e.memset(selB[:], 0.0), writes=[selB])
    pool(lambda e: e.memset(selB[64:128, :], 1.0), writes=[selB])
    pool(lambda e: e.memset(blk1[:], 0.0), writes=[blk1])
    pool(lambda e: e.memset(blk1[0:64, 0:64], 1.0), writes=[blk1])
    pool(lambda e: e.memset(blk1[64:128, 64:128], 1.0), writes=[blk1])
    pool(lambda e: e.memset(hal[:], 0.0), writes=[hal])

    def bcast_rows(ap1d, n):
        return ap1d.partition_broadcast(128)

    P.dma(grep[:], sgu_ln_g.h.ap().partition_broadcast(128), reads=[sgu_ln_g], writes=[grep])
    P.dma(brep[:], sgu_ln_b.h.ap().partition_broadcast(128), reads=[sgu_ln_b], writes=[brep])
    P.dma(wfin[:], norm_final.h.ap().partition_broadcast(128), reads=[norm_final], writes=[wfin])
    P.dma(alr[:], gdn_a_log.h.ap().partition_broadcast(128), reads=[gdn_a_log], writes=[alr])
    P.dma(dtr[:], gdn_dt_bias.h.ap().partition_broadcast(128), reads=[gdn_dt_bias], writes=[dtr])
    P.dma(flg[:], flag.h.ap().partition_broadcast(128), reads=[flag], writes=[flg])
    act(lambda e: e.activation(out=alr[:], in_=alr[:], func=AF.Exp), reads=[alr], writes=[alr])
    dve(lambda e: e.tensor_scalar(out=alr[:], in0=alr[:], scalar1=-1.0, scalar2=None, op0=ALU.mult), reads=[alr], writes=[alr])

    def small_T(dst_ap, dst_t, src_ap, src_t, R):
        P.dma(stg[0:R, :], src_ap, reads=[src_t], writes=[stg])
        pt = nps()
        pe(lambda e: e.transpose(pt.h[:, 0:R], stg[0:R, :], identF[0:R, 0:R]), reads=[stg, identF], writes=[pt])
        act(lambda e: e.copy(out=dst_ap, in_=pt.h[:, 0:R]), reads=[pt], writes=[dst_t])

    small_T(cw[:].rearrange("p i c -> p (i c)"), cw, gdn_w_conv.h.ap().rearrange("i (c p) -> (i c) p", p=128), gdn_w_conv, 128)
    small_T(nsc[:, 0:16], nsc, norm_mix.h.ap().rearrange("l (k p) -> (l k) p", p=128), norm_mix, 16)
    small_T(nsc[:, 16:32], nsc, norm_ffn.h.ap().rearrange("l (k p) -> (l k) p", p=128), norm_ffn, 16)
    small_T(nsc[:, 32:33], nsc, gdn_w_onorm.h.ap().rearrange("(o p) -> o p", o=1), gdn_w_onorm, 1)

    def setup_sgu_kind(kind):
        if kind == 0:
            P.dma(vtk[:, 0, 0:1024].rearrange("p (g s) -> p g s", g=8), sgu_w_s.h.ap().rearrange("g t s -> t g s"),
                  reads=[sgu_w_s], writes=[(vtk, 0)])
            P.dma(bsrow[0][:], sgu_b_s.h.ap().rearrange("(o g) t -> o (g t)", o=1), reads=[sgu_b_s], writes=[bsrow[0]])
        else:
            pool(lambda e: e.memset(vtk[:, 0, 0:1024], 0.0), writes=[(vtk, 0)])
            v3 = vtk[:, 0, 0:1024].rearrange("p (g s) -> p g s", g=8)
            P.dma(v3[0:64, :, 0:64], sgu_w_s.h.ap().rearrange("g t s -> t g s")[0:64, :, 0:64], reads=[sgu_w_s], writes=[(vtk, 0)])
            P.dma(v3[64:128, :, 64:128], sgu_w_s.h.ap().rearrange("g t s -> t g s")[0:64, :, 0:64], reads=[sgu_w_s], writes=[(vtk, 0)])
            b4 = bsrow[1][:].rearrange("o (g h t) -> o g h t", g=8, h=2)
            for hh in range(2):
                P.dma(b4[:, :, hh, :], sgu_b_s.h.ap().rearrange("(o g) t -> o g t", o=1)[:, :, 0:64], reads=[sgu_b_s], writes=[bsrow[1]])
        msk = tril if kind == 0 else incl
        for g4 in range(2):
            pt = nps()
            for j in range(4):
                g = g4 * 4 + j
                pe(lambda e, g=g, j=j, pt=pt: e.transpose(pt.h[:, j * 128:(j + 1) * 128], vtk[:, 0, g * 128:(g + 1) * 128], identF[:]),
                   reads=[(vtk, 0), identF], writes=[pt])
            dve(lambda e, g4=g4, pt=pt, kind=kind, msk=msk: e.tensor_tensor(
                out=wsT[kind][:, g4 * 4:(g4 + 1) * 4, :], in0=pt.h[:, 0:512].rearrange("p (g t) -> p g t", g=4),
                in1=msk[:].unsqueeze(1).to_broadcast([128, 4, 128]), op=ALU.mult), reads=[pt, msk], writes=[wsT[kind]])

    wdefs = [
        ("w_sgu_in", sgu_w_in, sgu_w_in.h.ap(), D, 4096, (0, 0)),
        ("w_sgu_out", sgu_w_out, sgu_w_out.h.ap(), DSGU, D, None),
        ("w_gate0", ffn_w_gate, ffn_w_gate.h.ap()[0], D, DFF, (16, 0)),
        ("w_up0", ffn_w_up, ffn_w_up.h.ap()[0], D, DFF, (16, 0)),
        ("w_down0", ffn_w_down, ffn_w_down.h.ap()[0], DFF, D, None),
        ("w_gdn_in", gdn_w_in, gdn_w_in.h.ap(), D, GIN, (8, 0)),
        ("w_gdn_out", gdn_w_out, gdn_w_out.h.ap(), DSGU, D, (32, 1)),
        ("w_gate1", ffn_w_gate, ffn_w_gate.h.ap()[1], D, DFF, (24, 0)),
        ("w_up1", ffn_w_up, ffn_w_up.h.ap()[1], D, DFF, (24, 0)),
        ("w_down1", ffn_w_down, ffn_w_down.h.ap()[1], DFF, D, None),
    ]
    WS = {}
    pro_ops = []
    cnt = 0
    engs = ["act", "dve"]
    stf = [(vtk, 0, lambda w: vtk[:, 0, 0:w]), (vtk, 1, lambda w: vtk[:, 1, 0:w]),
           (Sf[0], None, lambda w: Sf[0][:].rearrange("p h d -> p (h d)")[:, 0:w]), (Sf[1], None, lambda w: Sf[1][:].rearrange("p h d -> p (h d)")[:, 0:w])]
    stb = [(vtok, 0, lambda w: vtok[:, 0, 0:w]), (vtok, 1, lambda w: vtok[:, 1, 0:w]),
           (Sb[0], None, lambda w: Sb[0][:].rearrange("p h d -> p (h d)")[:, 0:w]), (Sb[1], None, lambda w: Sb[1][:].rearrange("p h d -> p (h d)")[:, 0:w])]
    for (name, srct, srcap, K, C, fold) in wdefs:
        scr = P.dram(name, [K, C], BF16)
        WS[name] = (scr, K // 128, C)
        for kc in range(K // 128):
            for c0 in range(0, C, 2048):
                cwid = min(2048, C - c0)
                i = cnt % 4
                cnt += 1
                ft, fk, fap = stf[i]
                bt, bk, bap = stb[i]
                fkey = (ft, fk) if fk is not None else ft
                bkey = (bt, bk) if bk is not None else bt
                P.dma(fap(cwid), srcap[kc * 128:(kc + 1) * 128, c0:c0 + cwid], reads=[srct], writes=[fkey])
                en = engs[cnt % 2]
                if fold is None:
                    if en == "act":
                        act(lambda e: e.copy(out=bap(cwid), in_=fap(cwid)), reads=[fkey], writes=[bkey])
                    else:
                        dve(lambda e: e.tensor_copy(out=bap(cwid), in_=fap(cwid)), reads=[fkey], writes=[bkey])
                else:
                    col = fold[0] + (kc if fold[1] == 0 else 0)
                    if en == "act":
                        act(lambda e: e.activation(out=bap(cwid), in_=fap(cwid), func=AF.Copy, scale=nsc[:, col:col + 1]), reads=[fkey, nsc], writes=[bkey])
                    else:
                        dve(lambda e: e.tensor_scalar(out=bap(cwid), in0=fap(cwid), scalar1=nsc[:, col:col + 1], scalar2=None, op0=ALU.mult),
                            reads=[fkey, nsc], writes=[bkey])
                o = P.dma(scr.h.ap()[kc * 128:(kc + 1) * 128, c0:c0 + cwid], bap(cwid), reads=[bkey], writes=[(scr, (kc, c0))], eng="act")
                pro_ops.append(o)
    fence = P.dma(stg[0:1, 0:4], onesF[0:1, 0:4], reads=[onesF], writes=[stg], extra=pro_ops)
    for name in WS:
        WS[name][0].last_w = {None: fence}
        WS[name][0].readers = {}

    def wpieces(name, c_lo, c_hi, KC):
        scr = WS[name][0]
        pcmax = (WSLOT // KC) // 128 * 128
        c0 = c_lo
        while c0 < c_hi:
            pc = min(pcmax, c_hi - c0)
            slot = nws()
            view = slot[:, 0:KC * pc].rearrange("p (k c) -> p k c", k=KC)
            P.dma(view, scr.h.ap().rearrange("(k p) c -> p k c", p=128)[:, :, c0:c0 + pc], reads=[scr], writes=[slot])
            yield slot, view, c0, pc
            c0 += pc

    def linear_fm(name, c_lo, c_hi, KC, rhs_t, rhs_ap, NTt, evac, oc_base=0, piped=False):
        gmax = min(4, max(1, 1024 // NTt))
        per_bank = max(1, 512 // NTt)
        pend = {"b1": None, "b2": None, "b2n": None}
        for slot, view, c0, pc in wpieces(name, c_lo, c_hi, KC):
            nch = pc // 128
            j0 = 0
            while j0 < nch:
                n = min(gmax, nch - j0)
                pt = nps()
                for j in range(n):
                    off = (j // per_bank) * 512 + (j % per_bank) * NTt
                    for k in range(KC):
                        pe(lambda e, pt=pt, off=off, view=view, k=k, jj=j0 + j: e.matmul(
                            pt.h[:, off:off + NTt], lhsT=view[:, k, jj * 128:(jj + 1) * 128], rhs=rhs_ap(k),
                            start=(k == 0), stop=(k == KC - 1)), reads=[slot, rhs_t], writes=[pt])
                if per_bank * NTt == 512 or n <= per_bank:
                    ap3 = pt.h[:, 0:n * NTt].rearrange("p (j t) -> p j t", j=n)
                else:
                    raise NotImplementedError
                called = [False]

                def mid():
                    called[0] = True
                    if pend["b1"] is not None:
                        pend["b1"]()
                    if pend["b2"] is not None:
                        pend["b2"]()
                if piped:
                    d_ = evac(oc_base + (c0 - c_lo) // 128 + j0, n, pt, ap3, mid)
                else:
                    d_ = evac(oc_base + (c0 - c_lo) // 128 + j0, n, pt, ap3)
                if not called[0]:
                    mid()
                pend["b2"] = pend["b2n"]
                pend["b1"], pend["b2n"] = d_ if d_ is not None else (None, None)
                j0 += n
        for k_ in ("b1", "b2"):
            if pend[k_] is not None:
                pend[k_]()
        if pend["b2n"] is not None:
            pend["b2n"]()

    def rmsnorm_to_hT(NTt):
        act(lambda e: e.activation(out=hT[:, :, 0:NTt], in_=xT[:, :, 0:NTt], func=AF.Square), reads=[xT], writes=[hT])
        pt = nps()
        for k in range(8):
            pe(lambda e, k=k, pt=pt: e.matmul(pt.h[:, 0:NTt], lhsT=onesB[:], rhs=hT[:, k, 0:NTt], start=(k == 0), stop=(k == 7)),
               reads=[onesB, hT], writes=[pt])
        act(lambda e, pt=pt: e.activation(out=cacc[:, 0, 0:NTt], in_=pt.h[:, 0:NTt], func=AF.Ln, scale=1.0 / D, bias=EPS), reads=[pt], writes=[cacc])
        act(lambda e: e.activation(out=cacc[:, 0, 0:NTt], in_=cacc[:, 0, 0:NTt], func=AF.Exp, scale=-0.5), reads=[cacc], writes=[cacc])
        dve(lambda e: e.tensor_tensor(out=hT[:, :, 0:NTt], in0=xT[:, :, 0:NTt], in1=cacc[:, 0, 0:NTt].unsqueeze(1).to_broadcast([128, 8, NTt]),
                                      op=ALU.mult), reads=[xT, cacc], writes=[hT])

    def resid_evac(NTt):
        def ev(oc0, n, pt, ap3):
            dve(lambda e: e.tensor_tensor(out=xT[:, oc0:oc0 + n, 0:NTt], in0=ap3, in1=xT[:, oc0:oc0 + n, 0:NTt], op=ALU.add),
                reads=[pt, xT], writes=[xT])
        return ev

    def ffn(layer, NTt):
        rmsnorm_to_hT(NTt)
        hrhs = lambda k: hT[:, k, 0:NTt]

        def ev_gate(oc0, n, pt, ap3):
            act(lambda e: e.activation(out=A1[:, oc0:oc0 + n, 0:NTt], in_=ap3, func=AF.Silu), reads=[pt], writes=[(A1, c) for c in range(oc0, oc0 + n)])

        def ev_up(oc0, n, pt, ap3):
            dve(lambda e: e.tensor_tensor(out=A1[:, oc0:oc0 + n, 0:NTt], in0=ap3, in1=A1[:, oc0:oc0 + n, 0:NTt], op=ALU.mult),
                reads=[pt] + [(A1, c) for c in range(oc0, oc0 + n)], writes=[(A1, c) for c in range(oc0, oc0 + n)])
        linear_fm("w_gate%d" % layer, 0, DFF, 8, hT, hrhs, NTt, ev_gate)
        linear_fm("w_up%d" % layer, 0, DFF, 8, hT, hrhs, NTt, ev_up)
        linear_fm("w_down%d" % layer, 0, D, 22, A1, lambda k: A1[:, k, 0:NTt], NTt, resid_evac(NTt))

    XW = NTMAX + 12
    convsets = [
        (xc.h[:, :, :], xc, cacc.h[:, :, :], cacc, vTg.h[:, :, :], vTg),
        (vtk.h[:, 0, 0:4 * XW].rearrange("p (j w) -> p j w", j=4), (vtk, 0),
         vnb.h[:, :].bitcast(F32)[:, 0:4 * NTMAX].rearrange("p (j t) -> p j t", j=4), vnb,
         vtk.h[:, 0, 4 * XW + 16:4 * XW + 16 + 2 * NTMAX].bitcast(BF16).rearrange("p (j t) -> p j t", j=4), (vtk, 0)),
    ]
    convc = [0]

    def run_tile(kind, tok0, NTt, first, last, seqs, mode="full", apply_flag=False, need_q_halo=False):
        full = mode == "full"
        nb = NTt // 128
        src = (xp if full else xq) if kind == 0 else xs
        dst = yp if kind == 0 else ys
        nseg = 1 if kind == 0 else NTt // 64
        L = NTt // nseg
        if apply_flag:
            dve(lambda e: e.tensor_scalar(out=Sf[0][:], in0=Sf[0][:], scalar1=flg[:, 0:1], scalar2=None, op0=ALU.mult), reads=[Sf[0], flg], writes=[Sf[0]])
            act(lambda e: e.copy(out=Sb[0][:], in_=Sf[0][:]), reads=[Sf[0]], writes=[Sb[0]])
            dve(lambda e: e.tensor_scalar(out=hal[:], in0=hal[:], scalar1=flg[:, 0:1], scalar2=None, op0=ALU.mult), reads=[hal, flg], writes=[hal])
        for b in range(nb):
            P.dma(vtk[:, 1, 0:1024], src.h.ap()[tok0 + b * 128: tok0 + (b + 1) * 128, :], reads=[src], writes=[(vtk, 1)])
            for hf in range(2):
                pt = nps()
                for j in range(4):
                    k = hf * 4 + j
                    pe(lambda e, pt=pt, j=j, k=k: e.transpose(pt.h[:, j * 128:(j + 1) * 128], vtk[:, 1, k * 128:(k + 1) * 128], identF[:]),
                       reads=[(vtk, 1), identF], writes=[pt])
                act(lambda e, pt=pt, hf=hf, b=b: e.copy(out=xT[:, hf * 4:(hf + 1) * 4, b * 128:(b + 1) * 128],
                                                        in_=pt.h[:, 0:512].rearrange("p (j t) -> p j t", j=4)), reads=[pt], writes=[xT])
        rmsnorm_to_hT(NTt)
        hrhs = lambda k: hT[:, k, 0:NTt]

        def ev_u(oc0, n, pt, ap3):
            act(lambda e: e.activation(out=A1[:, oc0:oc0 + n, 0:NTt], in_=ap3, func=AF.Gelu), reads=[pt], writes=[(A1, c) for c in range(oc0, oc0 + n)])
        linear_fm("w_sgu_in", 0, 2048, 8, hT, hrhs, NTt, ev_u)
        for slot, view, c0, pc in wpieces("w_sgu_in", 2048, 4096, 8):
            for b in range(nb):
                for q in range(pc // 512):
                    pt = nps()
                    for k in range(8):
                        pe(lambda e, pt=pt, k=k, b=b, q=q, view=view: e.matmul(pt.h[:, 0:512], lhsT=hT[:, k, b * 128:(b + 1) * 128],
                                                                               rhs=view[:, k, q * 512:(q + 1) * 512], start=(k == 0), stop=(k == 7)),
                           reads=[slot, hT], writes=[pt])
                    cc = c0 - 2048 + q * 512
                    act(lambda e, pt=pt, b=b, cc=cc: e.activation(out=vtk[:, b, cc:cc + 512], in_=pt.h[:, 0:512], func=AF.Gelu),
                        reads=[pt], writes=[(vtk, b)])
        for b in range(nb):
            for q in range(4):
                dve(lambda e, b=b, q=q: e.bn_stats(out=bst[:, q, :], in_=vtk[:, b, q * 512:(q + 1) * 512]), reads=[(vtk, b)], writes=[bst])
            dve(lambda e: e.bn_aggr(out=bmv[:, 0:2], in_=bst[:].rearrange("p q s -> p (q s)")), reads=[bst], writes=[bmv])
            act(lambda e: e.activation(out=bmv[:, 2:3], in_=bmv[:, 1:2], func=AF.Ln, bias=1e-5), reads=[bmv], writes=[bmv])
            act(lambda e: e.activation(out=bmv[:, 2:3], in_=bmv[:, 2:3], func=AF.Exp, scale=-0.5), reads=[bmv], writes=[bmv])
            dve(lambda e, b=b: e.scalar_tensor_tensor(out=vtk[:, b, :], in0=vtk[:, b, :], scalar=bmv[:, 0:1], in1=grep[:],
                                                      op0=ALU.subtract, op1=ALU.mult), reads=[(vtk, b), bmv, grep], writes=[(vtk, b)])
            dve(lambda e, b=b: e.scalar_tensor_tensor(out=vtk[:, b, :], in0=vtk[:, b, :], scalar=bmv[:, 2:3], in1=brep[:],
                                                      op0=ALU.mult, op1=ALU.add), reads=[(vtk, b), bmv, brep], writes=[(vtk, b)])
            if kind == 1:
                P.dma(svs.h.ap()[tok0 + b * 128: tok0 + (b + 1) * 128, :], vtk[:, b, :], reads=[(vtk, b)], writes=[svs])
        for b in range(nb):
            act(lambda e, b=b: e.copy(out=vnb[:], in_=vtk[:, b, :]), reads=[(vtk, b)], writes=[vnb])
            for c4 in range(4):
                pt = nps()
                brow = bsrow[kind][0:1, c4 * 256:(c4 + 1) * 256].rearrange("o (g t) -> o g t", g=2).unsqueeze(2).to_broadcast([1, 2, 2, 128])
                pe(lambda e, pt=pt, brow=brow: e.matmul(pt.h[:, 0:512], lhsT=onesF[0:1, :], rhs=brow, start=True, stop=False),
                   reads=[onesF, bsrow[kind]], writes=[pt])
                for j in range(4):
                    ci = c4 * 4 + j
                    g = ci // 2
                    pe(lambda e, pt=pt, j=j, ci=ci, g=g: e.matmul(pt.h[:, j * 128:(j + 1) * 128], lhsT=vnb[:, ci * 128:(ci + 1) * 128],
                                                                 rhs=wsT[kind][:, g, :], start=False, stop=(j == 3)), reads=[vnb, wsT[kind]], writes=[pt])
                dve(lambda e, pt=pt, c4=c4, b=b: e.tensor_tensor(out=A1[:, c4 * 4:(c4 + 1) * 4, b * 128:(b + 1) * 128],
                                                                 in0=pt.h[:, 0:512].rearrange("p (j t) -> p j t", j=4),
                                                                 in1=A1[:, c4 * 4:(c4 + 1) * 4, b * 128:(b + 1) * 128], op=ALU.mult),
                    reads=[pt] + [(A1, c) for c in range(c4 * 4, c4 * 4 + 4)], writes=[(A1, c) for c in range(c4 * 4, c4 * 4 + 4)])
        linear_fm("w_sgu_out", 0, D, 16, A1, lambda k: A1[:, k, 0:NTt], NTt, resid_evac(NTt))
        ffn(0, NTt)
        rmsnorm_to_hT(NTt)
        if kind == 1:
            for c4 in range(8):
                P.dma(vtk[0:nseg * 3, 1, 0:512], sc.h.ap().rearrange("s i c -> (s i) c")[seqs[0] * 3:(seqs[0] + nseg) * 3, c4 * 512:(c4 + 1) * 512],
                      reads=[sc], writes=[(vtk, 1)])
                pt = nps()
                for j in range(4):
                    pe(lambda e, pt=pt, j=j: e.transpose(pt.h[:, j * 16:j * 16 + nseg * 3], vtk[0:nseg * 3, 1, j * 128:(j + 1) * 128],
                                                       identF[0:nseg * 3, 0:nseg * 3]), reads=[(vtk, 1), identF], writes=[pt])
                act(lambda e, pt=pt, c4=c4: e.copy(out=hal[:, c4 * 4:(c4 + 1) * 4, 0:nseg, :],
                                                   in_=pt.h[:, 0:64].rearrange("p (j x) -> p j x", j=4)[:, :, 0:nseg * 3].rearrange("p j (s i) -> p j s i", i=3)),
                    reads=[pt], writes=[hal])
        W = L + 3

        def ev_qkv(oc0, n, pt, ap3, mid):
            XC, XCK, CA, CAK, VT, VTK = convsets[convc[0] % 2]
            convc[0] += 1
            xv = XC[:, 0:n, 0:nseg * W].rearrange("p j (s w) -> p j s w", s=nseg)
            act(lambda e: e.copy(out=xv[:, :, :, 3:3 + L], in_=ap3.rearrange("p j (s l) -> p j s l", s=nseg)), reads=[pt], writes=[XCK])
            pool(lambda e: e.tensor_copy(out=xv[:, :, :, 0:3], in_=hal[:, oc0:oc0 + n, 0:nseg, :]), reads=[hal], writes=[XCK])
            pool(lambda e: e.tensor_copy(out=hal[:, oc0:oc0 + n, 0:nseg, :], in_=xv[:, :, :, L:L + 3]), reads=[XCK], writes=[hal])
            mid()
            avs = [CA[:, j, 0:NTt].rearrange("p (s l) -> p s l", s=nseg) for j in range(n)]
            CK = [(CAK if isinstance(CAK, tuple) else (CAK, None)) for j in range(n)]
            for j in range(n):
                c = oc0 + j
                dve(lambda e, j=j, c=c: e.tensor_scalar(out=avs[j], in0=xv[:, j, :, 0:L], scalar1=cw[:, 0, c:c + 1], scalar2=None, op0=ALU.mult),
                    reads=[XCK, cw], writes=[(CK[j][0], ("c", j))])
            for i in range(1, 4):
                for j in range(n):
                    c = oc0 + j
                    dve(lambda e, j=j, c=c, i=i: e.scalar_tensor_tensor(out=avs[j], in0=xv[:, j, :, i:i + L], scalar=cw[:, i, c:c + 1], in1=avs[j],
                                                                      op0=ALU.mult, op1=ALU.add), reads=[XCK, cw, (CK[j][0], ("c", j))], writes=[(CK[j][0], ("c", j))])
            KQK = [(kqT, c) for c in range(oc0, oc0 + n)]
            if oc0 < 16:
                act(lambda e: e.activation(out=kqT[:, oc0:oc0 + n, 0:NTt], in_=CA[:, 0:n, 0:NTt], func=AF.Silu), reads=[CAK], writes=KQK)
                act(lambda e: e.activation(out=VT[:, 0:n, 0:NTt], in_=kqT[:, oc0:oc0 + n, 0:NTt], func=AF.Square), reads=KQK, writes=[VTK])

                def partB1():
                    p2 = nps()
                    per_bank = max(1, 512 // NTt)
                    for j in range(n):
                        off = (j // per_bank) * 512 + (j % per_bank) * NTt
                        pe(lambda e, j=j, off=off, p2=p2: e.matmul(p2.h[:, off:off + NTt], lhsT=onesB[:], rhs=VT[:, j, 0:NTt], start=True, stop=True),
                           reads=[onesB, VTK], writes=[p2])
                    a3 = p2.h[:, 0:n * NTt].rearrange("p (j t) -> p j t", j=n)
                    act(lambda e: e.activation(out=CA[:, 0:n, 0:NTt], in_=a3, func=AF.Ln, bias=EPS), reads=[p2], writes=[CAK])
                    qb = -0.5 * float(np.log(128.0)) if oc0 < 8 else 0.0
                    act(lambda e: e.activation(out=CA[:, 0:n, 0:NTt], in_=CA[:, 0:n, 0:NTt], func=AF.Exp, scale=-0.5, bias=qb), reads=[CAK], writes=[CAK])

                def partB2():
                    dve(lambda e: e.tensor_tensor(out=kqT[:, oc0:oc0 + n, 0:NTt], in0=kqT[:, oc0:oc0 + n, 0:NTt], in1=CA[:, 0:n, 0:NTt], op=ALU.mult),
                        reads=KQK + [CAK], writes=KQK)
                    if oc0 >= 8:
                        for b in range(nb):
                            p3 = nps()
                            for j in range(n):
                                pe(lambda e, j=j, b=b, p3=p3: e.transpose(pbf(p3)[:, j * 128:(j + 1) * 128], kqT[:, oc0 + j, b * 128:(b + 1) * 128], identB[:]),
                                   reads=KQK + [identB], writes=[p3])
                            act(lambda e, b=b, p3=p3: e.copy(out=ktok[:, b, (oc0 - 8) * 128:(oc0 - 8 + n) * 128], in_=pbf(p3)[:, 0:n * 128]),
                                reads=[p3], writes=[(ktok, b)])
                return (partB1, partB2)
            else:
                act(lambda e: e.activation(out=VT[:, 0:n, 0:NTt], in_=CA[:, 0:n, 0:NTt], func=AF.Silu), reads=[CAK], writes=[VTK])

                def partB2():
                    for b in range(nb):
                        p3 = nps()
                        for j in range(n):
                            pe(lambda e, j=j, b=b, p3=p3: e.transpose(pbf(p3)[:, j * 128:(j + 1) * 128], VT[:, j, b * 128:(b + 1) * 128], identB[:]),
                               reads=[VTK, identB], writes=[p3])
                        act(lambda e, b=b, p3=p3: e.copy(out=vtok[:, b, (oc0 - 16) * 128:(oc0 - 16 + n) * 128], in_=pbf(p3)[:, 0:n * 128]),
                            reads=[p3], writes=[(vtok, b)])
                return (None, partB2)
        if full or need_q_halo:
            linear_fm("w_gdn_in", 0, 4096, 8, hT, hrhs, NTt, ev_qkv, piped=True)
        else:
            linear_fm("w_gdn_in", 1024, 4096, 8, hT, hrhs, NTt, ev_qkv, oc_base=8, piped=True)
        if (kind == 1 or last) and full:
            odst = scs if kind == 1 else scp
            orow = odst.h.ap().rearrange("s i c -> (s i) c") if kind == 1 else odst.h.ap()
            r0 = seqs[0] * 3 if kind == 1 else 0
            for c4 in range(8):
                pt = nps()
                for j in range(4):
                    pe(lambda e, pt=pt, j=j, c4=c4: e.transpose(pt.h[0:nseg * 3, j * 128:(j + 1) * 128],
                                                              hal[:, c4 * 4 + j, 0:nseg, :].rearrange("p s i -> p (s i)"), identF[:]),
                       reads=[hal, identF], writes=[pt])
                act(lambda e, pt=pt: e.copy(out=vtk[0:nseg * 3, 1, 0:512], in_=pt.h[0:nseg * 3, 0:512]), reads=[pt], writes=[(vtk, 1)])
                P.dma(orow[r0:r0 + nseg * 3, c4 * 512:(c4 + 1) * 512], vtk[0:nseg * 3, 1, 0:512], reads=[(vtk, 1)], writes=[odst])

        def ev_z(oc0, n, pt, ap3):
            act(lambda e: e.activation(out=A1[:, oc0:oc0 + n, 0:NTt], in_=ap3, func=AF.Silu), reads=[pt], writes=[(A1, c) for c in range(oc0, oc0 + n)])
        if full:
            linear_fm("w_gdn_in", 4096, 6144, 8, hT, hrhs, NTt, ev_z)
        gslot, gview, _, _ = next(wpieces("w_gdn_in", 6144, 6176, 8))
        if kind == 0 and first and not apply_flag:
            pool(lambda e: e.memset(Sf[0][:], 0.0), writes=[Sf[0]])
            pool(lambda e: e.memset(Sb[0][:], 0.0), writes=[Sb[0]])
        for b in range(nb):
            pg = nps()
            for k in range(8):
                pe(lambda e, k=k, b=b, pg=pg: e.matmul(pg.h[:, 0:32], lhsT=hT[:, k, b * 128:(b + 1) * 128], rhs=gview[:, k, 0:32],
                                                       start=(k == 0), stop=(k == 7)), reads=[hT, gslot], writes=[pg])
            act(lambda e, pg=pg: e.copy(out=gts_[b][:], in_=pg.h[:, 0:32]), reads=[pg], writes=[gts_[b]])
            G = lambda i, b=b: gsm_[b][:, i, :]
            act(lambda e: e.activation(out=G(0), in_=gts_[b][:, 0:16], func=AF.Exp, scale=-1.0), reads=[gts_[b]], writes=[gsm_[b]])
            dve(lambda e: e.tensor_scalar(out=G(0), in0=G(0), scalar1=1.0, scalar2=None, op0=ALU.add), reads=[gsm_[b]], writes=[gsm_[b]])
            dve(lambda e: e.reciprocal(out=G(1), in_=G(0)), reads=[gsm_[b]], writes=[gsm_[b]])
            dve(lambda e: e.tensor_scalar(out=G(2), in0=G(1), scalar1=-1.0, scalar2=None, op0=ALU.mult), reads=[gsm_[b]], writes=[gsm_[b]])
            dve(lambda e: e.tensor_tensor(out=G(9), in0=gts_[b][:, 16:32], in1=dtr[:], op=ALU.add), reads=[gts_[b], dtr], writes=[gsm_[b]])
            act(lambda e: e.activation(out=G(9), in_=G(9), func=AF.Exp), reads=[gsm_[b]], writes=[gsm_[b]])
            act(lambda e: e.activation(out=G(9), in_=G(9), func=AF.Ln, bias=1.0), reads=[gsm_[b]], writes=[gsm_[b]])
            dve(lambda e: e.tensor_tensor(out=G(3), in0=G(9), in1=alr[:], op=ALU.mult), reads=[gsm_[b], alr], writes=[gsm_[b]])
            pc_ = nps()
            pe(lambda e, pc_=pc_: e.matmul(pc_.h[:, 0:16], lhsT=incl[:], rhs=G(3), start=True, stop=True), reads=[incl, gsm_[b]], writes=[pc_])
            pe(lambda e, pc_=pc_: e.matmul(pc_.h[:, 16:32], lhsT=blk1[:], rhs=G(3), start=True, stop=True), reads=[blk1, gsm_[b]], writes=[pc_])
            pe(lambda e, pc_=pc_: e.matmul(pc_.h[:, 32:48], lhsT=selA[:], rhs=G(3), start=True, stop=True), reads=[selA, gsm_[b]], writes=[pc_])
            pe(lambda e, pc_=pc_: e.matmul(pc_.h[:, 48:64], lhsT=selB[:], rhs=G(3), start=True, stop=True), reads=[selB, gsm_[b]], writes=[pc_])
            act(lambda e, pc_=pc_: e.copy(out=G(4), in_=pc_.h[:, 0:16]), reads=[pc_], writes=[gsm_[b]])
            act(lambda e, pc_=pc_: e.activation(out=G(5), in_=pc_.h[:, 0:16], func=AF.Exp), reads=[pc_], writes=[gsm_[b]])
            dve(lambda e, pc_=pc_: e.tensor_tensor(out=G(6), in0=pc_.h[:, 16:32], in1=G(4), op=ALU.subtract), reads=[pc_, gsm_[b]], writes=[gsm_[b]])
            act(lambda e: e.activation(out=G(6), in_=G(6), func=AF.Exp), reads=[gsm_[b]], writes=[gsm_[b]])
            act(lambda e, pc_=pc_: e.activation(out=G(7), in_=pc_.h[:, 32:48], func=AF.Exp), reads=[pc_], writes=[gsm_[b]])
            act(lambda e, pc_=pc_: e.activation(out=G(8), in_=pc_.h[:, 48:64], func=AF.Exp), reads=[pc_], writes=[gsm_[b]])
            dbg("gts", gts_[b], gts_[b][:], [128, 32])
            dbg("gsm", gsm_[b], gsm_[b][:], [128, 12, 16])
        for b in range(nb):
            gsm = gsm_[b]
            G = lambda i, b=b: gsm_[b][:, i, :]
            if kind == 1:
                for ch in range(2):
                    sq_ = seqs[0] + b * 2 + ch
                    P.dma(Sf[ch][:], sg.h.ap()[sq_].rearrange("h k v -> k h v"), reads=[sg], writes=[Sf[ch]])
                    act(lambda e, ch=ch: e.copy(out=Sb[ch][:], in_=Sf[ch][:]), reads=[Sf[ch]], writes=[Sb[ch]])
            bs_ = slice(b * 128, (b + 1) * 128)
            NG = NGROUPS
            HG = 8 // NG
            GW = HG * 128
            grp = list(range(NG))
            for hf in range(2):
                h0 = hf * 8
                hs = [slice(g * HG, (g + 1) * HG) for g in grp]
                hgl = [slice(h0 + g * HG, h0 + (g + 1) * HG) for g in grp]
                K_ = lambda t, g: (t, ("g", g))
                X032, XT32, X132 = Gtri, E1, Es
                fl = lambda t, g: t[:, hs[g], :].rearrange("p h t -> p (h t)")
                prt_, pgr_, pkq_ = {}, {}, {}
                for g in grp:
                    dve(lambda e, g=g: e.tensor_tensor(out=Gtri[:, hs[g], :], in0=G(3)[:, hgl[g]].unsqueeze(2).to_broadcast([128, HG, 128]),
                                                       in1=incl[:].unsqueeze(1).to_broadcast([128, HG, 128]), op=ALU.mult),
                        reads=[gsm, incl], writes=[K_(Gtri, g)])
                for g in grp:
                    prt = nps1()
                    prt_[g] = prt
                    pe(lambda e, g=g, prt=prt: e.matmul(prt.ap(0, GW), lhsT=onesF[:], rhs=fl(Gtri, g), start=True, stop=False),
                       reads=[onesF, K_(Gtri, g)], writes=[prt.key])
                    pe(lambda e, g=g, prt=prt: e.matmul(prt.ap(0, GW), lhsT=identB[:], rhs=negm[:].unsqueeze(1).to_broadcast([128, HG, 128]),
                                                        start=False, stop=True), reads=[identB, negm], writes=[prt.key])
                    if full:
                        pgr = nps1()
                        pgr_[g] = pgr
                        pe(lambda e, g=g, pgr=pgr: e.matmul(pgr.ap(0, GW), lhsT=onesF[:], rhs=fl(Gtri, g), start=True, stop=True),
                           reads=[onesF, K_(Gtri, g)], writes=[pgr.key])
                v3 = lambda ph: ph.ap(0, GW).rearrange("p (h t) -> p h t", h=HG)
                for g in grp:
                    dve(lambda e, g=g: e.tensor_tensor(out=E1[:, hs[g], :], in0=v3(prt_[g]), in1=G(4)[:, hgl[g]].unsqueeze(2).to_broadcast([128, HG, 128]),
                                                       op=ALU.subtract), reads=[prt_[g].key, gsm], writes=[K_(E1, g)])
                for g in grp:
                    act(lambda e, g=g: e.activation(out=E1[:, hs[g], :], in_=E1[:, hs[g], :], func=AF.Exp), reads=[K_(E1, g)], writes=[K_(E1, g)])
                for g in grp:
                    pool(lambda e, g=g: e.tensor_tensor(out=Es[:, hs[g], :], in0=E1[:, hs[g], :], in1=strict[:].unsqueeze(1).to_broadcast([128, HG, 128]), op=ALU.mult),
                         reads=[K_(E1, g), strict], writes=[K_(Es, g)])
                for g in grp:
                    pkq = nps1()
                    pkq_[g] = pkq
                    for j in range(HG // 2):
                        hk = hf * 4 + g * (HG // 2) + j
                        if full:
                            pe(lambda e, j=j, hk=hk, pkq=pkq: e.matmul(pkq.ap(j * 256, (j + 1) * 256), lhsT=kqT[:, 8 + hk, bs_], rhs=kqT[:, hk:hk + 9:8, bs_],
                                                                       start=True, stop=True), reads=[kqT], writes=[pkq.key])
                        else:
                            pe(lambda e, j=j, hk=hk, pkq=pkq: e.matmul(pkq.ap(j * 256 + 128, (j + 1) * 256), lhsT=kqT[:, 8 + hk, bs_], rhs=kqT[:, 8 + hk, bs_],
                                                                       start=True, stop=True), reads=[(kqT, 8 + hk)], writes=[pkq.key])
                kq4 = lambda ph: ph.ap(0, GW).rearrange("p (j c t) -> p j c t", j=HG // 2, c=2)
                if full:
                    for g in grp:
                        dve(lambda e, g=g: e.tensor_tensor(out=PT[:, hs[g], :].rearrange("p (j r) t -> p j r t", r=2),
                                                           in0=kq4(pkq_[g])[:, :, 0:1, :].to_broadcast([128, HG // 2, 2, 128]),
                                                           in1=E1[:, hs[g], :].rearrange("p (j r) t -> p j r t", r=2), op=ALU.mult),
                            reads=[pkq_[g].key, K_(E1, g)], writes=[K_(PT, g)])
                for g in grp:
                    for hh in range(HG):
                        hw = g * HG + hh
                        dve(lambda e, g=g, hh=hh, hw=hw: e.scalar_tensor_tensor(out=X032[:, hw, :], in0=kq4(pkq_[g])[:, hh // 2, 1, :], scalar=G(2)[:, h0 + hw:h0 + hw + 1],
                                                                               in1=Es[:, hw, :], op0=ALU.mult, op1=ALU.mult),
                            reads=[pkq_[g].key, gsm, K_(Es, g)], writes=[K_(X032, g)])
                if full:
                    for g in grp:
                        act(lambda e, g=g: e.activation(out=Es[:, hs[g], :], in_=v3(pgr_[g]), func=AF.Exp), reads=[pgr_[g].key], writes=[K_(Es, g)])
                    for g in grp:
                        q4 = kqT[:, hf * 4 + g * (HG // 2):hf * 4 + (g + 1) * (HG // 2), bs_].unsqueeze(2).to_broadcast([128, HG // 2, 2, 128])
                        dve(lambda e, g=g, q4=q4: e.tensor_tensor(out=qdT[:, hs[g], :].rearrange("p (j r) t -> p j r t", r=2), in0=q4,
                                                                  in1=Es[:, hs[g], :].rearrange("p (j r) t -> p j r t", r=2), op=ALU.mult),
                            reads=[kqT, K_(Es, g)], writes=[K_(qdT, g)])
                for g in grp:
                    pxt = nps1()
                    for hh in range(HG):
                        pe(lambda e, g=g, hh=hh, pxt=pxt: e.transpose(pxt.ap(hh * 128, (hh + 1) * 128), X032[:, g * HG + hh, :], identF[:]),
                           reads=[K_(X032, g), identF], writes=[pxt.key])
                    act(lambda e, g=g, pxt=pxt: e.copy(out=fl(XT32, g), in_=pxt.ap(0, GW)), reads=[pxt.key], writes=[K_(XT32, g)])
                for g in grp:
                    dve(lambda e, g=g: e.tensor_tensor(out=P32[:, hs[g], :], in0=X032[:, hs[g], :], in1=identF[:].unsqueeze(1).to_broadcast([128, HG, 128]), op=ALU.add),
                        reads=[K_(X032, g), identF], writes=[K_(P32, g)])
                for g in grp:
                    px = nps1()
                    for hh in range(HG):
                        hw = g * HG + hh
                        pe(lambda e, hh=hh, hw=hw, px=px: e.matmul(px.ap(hh * 128, (hh + 1) * 128), lhsT=XT32[:, hw, :], rhs=X032[:, hw, :], start=True, stop=True),
                           reads=[K_(XT32, g), K_(X032, g)], writes=[px.key])
                    act(lambda e, g=g, px=px: e.copy(out=fl(X132, g), in_=px.ap(0, GW)), reads=[px.key], writes=[K_(X132, g)])
                    act(lambda e, g=g: e.copy(out=Xa[1][:, hs[g], :], in_=X132[:, hs[g], :]), reads=[K_(X132, g)], writes=[K_(Xa[1], g)])
                for g in grp:
                    pxT = nps1()
                    for hh in range(HG):
                        hw = g * HG + hh
                        pe(lambda e, hh=hh, hw=hw, pxT=pxT: e.matmul(pxT.ap(hh * 128, (hh + 1) * 128), lhsT=X032[:, hw, :], rhs=XT32[:, hw, :], start=True, stop=True),
                           reads=[K_(XT32, g), K_(X032, g)], writes=[pxT.key])
                    dve(lambda e, g=g, pxT=pxT: e.tensor_copy(out=fl(Xt[1], g), in_=pxT.ap(0, GW)), reads=[pxT.key], writes=[K_(Xt[1], g)])
                for g in grp:
                    ppm = nps1()
                    for hh in range(HG):
                        hw = g * HG + hh
                        pe(lambda e, hh=hh, hw=hw, ppm=ppm: e.matmul(ppm.ap(hh * 128, (hh + 1) * 128), lhsT=XT32[:, hw, :], rhs=X132[:, hw, :], start=True, stop=True),
                           reads=[K_(XT32, g), K_(X132, g)], writes=[ppm.key])
                    dve(lambda e, g=g: e.tensor_tensor(out=P32[:, hs[g], :], in0=P32[:, hs[g], :], in1=X132[:, hs[g], :], op=ALU.add),
                        reads=[K_(P32, g), K_(X132, g)], writes=[K_(P32, g)])
                    dve(lambda e, g=g, ppm=ppm: e.tensor_tensor(out=fl(P32, g), in0=ppm.ap(0, GW), in1=fl(P32, g), op=ALU.add),
                        reads=[ppm.key, K_(P32, g)], writes=[K_(P32, g)])
                    act(lambda e, g=g: e.copy(out=Pm[1][:, hs[g], :], in_=P32[:, hs[g], :]), reads=[K_(P32, g)], writes=[K_(Pm[1], g)])
                cur = 1
                for lev in range(2, 6):
                    nxt = 1 - cur
                    if lev < 5:
                        for g in grp:
                            px = nps1()
                            for hh in range(HG):
                                hw = g * HG + hh
                                pe(lambda e, hh=hh, hw=hw, px=px, cur=cur: e.matmul(px.ap(hh * 128, (hh + 1) * 128), lhsT=Xt[cur][:, hw, :], rhs=Xa[cur][:, hw, :],
                                                                                    start=True, stop=True), reads=[K_(Xt[cur], g), K_(Xa[cur], g)], writes=[px.key])
                            act(lambda e, g=g, px=px, nxt=nxt: e.copy(out=fl(Xa[nxt], g), in_=px.ap(0, GW)), reads=[px.key], writes=[K_(Xa[nxt], g)])
                    pxTs = {}
                    for g in grp:
                        pxT = nps1()
                        for hh in range(HG):
                            hw = g * HG + hh
                            pe(lambda e, hh=hh, hw=hw, pxT=pxT, cur=cur: e.matmul(pxT.ap(hh * 128, (hh + 1) * 128), lhsT=Xa[cur][:, hw, :], rhs=Xt[cur][:, hw, :],
                                                                                  start=True, stop=True), reads=[K_(Xt[cur], g), K_(Xa[cur], g)], writes=[pxT.key])
                        dve(lambda e, g=g, pxT=pxT, nxt=nxt: e.tensor_copy(out=fl(Xt[nxt], g), in_=pxT.ap(0, GW)), reads=[pxT.key], writes=[K_(Xt[nxt], g)])
                    for g in grp:
                        ppm = nps1()
                        for hh in range(HG):
                            hw = g * HG + hh
                            pe(lambda e, hh=hh, hw=hw, ppm=ppm, nxt=nxt, cur=cur: e.matmul(ppm.ap(hh * 128, (hh + 1) * 128), lhsT=Xt[nxt][:, hw, :], rhs=Pm[cur][:, hw, :],
                                                                                           start=True, stop=True), reads=[K_(Xt[nxt], g), K_(Pm[cur], g)], writes=[ppm.key])
                        dve(lambda e, g=g, ppm=ppm: e.tensor_tensor(out=fl(P32, g), in0=ppm.ap(0, GW), in1=fl(P32, g), op=ALU.add),
                            reads=[ppm.key, K_(P32, g)], writes=[K_(P32, g)])
                        act(lambda e, g=g, nxt=nxt: e.copy(out=Pm[nxt][:, hs[g], :], in_=P32[:, hs[g], :]), reads=[K_(P32, g)], writes=[K_(Pm[nxt], g)])
                    cur = nxt
                Tt = Pm[cur]
                dbg("Tt", Tt, Tt[:], [128, 8, 128], BF16)
                for g in grp:
                    kc0 = hf * 512 + g * (HG // 2) * 128
                    k4 = ktok[:, b, kc0:kc0 + (HG // 2) * 128].rearrange("p (j d) -> p j d", j=HG // 2).unsqueeze(2).to_broadcast([128, HG // 2, 2, 128])
                    for dst_, gi in ((kg, 5), (kd, 6)):
                        pool(lambda e, g=g, k4=k4, dst_=dst_, gi=gi: e.tensor_tensor(
                            out=dst_[:, hs[g], :].rearrange("p (j r) d -> p j r d", r=2), in0=k4,
                            in1=G(gi)[:, hgl[g]].rearrange("p (j r) -> p j r", r=2).unsqueeze(3).to_broadcast([128, HG // 2, 2, 128]), op=ALU.mult),
                            reads=[(ktok, b), gsm], writes=[K_(dst_, g)])
                for g in grp:
                    pw = nps1()
                    for hh in range(HG):
                        hw = g * HG + hh
                        pe(lambda e, hh=hh, hw=hw, pw=pw, Tt=Tt: e.matmul(pw.ap(hh * 128, (hh + 1) * 128), lhsT=kg[:, hw, :], rhs=Tt[:, hw, :], start=True, stop=True),
                           reads=[K_(kg, g), K_(Tt, g)], writes=[pw.key])
                    act(lambda e, g=g, pw=pw: e.mul(out=fl(nWT, g), in_=pw.ap(0, GW), mul=-1.0), reads=[pw.key], writes=[K_(nWT, g)])
                for ch in range(2):
                    si = ch if kind == 1 else 0
                    r_ = slice(ch * 64, ch * 64 + 64)
                    SK = lambda t, g: (t, ("s", hf, g))
                    for g in grp:
                        pd = nps1()
                        for hh in range(HG):
                            hw = g * HG + hh
                            hg = h0 + hw
                            pe(lambda e, hh=hh, hw=hw, hg=hg, pd=pd, Tt=Tt, r_=r_: e.matmul(pd.ap(hh * 128, (hh + 1) * 128, r_), lhsT=Tt[r_, hw, r_], rhs=vtok[r_, b, hg * 128:(hg + 1) * 128],
                                                                                           start=True, stop=False), reads=[K_(Tt, g), (vtok, b)], writes=[pd.key])
                            pe(lambda e, hh=hh, hw=hw, hg=hg, pd=pd, r_=r_, si=si: e.matmul(pd.ap(hh * 128, (hh + 1) * 128, r_), lhsT=nWT[:, hw, r_], rhs=Sb[si][:, hg, :],
                                                                                           start=False, stop=True), reads=[K_(nWT, g), SK(Sb[si], g)], writes=[pd.key])
                        dve(lambda e, g=g, pd=pd, r_=r_: e.tensor_tensor(out=dlt[r_, hs[g], :], in0=pd.ap(0, GW, r_).rearrange("p (h d) -> p h d", h=HG),
                                                                         in1=G(1)[r_, hgl[g]].unsqueeze(2).to_broadcast([64, HG, 128]), op=ALU.mult),
                            reads=[pd.key, gsm], writes=[(dlt, (ch, g))])
                    pos = {}
                    if full:
                        for g in grp:
                            po = nps1()
                            pos[g] = po
                            for hh in range(HG):
                                hw = g * HG + hh
                                hg = h0 + hw
                                pe(lambda e, hh=hh, hw=hw, hg=hg, po=po, r_=r_, si=si: e.matmul(po.ap(hh * 128, (hh + 1) * 128, r_), lhsT=qdT[:, hw, r_], rhs=Sb[si][:, hg, :],
                                                                                               start=True, stop=False), reads=[K_(qdT, g), SK(Sb[si], g)], writes=[po.key])
                                pe(lambda e, hh=hh, hw=hw, po=po, r_=r_: e.matmul(po.ap(hh * 128, (hh + 1) * 128, r_), lhsT=PT[r_, hw, r_], rhs=dlt[r_, hw, :],
                                                                                 start=False, stop=True), reads=[K_(PT, g), (dlt, (ch, g))], writes=[po.key])
                    for g in grp:
                        psu = nps1()
                        for hh in range(HG):
                            hw = g * HG + hh
                            pe(lambda e, hh=hh, hw=hw, psu=psu, r_=r_: e.matmul(psu.ap(hh * 128, (hh + 1) * 128), lhsT=kd[r_, hw, :], rhs=dlt[r_, hw, :], start=True, stop=True),
                               reads=[K_(kd, g), (dlt, (ch, g))], writes=[psu.key])
                        for hh in range(HG):
                            hg = h0 + g * HG + hh
                            dve(lambda e, hh=hh, hg=hg, psu=psu, si=si, ch=ch: e.scalar_tensor_tensor(out=Sf[si][:, hg, :], in0=Sf[si][:, hg, :],
                                                                                                     scalar=G(7 + ch)[:, hg:hg + 1], in1=psu.ap(hh * 128, (hh + 1) * 128),
                                                                                                     op0=ALU.mult, op1=ALU.add),
                                reads=[psu.key, gsm, SK(Sf[si], g)], writes=[SK(Sf[si], g)])
                        act(lambda e, g=g, si=si: e.copy(out=Sb[si][:, hgl[g], :], in_=Sf[si][:, hgl[g], :]), reads=[SK(Sf[si], g)], writes=[SK(Sb[si], g)])
                    if full:
                        for g in grp:
                            po = pos[g]
                            o3 = po.ap(0, GW, r_).rearrange("p (h d) -> p h d", h=HG)
                            OK_ = (oss, (ch, g))
                            act(lambda e, g=g, o3=o3: e.activation(out=E1[r_, hs[g], :], in_=o3, func=AF.Square), reads=[po.key], writes=[(E1, ("o", ch, g))])
                            dve(lambda e, g=g: e.tensor_reduce(out=oss[r_, 0, hs[g]], in_=E1[r_, hs[g], :], axis=AX.X, op=ALU.add), reads=[(E1, ("o", ch, g))], writes=[OK_])
                            act(lambda e, g=g: e.activation(out=oss[r_, 1, hs[g]], in_=oss[r_, 0, hs[g]], func=AF.Ln, scale=1.0 / 128, bias=EPS), reads=[OK_], writes=[OK_])
                            act(lambda e, g=g: e.activation(out=oss[r_, 2, hs[g]], in_=oss[r_, 1, hs[g]], func=AF.Exp, scale=-0.5), reads=[OK_], writes=[OK_])
                            dve(lambda e, g=g, o3=o3: e.tensor_tensor(out=onb[r_, hs[g], :], in0=o3, in1=oss[r_, 2, hs[g]].unsqueeze(2).to_broadcast([64, HG, 128]),
                                                                      op=ALU.mult), reads=[po.key, OK_], writes=[(onb, (ch, g))])
                    if kind == 1 and hf == 1:
                        sq_ = seqs[0] + b * 2 + ch
                        P.dma(sgs.h.ap()[sq_].rearrange("h k v -> k h v"), Sf[si][:], reads=[Sf[si]], writes=[sgs])
                dbg("onb", onb, onb[:], [128, 8, 128], BF16)
                if full:
                    for g in grp:
                        pot = nps1()
                        for hh in range(HG):
                            pe(lambda e, g=g, hh=hh, pot=pot: e.transpose(pot.apb(hh * 128, (hh + 1) * 128), onb[:, g * HG + hh, :], identB[:]),
                               reads=[(onb, (0, g)), (onb, (1, g)), identB], writes=[pot.key])
                        a0 = h0 + g * HG
                        dve(lambda e, g=g, pot=pot, a0=a0: e.tensor_tensor(out=A1[:, a0:a0 + HG, bs_], in0=pot.apb(0, GW).rearrange("p (h t) -> p h t", h=HG),
                                                                           in1=A1[:, a0:a0 + HG, bs_], op=ALU.mult),
                            reads=[pot.key] + [(A1, a0 + i) for i in range(HG)], writes=[(A1, a0 + i) for i in range(HG)])
        if kind == 0 and last and full:
            P.dma(sgp.h.ap().rearrange("h k v -> k h v"), Sf[0][:], reads=[Sf[0]], writes=[sgp])
        if not full:
            return
        linear_fm("w_gdn_out", 0, D, 16, A1, lambda k: A1[:, k, 0:NTt], NTt, resid_evac(NTt))
        ffn(1, NTt)
        act(lambda e: e.activation(out=hT[:, :, 0:NTt], in_=xT[:, :, 0:NTt], func=AF.Square), reads=[xT], writes=[hT])
        pt = nps()
        for k in range(8):
            pe(lambda e, k=k, pt=pt: e.matmul(pt.h[:, 0:NTt], lhsT=onesB[:], rhs=hT[:, k, 0:NTt], start=(k == 0), stop=(k == 7)), reads=[onesB, hT], writes=[pt])
        act(lambda e, pt=pt: e.activation(out=cacc[:, 0, 0:NTt], in_=pt.h[:, 0:NTt], func=AF.Ln, scale=1.0 / D, bias=EPS), reads=[pt], writes=[cacc])
        act(lambda e: e.activation(out=cacc[:, 0, 0:NTt], in_=cacc[:, 0, 0:NTt], func=AF.Exp, scale=-0.5), reads=[cacc], writes=[cacc])
        for b in range(nb):
            prs = nps()
            pe(lambda e, b=b, prs=prs: e.transpose(prs.h[:, 0:128], cacc[:, 0, b * 128:(b + 1) * 128], identF[:]), reads=[cacc, identF], writes=[prs])
            act(lambda e, prs=prs: e.copy(out=bmv[:, 3:4], in_=prs.h[:, 0:1]), reads=[prs], writes=[bmv])
            for hf in range(2):
                pt = nps()
                for j in range(4):
                    k = hf * 4 + j
                    pe(lambda e, pt=pt, j=j, k=k, b=b: e.transpose(pt.h[:, j * 128:(j + 1) * 128], xT[:, k, b * 128:(b + 1) * 128], identF[:]),
                       reads=[xT, identF], writes=[pt])
                dve(lambda e, pt=pt, hf=hf: e.scalar_tensor_tensor(out=vtk[:, 1, hf * 512:(hf + 1) * 512], in0=pt.h[:, 0:512], scalar=bmv[:, 3:4],
                                                                   in1=wfin[:, hf * 512:(hf + 1) * 512], op0=ALU.mult, op1=ALU.mult),
                    reads=[pt, bmv, wfin], writes=[(vtk, 1)])
            P.dma(dst.h.ap()[tok0 + b * 128: tok0 + (b + 1) * 128, :], vtk[:, 1, 0:1024], reads=[(vtk, 1)], writes=[dst], eng="act")

    if n_ptiles:
        setup_sgu_kind(0)
    for ti in range(n_state):
        run_tile(0, ti * NT, NT, ti == 0, False, None, mode="state", need_q_halo=(ti == n_state - 1))
    for ti in range(n_ptiles):
        run_tile(0, ti * NT, NT, ti == 0, ti == n_ptiles - 1, None, apply_flag=(ti == 0 and n_state > 0))
    if NS:
        setup_sgu_kind(1)
        run_tile(1, 0, NSAMP, True, True, (0,))
    stats = P.emit()
    return nc, stats


WNAMES = ["norm_mix", "norm_ffn", "norm_final", "sgu_w_in", "sgu_ln_g", "sgu_ln_b", "sgu_w_s", "sgu_b_s", "sgu_w_out",
          "gdn_w_in", "gdn_w_conv", "gdn_a_log", "gdn_dt_bias", "gdn_w_onorm", "gdn_w_out", "ffn_w_gate", "ffn_w_up", "ffn_w_down"]
SQUEEZE0 = {"sgu_w_in", "sgu_ln_g", "sgu_ln_b", "sgu_w_s", "sgu_b_s", "sgu_w_out", "gdn_w_in", "gdn_w_conv", "gdn_a_log",
            "gdn_dt_bias", "gdn_w_onorm", "gdn_w_out"}

_CACHE = {}


def kernel(**inputs):
    x_prompt = np.ascontiguousarray(inputs["x_prompt"], dtype=np.float32)
    x_sample = np.ascontiguousarray(inputs["x_sample"], dtype=np.float32)
    state_gdn = np.ascontiguousarray(inputs["state_gdn"], dtype=np.float32)
    state_conv = np.ascontiguousarray(inputs["state_conv"], dtype=np.float32)
    B, SEQ, _ = x_prompt.shape
    DB, DS, _ = x_sample.shape
    n_cores = 8
    NT = 256
    NS = DB // n_cores
    HALF = SEQ // 2
    n_half = HALF // NT
    key = (n_half, NT, NS)
    if key not in _CACHE:
        _CACHE[key] = build_program(n_half, NT, NS, n_state=n_half)[0]
    nc = _CACHE[key]
    wd = {}
    for n in WNAMES:
        a = np.ascontiguousarray(inputs[n], dtype=np.float32)
        wd[n] = a[0] if n in SQUEEZE0 else a
    in_maps = []
    for c in range(n_cores):
        m = dict(wd)
        sq, second = c % B, c // B
        m["xq"] = x_prompt[sq, :HALF]
        m["xp"] = x_prompt[sq, HALF:] if second else x_prompt[sq, :HALF]
        m["flag"] = np.array([1.0 if second else 0.0], dtype=np.float32)
        m["xs"] = x_sample[c * NS:(c + 1) * NS].reshape(NS * DS, D)
        m["sg"] = state_gdn[0, c * NS:(c + 1) * NS]
        m["sc"] = state_conv[0, c * NS:(c + 1) * NS]
        in_maps.append(m)
    res = run_bass_kernel_spmd(nc, in_maps, core_ids=list(range(n_cores)))
    r = res.results
    y_prompt = np.stack([np.concatenate([r[c]["yp"], r[c + B]["yp"]], axis=0) for c in range(B)]).astype(np.float32)
    y_sample = np.concatenate([r[c]["ys"].reshape(NS, DS, D) for c in range(n_cores)]).astype(np.float32)
    ns_gdn_p = np.stack([r[c + B]["sgp"] for c in range(B)])[None].astype(np.float32)
    ns_conv_p = np.stack([r[c + B]["scp"] for c in range(B)])[None].astype(np.float32)
    ns_gdn_s = np.concatenate([r[c]["sgs"] for c in range(n_cores)])[None].astype(np.float32)
    ns_conv_s = np.concatenate([r[c]["scs"] for c in range(n_cores)])[None].astype(np.float32)
    ns_v_s = np.concatenate([r[c]["svs"].reshape(NS, DS, DSGU) for c in range(n_cores)])[None].astype(np.float32)
    return (y_prompt, y_sample, ns_gdn_p, ns_conv_p, ns_gdn_s, ns_conv_s, ns_v_s)
```

```python
import numpy as np
import concourse.bass as bass
import concourse.mybir as mybir
from concourse.bass_utils import run_bass_kernel_spmd

F32 = mybir.dt.float32
BF16 = mybir.dt.bfloat16
AF = mybir.ActivationFunctionType
ALU = mybir.AluOpType
AX = mybir.AxisListType

COMPUTE = ("pe", "act", "dve", "pool")


class T:
    __slots__ = ("h", "name", "last_w", "readers", "dma_sem", "kind")

    def __init__(self, h, name, kind="sb"):
        self.h = h
        self.name = name
        self.kind = kind
        self.last_w = {}
        self.readers = {}
        self.dma_sem = None

    def __getitem__(self, k):
        return self.h[k]


class Op:
    __slots__ = ("eng", "fn", "deps", "idx", "signal", "sigval", "is_dma", "dsem", "dval")

    def __init__(self, eng, fn):
        self.eng = eng
        self.fn = fn
        self.deps = []
        self.signal = False
        self.sigval = None
        self.is_dma = False
        self.dsem = None
        self.dval = None


class _Rec:
    def __getattr__(self, name):
        def f(*a, **k):
            self.call = (name, a, k)
            return self
        return f


def _replay(name, a, k):
    return lambda e: getattr(e, name)(*a, **k)


class Prog:
    def __init__(self, nc):
        self.nc = nc
        self.q = {e: [] for e in ("pe", "act", "dve", "pool", "sp")}
        self.tiles = []
        self._sems = []

    def sb(self, name, shape, dtype):
        t = T(self.nc.alloc_sbuf_tensor(name, list(shape), dtype), name)
        self.tiles.append(t)
        return t

    def ps(self, name, shape, dtype=F32):
        t = T(self.nc.alloc_psum_tensor(name, list(shape), dtype), name, "ps")
        self.tiles.append(t)
        return t

    def dram(self, name, shape, dtype, kind="Internal"):
        t = T(self.nc.dram_tensor(name, list(shape), dtype, kind=kind), name, "dram")
        self.tiles.append(t)
        return t

    def ext(self, h, name):
        t = T(h, name, "dram")
        self.tiles.append(t)
        return t

    def new_sem(self, name):
        cm = self.nc.semaphore(name)
        s = cm.__enter__()
        self._sems.append(cm)
        return s

    @staticmethod
    def _conf(k1, k2):
        return k1 is None or k2 is None or k1 == k2

    @staticmethod
    def _rk(x):
        return (x, None) if isinstance(x, T) else x

    def _add_deps(self, op, reads, writes, extra=()):
        deps = list(extra)
        for (t, k) in reads:
            for kk, w in t.last_w.items():
                if self._conf(k, kk):
                    deps.append(w)
        for (t, k) in writes:
            for kk, w in t.last_w.items():
                if self._conf(k, kk):
                    deps.append(w)
            for kk, rs in t.readers.items():
                if self._conf(k, kk):
                    deps.extend(rs)
        best = {}
        for d in deps:
            if d is op:
                continue
            if d.is_dma:
                key = ("dma", id(d.dsem))
                if key not in best or best[key].dval < d.dval:
                    best[key] = d
            else:
                if d.eng == op.eng and op.eng == "pe":
                    continue
                key = d.eng
                if key not in best or best[key].idx < d.idx:
                    best[key] = d
        for d in best.values():
            op.deps.append(d)
            if not d.is_dma:
                d.signal = True
        for (t, k) in reads:
            t.readers.setdefault(k, []).append(op)
        for (t, k) in writes:
            if k is None:
                t.last_w = {None: op}
                t.readers = {}
            else:
                t.last_w[k] = op
                t.readers[k] = []

    def op(self, eng, fn, reads=(), writes=(), extra=()):
        rec = _Rec()
        fn(rec)
        o = Op(eng, _replay(*rec.call))
        o.idx = len(self.q[eng])
        self._add_deps(o, [self._rk(r) for r in reads], [self._rk(w) for w in writes], extra)
        self.q[eng].append(o)
        return o

    def dma(self, out_ap, in_ap, reads=(), writes=(), sem_of=None, extra=(), eng="sp"):
        reads = [self._rk(r) for r in reads]
        writes = [self._rk(w) for w in writes]
        if sem_of is None:
            cands = [x for x in list(writes) + list(reads) if x[0].kind == "sb"]
            sem_of = cands[0] if cands else (list(writes) + list(reads))[0]
        st, sk = self._rk(sem_of)
        if st.dma_sem is None:
            st.dma_sem = {}
        if sk not in st.dma_sem:
            st.dma_sem[sk] = [self.new_sem("d%d" % len(self._sems)), 0]
        ent = st.dma_sem[sk]
        ent[1] += 16
        o = Op(eng, lambda e: e.dma_start(out=out_ap, in_=in_ap))
        o.is_dma = True
        o.dsem = ent[0]
        o.dval = ent[1]
        o.idx = len(self.q[eng])
        self._add_deps(o, reads, writes, extra)
        self.q[eng].append(o)
        return o

    def emit(self):
        nc = self.nc
        sems = {e: self.new_sem("s_" + e) for e in COMPUTE}
        for e in COMPUTE:
            c = 0
            for o in self.q[e]:
                if (not o.is_dma) and o.signal:
                    c += 1
                    o.sigval = c
        engmap = {"pe": "tensor", "act": "scalar", "dve": "vector", "pool": "gpsimd", "sp": "sync"}
        stats = {"waits": 0, "instr": 0}

        def emit_queue(ename, engine):
            known = {}
            for o in self.q[ename]:
                for d in o.deps:
                    if d.is_dma:
                        key, val, sem = ("dma", id(d.dsem)), d.dval, d.dsem
                    else:
                        key, val, sem = d.eng, d.sigval, sems[d.eng]
                    if known.get(key, 0) >= val:
                        continue
                    engine.wait_ge(sem, val)
                    stats["waits"] += 1
                    known[key] = val
                ins = o.fn(engine)
                stats["instr"] += 1
                if o.is_dma:
                    ins.then_inc(o.dsem, 16)
                elif o.signal:
                    ins.then_inc(sems[ename], 1)
            if ename == "sp":
                for t in self.tiles:
                    if t.dma_sem:
                        for ent in t.dma_sem.values():
                            engine.wait_ge(ent[0], ent[1])

        with nc.Block() as block:
            for ename in ["sp", "pool", "act", "dve", "pe"]:
                getattr(block, engmap[ename])(lambda engine, _n=ename: emit_queue(_n, engine))
        return stats


D = 1024
DSGU = 2048
DFF = 2816
GIN = 6176
EPS = 1e-6
NEG = -30000.0
WSLOT = 4096


DBG = []
NGROUPS = 2


def build_program(n_ptiles, NT, NS, n_state=0):
    assert NT % 128 == 0 and NS % 2 == 0
    nc = bass.Bass("TRN2", target_bir_lowering=False)
    P = Prog(nc)
    NP = n_ptiles * NT
    NSAMP = NS * 64
    NTMAX = max(NT, NSAMP) if NS else NT
    assert NSAMP <= NT or n_ptiles == 0
    NBMAX = NTMAX // 128

    def din(name, shape):
        return P.ext(nc.dram_tensor(name, list(shape), F32, kind="ExternalInput"), name)

    def dout(name, shape):
        return P.ext(nc.dram_tensor(name, list(shape), F32, kind="ExternalOutput"), name)

    xp = din("xp", [max(NP, 1), D])
    xq = din("xq", [max(n_state * NT, 1), D])
    flag = din("flag", [1])
    xs = din("xs", [max(NSAMP, 1), D])
    sg = din("sg", [max(NS, 1), 16, 128, 128])
    sc = din("sc", [max(NS, 1), 3, 4096])
    norm_mix = din("norm_mix", [2, D])
    norm_ffn = din("norm_ffn", [2, D])
    norm_final = din("norm_final", [D])
    sgu_w_in = din("sgu_w_in", [D, 4096])
    sgu_ln_g = din("sgu_ln_g", [DSGU])
    sgu_ln_b = din("sgu_ln_b", [DSGU])
    sgu_w_s = din("sgu_w_s", [8, 128, 128])
    sgu_b_s = din("sgu_b_s", [8, 128])
    sgu_w_out = din("sgu_w_out", [DSGU, D])
    gdn_w_in = din("gdn_w_in", [D, GIN])
    gdn_w_conv = din("gdn_w_conv", [4, 4096])
    gdn_a_log = din("gdn_a_log", [16])
    gdn_dt_bias = din("gdn_dt_bias", [16])
    gdn_w_onorm = din("gdn_w_onorm", [128])
    gdn_w_out = din("gdn_w_out", [DSGU, D])
    ffn_w_gate = din("ffn_w_gate", [2, D, DFF])
    ffn_w_up = din("ffn_w_up", [2, D, DFF])
    ffn_w_down = din("ffn_w_down", [2, DFF, D])

    yp = dout("yp", [max(NP, 1), D])
    ys = dout("ys", [max(NSAMP, 1), D])
    sgp = dout("sgp", [16, 128, 128])
    scp = dout("scp", [3, 4096])
    sgs = dout("sgs", [max(NS, 1), 16, 128, 128])
    scs = dout("scs", [max(NS, 1), 3, 4096])
    svs = dout("svs", [max(NSAMP, 1), DSGU])

    xT = P.sb("xT", [128, 8, NTMAX], F32)
    hT = P.sb("hT", [128, 8, NTMAX], BF16)
    A1 = P.sb("A1", [128, 22, NTMAX], BF16)
    kqT = P.sb("kqT", [128, 16, NTMAX], BF16)
    wsl = [P.sb("wsl%d" % i, [128, WSLOT], BF16) for i in range(4)]
    vtk = P.sb("vtk", [128, NBMAX, 2048], F32)
    vnb = P.sb("vnb", [128, 2048], BF16)
    vtok = P.sb("vtok", [128, NBMAX, 2048], BF16)
    ktok = P.sb("ktok", [128, NBMAX, 1024], BF16)
    grep = P.sb("grep", [128, 2048], F32)
    brep = P.sb("brep", [128, 2048], F32)
    wfin = P.sb("wfin", [128, 1024], F32)
    identF = P.sb("identF", [128, 128], F32)
    identB = P.sb("identB", [128, 128], BF16)
    onesF = P.sb("onesF", [128, 128], F32)
    onesB = P.sb("onesB", [128, 128], BF16)
    tril = P.sb("tril", [128, 128], F32)
    incl = P.sb("incl", [128, 128], F32)
    strict = P.sb("strict", [128, 128], F32)
    negm = P.sb("negm", [128, 128], BF16)
    selA = P.sb("selA", [128, 128], F32)
    selB = P.sb("selB", [128, 128], F32)
    blk1 = P.sb("blk1", [128, 128], F32)
    wsT1 = P.sb("wsT", [128, 8, 128], BF16)
    bsrow1 = P.sb("bsrow", [1, 1024], F32)
    wsT = [wsT1, wsT1]
    bsrow = [bsrow1, bsrow1]
    cw = P.sb("cw", [128, 4, 32], F32)
    nsc = P.sb("nsc", [128, 40], F32)
    alr = P.sb("alr", [128, 16], F32)
    dtr = P.sb("dtr", [128, 16], F32)
    hal = P.sb("hal", [128, 32, 4, 3], F32)
    stg = P.sb("stg", [128, 128], F32)
    flg = P.sb("flg", [128, 1], F32)
    xc = P.sb("xc", [128, 4, NTMAX + 12], F32)
    cacc = P.sb("cacc", [128, 4, NTMAX], F32)
    vTg = P.sb("vTg", [128, 4, NTMAX], BF16)
    Sf = [P.sb("Sf%d" % i, [128, 16, 128], F32) for i in range(2)]
    Sb = [P.sb("Sb%d" % i, [128, 16, 128], BF16) for i in range(2)]
    gts_ = [P.sb("gts%d" % i, [128, 32], F32) for i in range(NBMAX)]
    gsm_ = [P.sb("gsm%d" % i, [128, 12, 16], F32) for i in range(NBMAX)]
    gsm_unused = None
    Gtri = P.sb("Gtri", [128, 8, 128], F32)
    E1 = P.sb("E1", [128, 8, 128], F32)
    Es = P.sb("Es", [128, 8, 128], F32)
    PT = P.sb("PT", [128, 8, 128], BF16)
    Xa = [P.sb("Xa%d" % i, [128, 8, 128], BF16) for i in range(2)]
    Xt = [P.sb("Xt%d" % i, [128, 8, 128], BF16) for i in range(2)]
    Pm = [P.sb("Pm%d" % i, [128, 8, 128], BF16) for i in range(2)]
    P32 = P.sb("P32", [128, 8, 128], F32)
    kg = P.sb("kg", [128, 8, 128], BF16)
    kd = P.sb("kd", [128, 8, 128], BF16)
    nWT = P.sb("nWT", [128, 8, 128], BF16)
    qdT = P.sb("qdT", [128, 8, 128], BF16)
    dlt = P.sb("dlt", [128, 8, 128], BF16)
    onb = P.sb("onb", [128, 8, 128], BF16)
    oss = P.sb("oss", [128, 3, 8], F32)
    bst = P.sb("bst", [128, 4, 6], F32)
    bmv = P.sb("bmv", [128, 4], F32)

    pp = [P.ps("pp%d" % i, [128, 1024], F32) for i in range(4)]
    ppc = [0]

    def nps():
        t = pp[ppc[0] % 4]
        ppc[0] += 1
        return t

    class PH:
        def __init__(self, t, half):
            self.t, self.off, self.key = t, half * 512, (t, ("h", half))

        def ap(self, a, b, rows=slice(None)):
            return self.t.h[rows, self.off + a:self.off + b]

        def apb(self, a, b):
            return self.t.h[:, :].bitcast(BF16)[:, 2 * self.off + a:2 * self.off + b]

    p1c = [0]

    def nps1():
        i = p1c[0]
        p1c[0] += 1
        return PH(pp[(i // 2) % 4], i % 2)

    def pbf(t):
        return t.h[:, :].bitcast(BF16)

    wsc = [0]

    def nws():
        t = wsl[wsc[0] % 4]
        wsc[0] += 1
        return t

    dbg_done = set()

    def dbg(name, t, ap, shape, dtype=F32):
        if name not in DBG or name in dbg_done:
            return
        dbg_done.add(name)
        o = P.ext(nc.dram_tensor("dbg_" + name, list(shape), dtype, kind="ExternalOutput"), "dbg_" + name)
        P.dma(o.h.ap(), ap, reads=[t], writes=[o])

    def pool(fn, reads=(), writes=()):
        return P.op("pool", fn, reads, writes)

    def dve(fn, reads=(), writes=()):
        return P.op("dve", fn, reads, writes)

    def act(fn, reads=(), writes=()):
        return P.op("act", fn, reads, writes)

    def pe(fn, reads=(), writes=()):
        return P.op("pe", fn, reads, writes)

    pool(lambda e: e.memset(onesF[:], 1.0), writes=[onesF])
    pool(lambda e: e.memset(onesB[:], 1.0), writes=[onesB])
    pool(lambda e: e.memset(identF[:], 0.0), writes=[identF])
    pool(lambda e: e.affine_select(out=identF[:], in_=onesF[:], pattern=[[-1, 128]], base=0, channel_multiplier=1,
                                   compare_op=ALU.is_equal, fill=0.0), reads=[onesF], writes=[identF])
    pool(lambda e: e.tensor_copy(out=identB[:], in_=identF[:]), reads=[identF], writes=[identB])
    pool(lambda e: e.memset(tril[:], 0.0), writes=[tril])
    pool(lambda e: e.affine_select(out=tril[:], in_=onesF[:], pattern=[[1, 128]], base=0, channel_multiplier=-1,
                                   compare_op=ALU.is_ge, fill=0.0), reads=[onesF], writes=[tril])
    pool(lambda e: e.tensor_copy(out=incl[:], in_=tril[:]), reads=[tril], writes=[incl])
    pool(lambda e: e.memset(incl[0:64, 64:128], 0.0), writes=[incl])
    pool(lambda e: e.memset(strict[:], 0.0), writes=[strict])
    pool(lambda e: e.affine_select(out=strict[:], in_=onesF[:], pattern=[[1, 128]], base=0, channel_multiplier=-1,
                                   compare_op=ALU.is_gt, fill=0.0), reads=[onesF], writes=[strict])
    pool(lambda e: e.memset(strict[0:64, 64:128], 0.0), writes=[strict])
    pool(lambda e: e.tensor_scalar(out=negm[:], in0=incl[:], scalar1=-1.0, scalar2=-NEG, op0=ALU.add, op1=ALU.mult),
         reads=[incl], writes=[negm])
    pool(lambda e: e.memset(selA[:], 0.0), writes=[selA])
    pool(lambda e: e.memset(selA[0:64, :], 1.0), writes=[selA])
    pool(lambda e: e.memset(selB[:], 0.0), writes=[selB])
    pool(lambda e: e.memset(selB[64:128, :], 1.0), writes=[selB])
    pool(lambda e: e.memset(blk1[:], 0.0), writes=[blk1])
    pool(lambda e: e.memset(blk1[0:64, 0:64], 1.0), writes=[blk1])
    pool(lambda e: e.memset(blk1[64:128, 64:128], 1.0), writes=[blk1])
    pool(lambda e: e.memset(hal[:], 0.0), writes=[hal])

    def bcast_rows(ap1d, n):
        return ap1d.partition_broadcast(128)

    P.dma(grep[:], sgu_ln_g.h.ap().partition_broadcast(128), reads=[sgu_ln_g], writes=[grep])
    P.dma(brep[:], sgu_ln_b.h.ap().partition_broadcast(128), reads=[sgu_ln_b], writes=[brep])
    P.dma(wfin[:], norm_final.h.ap().partition_broadcast(128), reads=[norm_final], writes=[wfin])
    P.dma(alr[:], gdn_a_log.h.ap().partition_broadcast(128), reads=[gdn_a_log], writes=[alr])
    P.dma(dtr[:], gdn_dt_bias.h.ap().partition_broadcast(128), reads=[gdn_dt_bias], writes=[dtr])
    P.dma(flg[:], flag.h.ap().partition_broadcast(128), reads=[flag], writes=[flg])
    act(lambda e: e.activation(out=alr[:], in_=alr[:], func=AF.Exp), reads=[alr], writes=[alr])
    dve(lambda e: e.tensor_scalar(out=alr[:], in0=alr[:], scalar1=-1.0, scalar2=None, op0=ALU.mult), reads=[alr], writes=[alr])

    def small_T(dst_ap, dst_t, src_ap, src_t, R):
        P.dma(stg[0:R, :], src_ap, reads=[src_t], writes=[stg])
        pt = nps()
        pe(lambda e: e.transpose(pt.h[:, 0:R], stg[0:R, :], identF[0:R, 0:R]), reads=[stg, identF], writes=[pt])
        act(lambda e: e.copy(out=dst_ap, in_=pt.h[:, 0:R]), reads=[pt], writes=[dst_t])

    small_T(cw[:].rearrange("p i c -> p (i c)"), cw, gdn_w_conv.h.ap().rearrange("i (c p) -> (i c) p", p=128), gdn_w_conv, 128)
    small_T(nsc[:, 0:16], nsc, norm_mix.h.ap().rearrange("l (k p) -> (l k) p", p=128), norm_mix, 16)
    small_T(nsc[:, 16:32], nsc, norm_ffn.h.ap().rearrange("l (k p) -> (l k) p", p=128), norm_ffn, 16)
    small_T(nsc[:, 32:33], nsc, gdn_w_onorm.h.ap().rearrange("(o p) -> o p", o=1), gdn_w_onorm, 1)

    def setup_sgu_kind(kind):
        if kind == 0:
            P.dma(vtk[:, 0, 0:1024].rearrange("p (g s) -> p g s", g=8), sgu_w_s.h.ap().rearrange("g t s -> t g s"),
                  reads=[sgu_w_s], writes=[(vtk, 0)])
            P.dma(bsrow[0][:], sgu_b_s.h.ap().rearrange("(o g) t -> o (g t)", o=1), reads=[sgu_b_s], writes=[bsrow[0]])
        else:
            pool(lambda e: e.memset(vtk[:, 0, 0:1024], 0.0), writes=[(vtk, 0)])
            v3 = vtk[:, 0, 0:1024].rearrange("p (g s) -> p g s", g=8)
            P.dma(v3[0:64, :, 0:64], sgu_w_s.h.ap().rearrange("g t s -> t g s")[0:64, :, 0:64], reads=[sgu_w_s], writes=[(vtk, 0)])
            P.dma(v3[64:128, :, 64:128], sgu_w_s.h.ap().rearrange("g t s -> t g s")[0:64, :, 0:64], reads=[sgu_w_s], writes=[(vtk, 0)])
            b4 = bsrow[1][:].rearrange("o (g h t) -> o g h t", g=8, h=2)
            for hh in range(2):
                P.dma(b4[:, :, hh, :], sgu_b_s.h.ap().rearrange("(o g) t -> o g t", o=1)[:, :, 0:64], reads=[sgu_b_s], writes=[bsrow[1]])
        msk = tril if kind == 0 else incl
        for g4 in range(2):
            pt = nps()
            for j in range(4):
                g = g4 * 4 + j
                pe(lambda e, g=g, j=j, pt=pt: e.transpose(pt.h[:, j * 128:(j + 1) * 128], vtk[:, 0, g * 128:(g + 1) * 128], identF[:]),
                   reads=[(vtk, 0), identF], writes=[pt])
            dve(lambda e, g4=g4, pt=pt, kind=kind, msk=msk: e.tensor_tensor(
                out=wsT[kind][:, g4 * 4:(g4 + 1) * 4, :], in0=pt.h[:, 0:512].rearrange("p (g t) -> p g t", g=4),
                in1=msk[:].unsqueeze(1).to_broadcast([128, 4, 128]), op=ALU.mult), reads=[pt, msk], writes=[wsT[kind]])

    wdefs = [
        ("w_sgu_in", sgu_w_in, sgu_w_in.h.ap(), D, 4096, (0, 0)),
        ("w_sgu_out", sgu_w_out, sgu_w_out.h.ap(), DSGU, D, None),
        ("w_gate0", ffn_w_gate, ffn_w_gate.h.ap()[0], D, DFF, (16, 0)),
        ("w_up0", ffn_w_up, ffn_w_up.h.ap()[0], D, DFF, (16, 0)),
        ("w_down0", ffn_w_down, ffn_w_down.h.ap()[0], DFF, D, None),
        ("w_gdn_in", gdn_w_in, gdn_w_in.h.ap(), D, GIN, (8, 0)),
        ("w_gdn_out", gdn_w_out, gdn_w_out.h.ap(), DSGU, D, (32, 1)),
        ("w_gate1", ffn_w_gate, ffn_w_gate.h.ap()[1], D, DFF, (24, 0)),
        ("w_up1", ffn_w_up, ffn_w_up.h.ap()[1], D, DFF, (24, 0)),
        ("w_down1", ffn_w_down, ffn_w_down.h.ap()[1], DFF, D, None),
    ]
    WS = {}
    pro_ops = []
    cnt = 0
    engs = ["act", "dve"]
    stf = [(vtk, 0, lambda w: vtk[:, 0, 0:w]), (vtk, 1, lambda w: vtk[:, 1, 0:w]),
           (Sf[0], None, lambda w: Sf[0][:].rearrange("p h d -> p (h d)")[:, 0:w]), (Sf[1], None, lambda w: Sf[1][:].rearrange("p h d -> p (h d)")[:, 0:w])]
    stb = [(vtok, 0, lambda w: vtok[:, 0, 0:w]), (vtok, 1, lambda w: vtok[:, 1, 0:w]),
           (Sb[0], None, lambda w: Sb[0][:].rearrange("p h d -> p (h d)")[:, 0:w]), (Sb[1], None, lambda w: Sb[1][:].rearrange("p h d -> p (h d)")[:, 0:w])]
    for (name, srct, srcap, K, C, fold) in wdefs:
        scr = P.dram(name, [K, C], BF16)
        WS[name] = (scr, K // 128, C)
        for kc in range(K // 128):
            for c0 in range(0, C, 2048):
                cwid = min(2048, C - c0)
                i = cnt % 4
                cnt += 1
                ft, fk, fap = stf[i]
                bt, bk, bap = stb[i]
                fkey = (ft, fk) if fk is not None else ft
                bkey = (bt, bk) if bk is not None else bt
                P.dma(fap(cwid), srcap[kc * 128:(kc + 1) * 128, c0:c0 + cwid], reads=[srct], writes=[fkey])
                en = engs[cnt % 2]
                if fold is None:
                    if en == "act":
                        act(lambda e: e.copy(out=bap(cwid), in_=fap(cwid)), reads=[fkey], writes=[bkey])
                    else:
                        dve(lambda e: e.tensor_copy(out=bap(cwid), in_=fap(cwid)), reads=[fkey], writes=[bkey])
                else:
                    col = fold[0] + (kc if fold[1] == 0 else 0)
                    if en == "act":
                        act(lambda e: e.activation(out=bap(cwid), in_=fap(cwid), func=AF.Copy, scale=nsc[:, col:col + 1]), reads=[fkey, nsc], writes=[bkey])
                    else:
                        dve(lambda e: e.tensor_scalar(out=bap(cwid), in0=fap(cwid), scalar1=nsc[:, col:col + 1], scalar2=None, op0=ALU.mult),
                            reads=[fkey, nsc], writes=[bkey])
                o = P.dma(scr.h.ap()[kc * 128:(kc + 1) * 128, c0:c0 + cwid], bap(cwid), reads=[bkey], writes=[(scr, (kc, c0))], eng="act")
                pro_ops.append(o)
    fence = P.dma(stg[0:1, 0:4], onesF[0:1, 0:4], reads=[onesF], writes=[stg], extra=pro_ops)
    for name in WS:
        WS[name][0].last_w = {None: fence}
        WS[name][0].readers = {}

    def wpieces(name, c_lo, c_hi, KC):
        scr = WS[name][0]
        pcmax = (WSLOT // KC) // 128 * 128
        c0 = c_lo
        while c0 < c_hi:
            pc = min(pcmax, c_hi - c0)
            slot = nws()
            view = slot[:, 0:KC * pc].rearrange("p (k c) -> p k c", k=KC)
            P.dma(view, scr.h.ap().rearrange("(k p) c -> p k c", p=128)[:, :, c0:c0 + pc], reads=[scr], writes=[slot])
            yield slot, view, c0, pc
            c0 += pc

    def linear_fm(name, c_lo, c_hi, KC, rhs_t, rhs_ap, NTt, evac, oc_base=0, piped=False):
        gmax = min(4, max(1, 1024 // NTt))
        per_bank = max(1, 512 // NTt)
        pend = {"b1": None, "b2": None, "b2n": None}
        for slot, view, c0, pc in wpieces(name, c_lo, c_hi, KC):
            nch = pc // 128
            j0 = 0
            while j0 < nch:
                n = min(gmax, nch - j0)
                pt = nps()
                for j in range(n):
                    off = (j // per_bank) * 512 + (j % per_bank) * NTt
                    for k in range(KC):
                        pe(lambda e, pt=pt, off=off, view=view, k=k, jj=j0 + j: e.matmul(
                            pt.h[:, off:off + NTt], lhsT=view[:, k, jj * 128:(jj + 1) * 128], rhs=rhs_ap(k),
                            start=(k == 0), stop=(k == KC - 1)), reads=[slot, rhs_t], writes=[pt])
                if per_bank * NTt == 512 or n <= per_bank:
                    ap3 = pt.h[:, 0:n * NTt].rearrange("p (j t) -> p j t", j=n)
                else:
                    raise NotImplementedError
                called = [False]

                def mid():
                    called[0] = True
                    if pend["b1"] is not None:
                        pend["b1"]()
                    if pend["b2"] is not None:
                        pend["b2"]()
                if piped:
                    d_ = evac(oc_base + (c0 - c_lo) // 128 + j0, n, pt, ap3, mid)
                else:
                    d_ = evac(oc_base + (c0 - c_lo) // 128 + j0, n, pt, ap3)
                if not called[0]:
                    mid()
                pend["b2"] = pend["b2n"]
                pend["b1"], pend["b2n"] = d_ if d_ is not None else (None, None)
                j0 += n
        for k_ in ("b1", "b2"):
            if pend[k_] is not None:
                pend[k_]()
        if pend["b2n"] is not None:
            pend["b2n"]()

    def rmsnorm_to_hT(NTt):
        act(lambda e: e.activation(out=hT[:, :, 0:NTt], in_=xT[:, :, 0:NTt], func=AF.Square), reads=[xT], writes=[hT])
        pt = nps()
        for k in range(8):
            pe(lambda e, k=k, pt=pt: e.matmul(pt.h[:, 0:NTt], lhsT=onesB[:], rhs=hT[:, k, 0:NTt], start=(k == 0), stop=(k == 7)),
               reads=[onesB, hT], writes=[pt])
        act(lambda e, pt=pt: e.activation(out=cacc[:, 0, 0:NTt], in_=pt.h[:, 0:NTt], func=AF.Ln, scale=1.0 / D, bias=EPS), reads=[pt], writes=[cacc])
        act(lambda e: e.activation(out=cacc[:, 0, 0:NTt], in_=cacc[:, 0, 0:NTt], func=AF.Exp, scale=-0.5), reads=[cacc], writes=[cacc])
        dve(lambda e: e.tensor_tensor(out=hT[:, :, 0:NTt], in0=xT[:, :, 0:NTt], in1=cacc[:, 0, 0:NTt].unsqueeze(1).to_broadcast([128, 8, NTt]),
                                      op=ALU.mult), reads=[xT, cacc], writes=[hT])

    def resid_evac(NTt):
        def ev(oc0, n, pt, ap3):
            dve(lambda e: e.tensor_tensor(out=xT[:, oc0:oc0 + n, 0:NTt], in0=ap3, in1=xT[:, oc0:oc0 + n, 0:NTt], op=ALU.add),
                reads=[pt, xT], writes=[xT])
        return ev

    def ffn(layer, NTt):
        rmsnorm_to_hT(NTt)
        hrhs = lambda k: hT[:, k, 0:NTt]

        def ev_gate(oc0, n, pt, ap3):
            act(lambda e: e.activation(out=A1[:, oc0:oc0 + n, 0:NTt], in_=ap3, func=AF.Silu), reads=[pt], writes=[(A1, c) for c in range(oc0, oc0 + n)])

        def ev_up(oc0, n, pt, ap3):
            dve(lambda e: e.tensor_tensor(out=A1[:, oc0:oc0 + n, 0:NTt], in0=ap3, in1=A1[:, oc0:oc0 + n, 0:NTt], op=ALU.mult),
                reads=[pt] + [(A1, c) for c in range(oc0, oc0 + n)], writes=[(A1, c) for c in range(oc0, oc0 + n)])
        linear_fm("w_gate%d" % layer, 0, DFF, 8, hT, hrhs, NTt, ev_gate)
        linear_fm("w_up%d" % layer, 0, DFF, 8, hT, hrhs, NTt, ev_up)
        linear_fm("w_down%d" % layer, 0, D, 22, A1, lambda k: A1[:, k, 0:NTt], NTt, resid_evac(NTt))

    XW = NTMAX + 12
    convsets = [
        (xc.h[:, :, :], xc, cacc.h[:, :, :], cacc, vTg.h[:, :, :], vTg),
        (vtk.h[:, 0, 0:4 * XW].rearrange("p (j w) -> p j w", j=4), (vtk, 0),
         vnb.h[:, :].bitcast(F32)[:, 0:4 * NTMAX].rearrange("p (j t) -> p j t", j=4), vnb,
         vtk.h[:, 0, 4 * XW + 16:4 * XW + 16 + 2 * NTMAX].bitcast(BF16).rearrange("p (j t) -> p j t", j=4), (vtk, 0)),
    ]
    convc = [0]

    def run_tile(kind, tok0, NTt, first, last, seqs, mode="full", apply_flag=False, need_q_halo=False):
        full = mode == "full"
        nb = NTt // 128
        src = (xp if full else xq) if kind == 0 else xs
        dst = yp if kind == 0 else ys
        nseg = 1 if kind == 0 else NTt // 64
        L = NTt // nseg
        if apply_flag:
            dve(lambda e: e.tensor_scalar(out=Sf[0][:], in0=Sf[0][:], scalar1=flg[:, 0:1], scalar2=None, op0=ALU.mult), reads=[Sf[0], flg], writes=[Sf[0]])
            act(lambda e: e.copy(out=Sb[0][:], in_=Sf[0][:]), reads=[Sf[0]], writes=[Sb[0]])
            dve(lambda e: e.tensor_scalar(out=hal[:], in0=hal[:], scalar1=flg[:, 0:1], scalar2=None, op0=ALU.mult), reads=[hal, flg], writes=[hal])
        for b in range(nb):
            P.dma(vtk[:, 1, 0:1024], src.h.ap()[tok0 + b * 128: tok0 + (b + 1) * 128, :], reads=[src], writes=[(vtk, 1)])
            for hf in range(2):
                pt = nps()
                for j in range(4):
                    k = hf * 4 + j
                    pe(lambda e, pt=pt, j=j, k=k: e.transpose(pt.h[:, j * 128:(j + 1) * 128], vtk[:, 1, k * 128:(k + 1) * 128], identF[:]),
                       reads=[(vtk, 1), identF], writes=[pt])
                act(lambda e, pt=pt, hf=hf, b=b: e.copy(out=xT[:, hf * 4:(hf + 1) * 4, b * 128:(b + 1) * 128],
                                                        in_=pt.h[:, 0:512].rearrange("p (j t) -> p j t", j=4)), reads=[pt], writes=[xT])
        rmsnorm_to_hT(NTt)
        hrhs = lambda k: hT[:, k, 0:NTt]

        def ev_u(oc0, n, pt, ap3):
            act(lambda e: e.activation(out=A1[:, oc0:oc0 + n, 0:NTt], in_=ap3, func=AF.Gelu), reads=[pt], writes=[(A1, c) for c in range(oc0, oc0 + n)])
        for slot, view, c0, pc in wpieces("w_sgu_in", 2048, 4096, 8):
            for b in range(nb):
                for q in range(pc // 512):
                    pt = nps()
                    for k in range(8):
                        pe(lambda e, pt=pt, k=k, b=b, q=q, view=view: e.matmul(pt.h[:, 0:512], lhsT=hT[:, k, b * 128:(b + 1) * 128],
                                                                               rhs=view[:, k, q * 512:(q + 1) * 512], start=(k == 0), stop=(k == 7)),
                           reads=[slot, hT], writes=[pt])
                    cc = c0 - 2048 + q * 512
                    act(lambda e, pt=pt, b=b, cc=cc: e.activation(out=vtk[:, b, cc:cc + 512], in_=pt.h[:, 0:512], func=AF.Gelu),
                        reads=[pt], writes=[(vtk, b)])
        for b in range(nb):
            for q in range(4):
                dve(lambda e, b=b, q=q: e.bn_stats(out=bst[:, q, :], in_=vtk[:, b, q * 512:(q + 1) * 512]), reads=[(vtk, b)], writes=[bst])
            dve(lambda e: e.bn_aggr(out=bmv[:, 0:2], in_=bst[:].rearrange("p q s -> p (q s)")), reads=[bst], writes=[bmv])
            act(lambda e: e.activation(out=bmv[:, 2:3], in_=bmv[:, 1:2], func=AF.Ln, bias=1e-5), reads=[bmv], writes=[bmv])
            act(lambda e: e.activation(out=bmv[:, 2:3], in_=bmv[:, 2:3], func=AF.Exp, scale=-0.5), reads=[bmv], writes=[bmv])
            dve(lambda e, b=b: e.scalar_tensor_tensor(out=vtk[:, b, :], in0=vtk[:, b, :], scalar=bmv[:, 0:1], in1=grep[:],
                                                      op0=ALU.subtract, op1=ALU.mult), reads=[(vtk, b), bmv, grep], writes=[(vtk, b)])
            dve(lambda e, b=b: e.scalar_tensor_tensor(out=vtk[:, b, :], in0=vtk[:, b, :], scalar=bmv[:, 2:3], in1=brep[:],
                                                      op0=ALU.mult, op1=ALU.add), reads=[(vtk, b), bmv, brep], writes=[(vtk, b)])
            if kind == 1:
                P.dma(svs.h.ap()[tok0 + b * 128: tok0 + (b + 1) * 128, :], vtk[:, b, :], reads=[(vtk, b)], writes=[svs])
        linear_fm("w_sgu_in", 0, 2048, 8, hT, hrhs, NTt, ev_u)
        for b in range(nb):
            act(lambda e, b=b: e.copy(out=vnb[:], in_=vtk[:, b, :]), reads=[(vtk, b)], writes=[vnb])
            for c4 in range(4):
                pt = nps()
                brow = bsrow[kind][0:1, c4 * 256:(c4 + 1) * 256].rearrange("o (g t) -> o g t", g=2).unsqueeze(2).to_broadcast([1, 2, 2, 128])
                pe(lambda e, pt=pt, brow=brow: e.matmul(pt.h[:, 0:512], lhsT=onesF[0:1, :], rhs=brow, start=True, stop=False),
                   reads=[onesF, bsrow[kind]], writes=[pt])
                for j in range(4):
                    ci = c4 * 4 + j
                    g = ci // 2
                    pe(lambda e, pt=pt, j=j, ci=ci, g=g: e.matmul(pt.h[:, j * 128:(j + 1) * 128], lhsT=vnb[:, ci * 128:(ci + 1) * 128],
                                                                 rhs=wsT[kind][:, g, :], start=False, stop=(j == 3)), reads=[vnb, wsT[kind]], writes=[pt])
                dve(lambda e, pt=pt, c4=c4, b=b: e.tensor_tensor(out=A1[:, c4 * 4:(c4 + 1) * 4, b * 128:(b + 1) * 128],
                                                                 in0=pt.h[:, 0:512].rearrange("p (j t) -> p j t", j=4),
                                                                 in1=A1[:, c4 * 4:(c4 + 1) * 4, b * 128:(b + 1) * 128], op=ALU.mult),
                    reads=[pt] + [(A1, c) for c in range(c4 * 4, c4 * 4 + 4)], writes=[(A1, c) for c in range(c4 * 4, c4 * 4 + 4)])
        linear_fm("w_sgu_out", 0, D, 16, A1, lambda k: A1[:, k, 0:NTt], NTt, resid_evac(NTt))
        ffn(0, NTt)
        rmsnorm_to_hT(NTt)
        if kind == 1:
            for c4 in range(8):
                P.dma(vtk[0:nseg * 3, 1, 0:512], sc.h.ap().rearrange("s i c -> (s i) c")[seqs[0] * 3:(seqs[0] + nseg) * 3, c4 * 512:(c4 + 1) * 512],
                      reads=[sc], writes=[(vtk, 1)])
                pt = nps()
                for j in range(4):
                    pe(lambda e, pt=pt, j=j: e.transpose(pt.h[:, j * 16:j * 16 + nseg * 3], vtk[0:nseg * 3, 1, j * 128:(j + 1) * 128],
                                                       identF[0:nseg * 3, 0:nseg * 3]), reads=[(vtk, 1), identF], writes=[pt])
                act(lambda e, pt=pt, c4=c4: e.copy(out=hal[:, c4 * 4:(c4 + 1) * 4, 0:nseg, :],
                                                   in_=pt.h[:, 0:64].rearrange("p (j x) -> p j x", j=4)[:, :, 0:nseg * 3].rearrange("p j (s i) -> p j s i", i=3)),
                    reads=[pt], writes=[hal])
        W = L + 3

        def ev_qkv(oc0, n, pt, ap3, mid):
            XC, XCK, CA, CAK, VT, VTK = convsets[convc[0] % 2]
            convc[0] += 1
            xv = XC[:, 0:n, 0:nseg * W].rearrange("p j (s w) -> p j s w", s=nseg)
            act(lambda e: e.copy(out=xv[:, :, :, 3:3 + L], in_=ap3.rearrange("p j (s l) -> p j s l", s=nseg)), reads=[pt], writes=[XCK])
            pool(lambda e: e.tensor_copy(out=xv[:, :, :, 0:3], in_=hal[:, oc0:oc0 + n, 0:nseg, :]), reads=[hal], writes=[XCK])
            pool(lambda e: e.tensor_copy(out=hal[:, oc0:oc0 + n, 0:nseg, :], in_=xv[:, :, :, L:L + 3]), reads=[XCK], writes=[hal])
            mid()
            avs = [CA[:, j, 0:NTt].rearrange("p (s l) -> p s l", s=nseg) for j in range(n)]
            CK = [(CAK if isinstance(CAK, tuple) else (CAK, None)) for j in range(n)]
            for j in range(n):
                c = oc0 + j
                dve(lambda e, j=j, c=c: e.tensor_scalar(out=avs[j], in0=xv[:, j, :, 0:L], scalar1=cw[:, 0, c:c + 1], scalar2=None, op0=ALU.mult),
                    reads=[XCK, cw], writes=[(CK[j][0], ("c", j))])
            for i in range(1, 4):
                for j in range(n):
                    c = oc0 + j
                    dve(lambda e, j=j, c=c, i=i: e.scalar_tensor_tensor(out=avs[j], in0=xv[:, j, :, i:i + L], scalar=cw[:, i, c:c + 1], in1=avs[j],
                                                                      op0=ALU.mult, op1=ALU.add), reads=[XCK, cw, (CK[j][0], ("c", j))], writes=[(CK[j][0], ("c", j))])
            KQK = [(kqT, c) for c in range(oc0, oc0 + n)]
            if oc0 < 16:
                act(lambda e: e.activation(out=kqT[:, oc0:oc0 + n, 0:NTt], in_=CA[:, 0:n, 0:NTt], func=AF.Silu), reads=[CAK], writes=KQK)
                act(lambda e: e.activation(out=VT[:, 0:n, 0:NTt], in_=kqT[:, oc0:oc0 + n, 0:NTt], func=AF.Square), reads=KQK, writes=[VTK])

                def partB1():
                    p2 = nps()
                    per_bank = max(1, 512 // NTt)
                    for j in range(n):
                        off = (j // per_bank) * 512 + (j % per_bank) * NTt
                        pe(lambda e, j=j, off=off, p2=p2: e.matmul(p2.h[:, off:off + NTt], lhsT=onesB[:], rhs=VT[:, j, 0:NTt], start=True, stop=True),
                           reads=[onesB, VTK], writes=[p2])
                    a3 = p2.h[:, 0:n * NTt].rearrange("p (j t) -> p j t", j=n)
                    act(lambda e: e.activation(out=CA[:, 0:n, 0:NTt], in_=a3, func=AF.Ln, bias=EPS), reads=[p2], writes=[CAK])
                    qb = -0.5 * float(np.log(128.0)) if oc0 < 8 else 0.0
                    act(lambda e: e.activation(out=CA[:, 0:n, 0:NTt], in_=CA[:, 0:n, 0:NTt], func=AF.Exp, scale=-0.5, bias=qb), reads=[CAK], writes=[CAK])

                def partB2():
                    dve(lambda e: e.tensor_tensor(out=kqT[:, oc0:oc0 + n, 0:NTt], in0=kqT[:, oc0:oc0 + n, 0:NTt], in1=CA[:, 0:n, 0:NTt], op=ALU.mult),
                        reads=KQK + [CAK], writes=KQK)
                    if oc0 >= 8:
                        for b in range(nb):
                            p3 = nps()
                            for j in range(n):
                                pe(lambda e, j=j, b=b, p3=p3: e.transpose(pbf(p3)[:, j * 128:(j + 1) * 128], kqT[:, oc0 + j, b * 128:(b + 1) * 128], identB[:]),
                                   reads=KQK + [identB], writes=[p3])
                            act(lambda e, b=b, p3=p3: e.copy(out=ktok[:, b, (oc0 - 8) * 128:(oc0 - 8 + n) * 128], in_=pbf(p3)[:, 0:n * 128]),
                                reads=[p3], writes=[(ktok, b)])
                return (partB1, partB2)
            else:
                act(lambda e: e.activation(out=VT[:, 0:n, 0:NTt], in_=CA[:, 0:n, 0:NTt], func=AF.Silu), reads=[CAK], writes=[VTK])

                def partB2():
                    for b in range(nb):
                        p3 = nps()
                        for j in range(n):
                            pe(lambda e, j=j, b=b, p3=p3: e.transpose(pbf(p3)[:, j * 128:(j + 1) * 128], VT[:, j, b * 128:(b + 1) * 128], identB[:]),
                               reads=[VTK, identB], writes=[p3])
                        act(lambda e, b=b, p3=p3: e.copy(out=vtok[:, b, (oc0 - 16) * 128:(oc0 - 16 + n) * 128], in_=pbf(p3)[:, 0:n * 128]),
                            reads=[p3], writes=[(vtok, b)])
                return (None, partB2)
        if full or need_q_halo:
            linear_fm("w_gdn_in", 0, 4096, 8, hT, hrhs, NTt, ev_qkv, piped=True)
        else:
            linear_fm("w_gdn_in", 1024, 4096, 8, hT, hrhs, NTt, ev_qkv, oc_base=8, piped=True)
        if (kind == 1 or last) and full:
            odst = scs if kind == 1 else scp
            orow = odst.h.ap().rearrange("s i c -> (s i) c") if kind == 1 else odst.h.ap()
            r0 = seqs[0] * 3 if kind == 1 else 0
            for c4 in range(8):
                pt = nps()
                for j in range(4):
                    pe(lambda e, pt=pt, j=j, c4=c4: e.transpose(pt.h[0:nseg * 3, j * 128:(j + 1) * 128],
                                                              hal[:, c4 * 4 + j, 0:nseg, :].rearrange("p s i -> p (s i)"), identF[:]),
                       reads=[hal, identF], writes=[pt])
                act(lambda e, pt=pt: e.copy(out=vtk[0:nseg * 3, 1, 0:512], in_=pt.h[0:nseg * 3, 0:512]), reads=[pt], writes=[(vtk, 1)])
                P.dma(orow[r0:r0 + nseg * 3, c4 * 512:(c4 + 1) * 512], vtk[0:nseg * 3, 1, 0:512], reads=[(vtk, 1)], writes=[odst])

        def ev_z(oc0, n, pt, ap3):
            act(lambda e: e.activation(out=A1[:, oc0:oc0 + n, 0:NTt], in_=ap3, func=AF.Silu), reads=[pt], writes=[(A1, c) for c in range(oc0, oc0 + n)])
        if full:
            linear_fm("w_gdn_in", 4096, 6144, 8, hT, hrhs, NTt, ev_z)
        gslot, gview, _, _ = next(wpieces("w_gdn_in", 6144, 6176, 8))
        if kind == 0 and first and not apply_flag:
            pool(lambda e: e.memset(Sf[0][:], 0.0), writes=[Sf[0]])
            pool(lambda e: e.memset(Sb[0][:], 0.0), writes=[Sb[0]])
        for b in range(nb):
            pg = nps()
            for k in range(8):
                pe(lambda e, k=k, b=b, pg=pg: e.matmul(pg.h[:, 0:32], lhsT=hT[:, k, b * 128:(b + 1) * 128], rhs=gview[:, k, 0:32],
                                                       start=(k == 0), stop=(k == 7)), reads=[hT, gslot], writes=[pg])
            act(lambda e, pg=pg: e.copy(out=gts_[b][:], in_=pg.h[:, 0:32]), reads=[pg], writes=[gts_[b]])
            G = lambda i, b=b: gsm_[b][:, i, :]
            act(lambda e: e.activation(out=G(0), in_=gts_[b][:, 0:16], func=AF.Exp, scale=-1.0), reads=[gts_[b]], writes=[gsm_[b]])
            dve(lambda e: e.tensor_scalar(out=G(0), in0=G(0), scalar1=1.0, scalar2=None, op0=ALU.add), reads=[gsm_[b]], writes=[gsm_[b]])
            dve(lambda e: e.reciprocal(out=G(1), in_=G(0)), reads=[gsm_[b]], writes=[gsm_[b]])
            dve(lambda e: e.tensor_scalar(out=G(2), in0=G(1), scalar1=-1.0, scalar2=None, op0=ALU.mult), reads=[gsm_[b]], writes=[gsm_[b]])
            dve(lambda e: e.tensor_tensor(out=G(9), in0=gts_[b][:, 16:32], in1=dtr[:], op=ALU.add), reads=[gts_[b], dtr], writes=[gsm_[b]])
            act(lambda e: e.activation(out=G(9), in_=G(9), func=AF.Exp), reads=[gsm_[b]], writes=[gsm_[b]])
            act(lambda e: e.activation(out=G(9), in_=G(9), func=AF.Ln, bias=1.0), reads=[gsm_[b]], writes=[gsm_[b]])
            dve(lambda e: e.tensor_tensor(out=G(3), in0=G(9), in1=alr[:], op=ALU.mult), reads=[gsm_[b], alr], writes=[gsm_[b]])
            pc_ = nps()
            pe(lambda e, pc_=pc_: e.matmul(pc_.h[:, 0:16], lhsT=incl[:], rhs=G(3), start=True, stop=True), reads=[incl, gsm_[b]], writes=[pc_])
            pe(lambda e, pc_=pc_: e.matmul(pc_.h[:, 16:32], lhsT=blk1[:], rhs=G(3), start=True, stop=True), reads=[blk1, gsm_[b]], writes=[pc_])
            pe(lambda e, pc_=pc_: e.matmul(pc_.h[:, 32:48], lhsT=selA[:], rhs=G(3), start=True, stop=True), reads=[selA, gsm_[b]], writes=[pc_])
            pe(lambda e, pc_=pc_: e.matmul(pc_.h[:, 48:64], lhsT=selB[:], rhs=G(3), start=True, stop=True), reads=[selB, gsm_[b]], writes=[pc_])
            act(lambda e, pc_=pc_: e.copy(out=G(4), in_=pc_.h[:, 0:16]), reads=[pc_], writes=[gsm_[b]])
            act(lambda e, pc_=pc_: e.activation(out=G(5), in_=pc_.h[:, 0:16], func=AF.Exp), reads=[pc_], writes=[gsm_[b]])
            dve(lambda e, pc_=pc_: e.tensor_tensor(out=G(6), in0=pc_.h[:, 16:32], in1=G(4), op=ALU.subtract), reads=[pc_, gsm_[b]], writes=[gsm_[b]])
            act(lambda e: e.activation(out=G(6), in_=G(6), func=AF.Exp), reads=[gsm_[b]], writes=[gsm_[b]])
            act(lambda e, pc_=pc_: e.activation(out=G(7), in_=pc_.h[:, 32:48], func=AF.Exp), reads=[pc_], writes=[gsm_[b]])
            act(lambda e, pc_=pc_: e.activation(out=G(8), in_=pc_.h[:, 48:64], func=AF.Exp), reads=[pc_], writes=[gsm_[b]])
            dbg("gts", gts_[b], gts_[b][:], [128, 32])
            dbg("gsm", gsm_[b], gsm_[b][:], [128, 12, 16])
        for b in range(nb):
            gsm = gsm_[b]
            G = lambda i, b=b: gsm_[b][:, i, :]
            if kind == 1:
                for ch in range(2):
                    sq_ = seqs[0] + b * 2 + ch
                    P.dma(Sf[ch][:], sg.h.ap()[sq_].rearrange("h k v -> k h v"), reads=[sg], writes=[Sf[ch]])
                    act(lambda e, ch=ch: e.copy(out=Sb[ch][:], in_=Sf[ch][:]), reads=[Sf[ch]], writes=[Sb[ch]])
            bs_ = slice(b * 128, (b + 1) * 128)
            NG = NGROUPS
            HG = 8 // NG
            GW = HG * 128
            grp = list(range(NG))
            for hf in range(2):
                h0 = hf * 8
                hs = [slice(g * HG, (g + 1) * HG) for g in grp]
                hgl = [slice(h0 + g * HG, h0 + (g + 1) * HG) for g in grp]
                K_ = lambda t, g: (t, ("g", g))
                X032, XT32, X132 = Gtri, E1, Es
                fl = lambda t, g: t[:, hs[g], :].rearrange("p h t -> p (h t)")
                prt_, pgr_, pkq_ = {}, {}, {}
                for g in grp:
                    dve(lambda e, g=g: e.tensor_tensor(out=Gtri[:, hs[g], :], in0=G(3)[:, hgl[g]].unsqueeze(2).to_broadcast([128, HG, 128]),
                                                       in1=incl[:].unsqueeze(1).to_broadcast([128, HG, 128]), op=ALU.mult),
                        reads=[gsm, incl], writes=[K_(Gtri, g)])
                for g in grp:
                    prt = nps1()
                    prt_[g] = prt
                    pe(lambda e, g=g, prt=prt: e.matmul(prt.ap(0, GW), lhsT=onesF[:], rhs=fl(Gtri, g), start=True, stop=False),
                       reads=[onesF, K_(Gtri, g)], writes=[prt.key])
                    pe(lambda e, g=g, prt=prt: e.matmul(prt.ap(0, GW), lhsT=identB[:], rhs=negm[:].unsqueeze(1).to_broadcast([128, HG, 128]),
                                                        start=False, stop=True), reads=[identB, negm], writes=[prt.key])
                    if full:
                        pgr = nps1()
                        pgr_[g] = pgr
                        pe(lambda e, g=g, pgr=pgr: e.matmul(pgr.ap(0, GW), lhsT=onesF[:], rhs=fl(Gtri, g), start=True, stop=True),
                           reads=[onesF, K_(Gtri, g)], writes=[pgr.key])
                v3 = lambda ph: ph.ap(0, GW).rearrange("p (h t) -> p h t", h=HG)
                for g in grp:
                    dve(lambda e, g=g: e.tensor_tensor(out=E1[:, hs[g], :], in0=v3(prt_[g]), in1=G(4)[:, hgl[g]].unsqueeze(2).to_broadcast([128, HG, 128]),
                                                       op=ALU.subtract), reads=[prt_[g].key, gsm], writes=[K_(E1, g)])
                for g in grp:
                    act(lambda e, g=g: e.activation(out=E1[:, hs[g], :], in_=E1[:, hs[g], :], func=AF.Exp), reads=[K_(E1, g)], writes=[K_(E1, g)])
                for g in grp:
                    pool(lambda e, g=g: e.tensor_tensor(out=Es[:, hs[g], :], in0=E1[:, hs[g], :], in1=strict[:].unsqueeze(1).to_broadcast([128, HG, 128]), op=ALU.mult),
                         reads=[K_(E1, g), strict], writes=[K_(Es, g)])
                for g in grp:
                    pkq = nps1()
                    pkq_[g] = pkq
                    for j in range(HG // 2):
                        hk = hf * 4 + g * (HG // 2) + j
                        if full:
                            pe(lambda e, j=j, hk=hk, pkq=pkq: e.matmul(pkq.ap(j * 256, (j + 1) * 256), lhsT=kqT[:, 8 + hk, bs_], rhs=kqT[:, hk:hk + 9:8, bs_],
                                                                       start=True, stop=True), reads=[kqT], writes=[pkq.key])
                        else:
                            pe(lambda e, j=j, hk=hk, pkq=pkq: e.matmul(pkq.ap(j * 256 + 128, (j + 1) * 256), lhsT=kqT[:, 8 + hk, bs_], rhs=kqT[:, 8 + hk, bs_],
                                                                       start=True, stop=True), reads=[(kqT, 8 + hk)], writes=[pkq.key])
                kq4 = lambda ph: ph.ap(0, GW).rearrange("p (j c t) -> p j c t", j=HG // 2, c=2)
                if full:
                    for g in grp:
                        dve(lambda e, g=g: e.tensor_tensor(out=PT[:, hs[g], :].rearrange("p (j r) t -> p j r t", r=2),
                                                           in0=kq4(pkq_[g])[:, :, 0:1, :].to_broadcast([128, HG // 2, 2, 128]),
                                                           in1=E1[:, hs[g], :].rearrange("p (j r) t -> p j r t", r=2), op=ALU.mult),
                            reads=[pkq_[g].key, K_(E1, g)], writes=[K_(PT, g)])
                for g in grp:
                    for hh in range(HG):
                        hw = g * HG + hh
                        dve(lambda e, g=g, hh=hh, hw=hw: e.scalar_tensor_tensor(out=X032[:, hw, :], in0=kq4(pkq_[g])[:, hh // 2, 1, :], scalar=G(2)[:, h0 + hw:h0 + hw + 1],
                                                                               in1=Es[:, hw, :], op0=ALU.mult, op1=ALU.mult),
                            reads=[pkq_[g].key, gsm, K_(Es, g)], writes=[K_(X032, g)])
                if full:
                    for g in grp:
                        act(lambda e, g=g: e.activation(out=Es[:, hs[g], :], in_=v3(pgr_[g]), func=AF.Exp), reads=[pgr_[g].key], writes=[K_(Es, g)])
                    for g in grp:
                        q4 = kqT[:, hf * 4 + g * (HG // 2):hf * 4 + (g + 1) * (HG // 2), bs_].unsqueeze(2).to_broadcast([128, HG // 2, 2, 128])
                        dve(lambda e, g=g, q4=q4: e.tensor_tensor(out=qdT[:, hs[g], :].rearrange("p (j r) t -> p j r t", r=2), in0=q4,
                                                                  in1=Es[:, hs[g], :].rearrange("p (j r) t -> p j r t", r=2), op=ALU.mult),
                            reads=[kqT, K_(Es, g)], writes=[K_(qdT, g)])
                for g in grp:
                    pxt = nps1()
                    for hh in range(HG):
                        pe(lambda e, g=g, hh=hh, pxt=pxt: e.transpose(pxt.ap(hh * 128, (hh + 1) * 128), X032[:, g * HG + hh, :], identF[:]),
                           reads=[K_(X032, g), identF], writes=[pxt.key])
                    act(lambda e, g=g, pxt=pxt: e.copy(out=fl(XT32, g), in_=pxt.ap(0, GW)), reads=[pxt.key], writes=[K_(XT32, g)])
                for g in grp:
                    dve(lambda e, g=g: e.tensor_tensor(out=P32[:, hs[g], :], in0=X032[:, hs[g], :], in1=identF[:].unsqueeze(1).to_broadcast([128, HG, 128]), op=ALU.add),
                        reads=[K_(X032, g), identF], writes=[K_(P32, g)])
                for g in grp:
                    px = nps1()
                    for hh in range(HG):
                        hw = g * HG + hh
                        pe(lambda e, hh=hh, hw=hw, px=px: e.matmul(px.ap(hh * 128, (hh + 1) * 128), lhsT=XT32[:, hw, :], rhs=X032[:, hw, :], start=True, stop=True),
                           reads=[K_(XT32, g), K_(X032, g)], writes=[px.key])
                    act(lambda e, g=g, px=px: e.copy(out=fl(X132, g), in_=px.ap(0, GW)), reads=[px.key], writes=[K_(X132, g)])
                    act(lambda e, g=g: e.copy(out=Xa[1][:, hs[g], :], in_=X132[:, hs[g], :]), reads=[K_(X132, g)], writes=[K_(Xa[1], g)])
                for g in grp:
                    pxT = nps1()
                    for hh in range(HG):
                        hw = g * HG + hh
                        pe(lambda e, hh=hh, hw=hw, pxT=pxT: e.matmul(pxT.ap(hh * 128, (hh + 1) * 128), lhsT=X032[:, hw, :], rhs=XT32[:, hw, :], start=True, stop=True),
                           reads=[K_(XT32, g), K_(X032, g)], writes=[pxT.key])
                    dve(lambda e, g=g, pxT=pxT: e.tensor_copy(out=fl(Xt[1], g), in_=pxT.ap(0, GW)), reads=[pxT.key], writes=[K_(Xt[1], g)])
                for g in grp:
                    ppm = nps1()
                    for hh in range(HG):
                        hw = g * HG + hh
                        pe(lambda e, hh=hh, hw=hw, ppm=ppm: e.matmul(ppm.ap(hh * 128, (hh + 1) * 128), lhsT=XT32[:, hw, :], rhs=X132[:, hw, :], start=True, stop=True),
                           reads=[K_(XT32, g), K_(X132, g)], writes=[ppm.key])
                    dve(lambda e, g=g: e.tensor_tensor(out=P32[:, hs[g], :], in0=P32[:, hs[g], :], in1=X132[:, hs[g], :], op=ALU.add),
                        reads=[K_(P32, g), K_(X132, g)], writes=[K_(P32, g)])
                    dve(lambda e, g=g, ppm=ppm: e.tensor_tensor(out=fl(P32, g), in0=ppm.ap(0, GW), in1=fl(P32, g), op=ALU.add),
                        reads=[ppm.key, K_(P32, g)], writes=[K_(P32, g)])
                    act(lambda e, g=g: e.copy(out=Pm[1][:, hs[g], :], in_=P32[:, hs[g], :]), reads=[K_(P32, g)], writes=[K_(Pm[1], g)])
                cur = 1
                for lev in range(2, 6):
                    nxt = 1 - cur
                    if lev < 5:
                        for g in grp:
                            px = nps1()
                            for hh in range(HG):
                                hw = g * HG + hh
                                pe(lambda e, hh=hh, hw=hw, px=px, cur=cur: e.matmul(px.ap(hh * 128, (hh + 1) * 128), lhsT=Xt[cur][:, hw, :], rhs=Xa[cur][:, hw, :],
                                                                                    start=True, stop=True), reads=[K_(Xt[cur], g), K_(Xa[cur], g)], writes=[px.key])
                            act(lambda e, g=g, px=px, nxt=nxt: e.copy(out=fl(Xa[nxt], g), in_=px.ap(0, GW)), reads=[px.key], writes=[K_(Xa[nxt], g)])
                    pxTs = {}
                    for g in grp:
                        pxT = nps1()
                        for hh in range(HG):
                            hw = g * HG + hh
                            pe(lambda e, hh=hh, hw=hw, pxT=pxT, cur=cur: e.matmul(pxT.ap(hh * 128, (hh + 1) * 128), lhsT=Xa[cur][:, hw, :], rhs=Xt[cur][:, hw, :],
                                                                                  start=True, stop=True), reads=[K_(Xt[cur], g), K_(Xa[cur], g)], writes=[pxT.key])
                        dve(lambda e, g=g, pxT=pxT, nxt=nxt: e.tensor_copy(out=fl(Xt[nxt], g), in_=pxT.ap(0, GW)), reads=[pxT.key], writes=[K_(Xt[nxt], g)])
                    for g in grp:
                        ppm = nps1()
                        for hh in range(HG):
                            hw = g * HG + hh
                            pe(lambda e, hh=hh, hw=hw, ppm=ppm, nxt=nxt, cur=cur: e.matmul(ppm.ap(hh * 128, (hh + 1) * 128), lhsT=Xt[nxt][:, hw, :], rhs=Pm[cur][:, hw, :],
                                                                                           start=True, stop=True), reads=[K_(Xt[nxt], g), K_(Pm[cur], g)], writes=[ppm.key])
                        dve(lambda e, g=g, ppm=ppm: e.tensor_tensor(out=fl(P32, g), in0=ppm.ap(0, GW), in1=fl(P32, g), op=ALU.add),
                            reads=[ppm.key, K_(P32, g)], writes=[K_(P32, g)])
                        act(lambda e, g=g, nxt=nxt: e.copy(out=Pm[nxt][:, hs[g], :], in_=P32[:, hs[g], :]), reads=[K_(P32, g)], writes=[K_(Pm[nxt], g)])
                    cur = nxt
                Tt = Pm[cur]
                dbg("Tt", Tt, Tt[:], [128, 8, 128], BF16)
                for g in grp:
                    kc0 = hf * 512 + g * (HG // 2) * 128
                    k4 = ktok[:, b, kc0:kc0 + (HG // 2) * 128].rearrange("p (j d) -> p j d", j=HG // 2).unsqueeze(2).to_broadcast([128, HG // 2, 2, 128])
                    for dst_, gi in ((kg, 5), (kd, 6)):
                        pool(lambda e, g=g, k4=k4, dst_=dst_, gi=gi: e.tensor_tensor(
                            out=dst_[:, hs[g], :].rearrange("p (j r) d -> p j r d", r=2), in0=k4,
                            in1=G(gi)[:, hgl[g]].rearrange("p (j r) -> p j r", r=2).unsqueeze(3).to_broadcast([128, HG // 2, 2, 128]), op=ALU.mult),
                            reads=[(ktok, b), gsm], writes=[K_(dst_, g)])
                for g in grp:
                    pw = nps1()
                    for hh in range(HG):
                        hw = g * HG + hh
                        pe(lambda e, hh=hh, hw=hw, pw=pw, Tt=Tt: e.matmul(pw.ap(hh * 128, (hh + 1) * 128), lhsT=kg[:, hw, :], rhs=Tt[:, hw, :], start=True, stop=True),
                           reads=[K_(kg, g), K_(Tt, g)], writes=[pw.key])
                    act(lambda e, g=g, pw=pw: e.mul(out=fl(nWT, g), in_=pw.ap(0, GW), mul=-1.0), reads=[pw.key], writes=[K_(nWT, g)])
                for ch in range(2):
                    si = ch if kind == 1 else 0
                    r_ = slice(ch * 64, ch * 64 + 64)
                    SK = lambda t, g: (t, ("s", hf, g))
                    for g in grp:
                        pd = nps1()
                        for hh in range(HG):
                            hw = g * HG + hh
                            hg = h0 + hw
                            pe(lambda e, hh=hh, hw=hw, hg=hg, pd=pd, Tt=Tt, r_=r_: e.matmul(pd.ap(hh * 128, (hh + 1) * 128, r_), lhsT=Tt[r_, hw, r_], rhs=vtok[r_, b, hg * 128:(hg + 1) * 128],
                                                                                           start=True, stop=False), reads=[K_(Tt, g), (vtok, b)], writes=[pd.key])
                            pe(lambda e, hh=hh, hw=hw, hg=hg, pd=pd, r_=r_, si=si: e.matmul(pd.ap(hh * 128, (hh + 1) * 128, r_), lhsT=nWT[:, hw, r_], rhs=Sb[si][:, hg, :],
                                                                                           start=False, stop=True), reads=[K_(nWT, g), SK(Sb[si], g)], writes=[pd.key])
                        dve(lambda e, g=g, pd=pd, r_=r_: e.tensor_tensor(out=dlt[r_, hs[g], :], in0=pd.ap(0, GW, r_).rearrange("p (h d) -> p h d", h=HG),
                                                                         in1=G(1)[r_, hgl[g]].unsqueeze(2).to_broadcast([64, HG, 128]), op=ALU.mult),
                            reads=[pd.key, gsm], writes=[(dlt, (ch, g))])
                    pos = {}
                    if full:
                        for g in grp:
                            po = nps1()
                            pos[g] = po
                            for hh in range(HG):
                                hw = g * HG + hh
                                hg = h0 + hw
                                pe(lambda e, hh=hh, hw=hw, hg=hg, po=po, r_=r_, si=si: e.matmul(po.ap(hh * 128, (hh + 1) * 128, r_), lhsT=qdT[:, hw, r_], rhs=Sb[si][:, hg, :],
                                                                                               start=True, stop=False), reads=[K_(qdT, g), SK(Sb[si], g)], writes=[po.key])
                                pe(lambda e, hh=hh, hw=hw, po=po, r_=r_: e.matmul(po.ap(hh * 128, (hh + 1) * 128, r_), lhsT=PT[r_, hw, r_], rhs=dlt[r_, hw, :],
                                                                                 start=False, stop=True), reads=[K_(PT, g), (dlt, (ch, g))], writes=[po.key])
                    for g in grp:
                        psu = nps1()
                        for hh in range(HG):
                            hw = g * HG + hh
                            pe(lambda e, hh=hh, hw=hw, psu=psu, r_=r_: e.matmul(psu.ap(hh * 128, (hh + 1) * 128), lhsT=kd[r_, hw, :], rhs=dlt[r_, hw, :], start=True, stop=True),
                               reads=[K_(kd, g), (dlt, (ch, g))], writes=[psu.key])
                        for hh in range(HG):
                            hg = h0 + g * HG + hh
                            dve(lambda e, hh=hh, hg=hg, psu=psu, si=si, ch=ch: e.scalar_tensor_tensor(out=Sf[si][:, hg, :], in0=Sf[si][:, hg, :],
                                                                                                     scalar=G(7 + ch)[:, hg:hg + 1], in1=psu.ap(hh * 128, (hh + 1) * 128),
                                                                                                     op0=ALU.mult, op1=ALU.add),
                                reads=[psu.key, gsm, SK(Sf[si], g)], writes=[SK(Sf[si], g)])
                        act(lambda e, g=g, si=si: e.copy(out=Sb[si][:, hgl[g], :], in_=Sf[si][:, hgl[g], :]), reads=[SK(Sf[si], g)], writes=[SK(Sb[si], g)])
                    if full:
                        for g in grp:
                            po = pos[g]
                            o3 = po.ap(0, GW, r_).rearrange("p (h d) -> p h d", h=HG)
                            OK_ = (oss, (ch, g))
                            act(lambda e, g=g, o3=o3: e.activation(out=E1[r_, hs[g], :], in_=o3, func=AF.Square), reads=[po.key], writes=[(E1, ("o", ch, g))])
                            dve(lambda e, g=g: e.tensor_reduce(out=oss[r_, 0, hs[g]], in_=E1[r_, hs[g], :], axis=AX.X, op=ALU.add), reads=[(E1, ("o", ch, g))], writes=[OK_])
                            act(lambda e, g=g: e.activation(out=oss[r_, 1, hs[g]], in_=oss[r_, 0, hs[g]], func=AF.Ln, scale=1.0 / 128, bias=EPS), reads=[OK_], writes=[OK_])
                            act(lambda e, g=g: e.activation(out=oss[r_, 2, hs[g]], in_=oss[r_, 1, hs[g]], func=AF.Exp, scale=-0.5), reads=[OK_], writes=[OK_])
                            dve(lambda e, g=g, o3=o3: e.tensor_tensor(out=onb[r_, hs[g], :], in0=o3, in1=oss[r_, 2, hs[g]].unsqueeze(2).to_broadcast([64, HG, 128]),
                                                                      op=ALU.mult), reads=[po.key, OK_], writes=[(onb, (ch, g))])
                    if kind == 1 and hf == 1:
                        sq_ = seqs[0] + b * 2 + ch
                        P.dma(sgs.h.ap()[sq_].rearrange("h k v -> k h v"), Sf[si][:], reads=[Sf[si]], writes=[sgs])
                dbg("onb", onb, onb[:], [128, 8, 128], BF16)
                if full:
                    for g in grp:
                        pot = nps1()
                        for hh in range(HG):
                            pe(lambda e, g=g, hh=hh, pot=pot: e.transpose(pot.apb(hh * 128, (hh + 1) * 128), onb[:, g * HG + hh, :], identB[:]),
                               reads=[(onb, (0, g)), (onb, (1, g)), identB], writes=[pot.key])
                        a0 = h0 + g * HG
                        dve(lambda e, g=g, pot=pot, a0=a0: e.tensor_tensor(out=A1[:, a0:a0 + HG, bs_], in0=pot.apb(0, GW).rearrange("p (h t) -> p h t", h=HG),
                                                                           in1=A1[:, a0:a0 + HG, bs_], op=ALU.mult),
                            reads=[pot.key] + [(A1, a0 + i) for i in range(HG)], writes=[(A1, a0 + i) for i in range(HG)])
        if kind == 0 and last and full:
            P.dma(sgp.h.ap().rearrange("h k v -> k h v"), Sf[0][:], reads=[Sf[0]], writes=[sgp])
        if not full:
            return
        linear_fm("w_gdn_out", 0, D, 16, A1, lambda k: A1[:, k, 0:NTt], NTt, resid_evac(NTt))
        ffn(1, NTt)
        act(lambda e: e.activation(out=hT[:, :, 0:NTt], in_=xT[:, :, 0:NTt], func=AF.Square), reads=[xT], writes=[hT])
        pt = nps()
        for k in range(8):
            pe(lambda e, k=k, pt=pt: e.matmul(pt.h[:, 0:NTt], lhsT=onesB[:], rhs=hT[:, k, 0:NTt], start=(k == 0), stop=(k == 7)), reads=[onesB, hT], writes=[pt])
        act(lambda e, pt=pt: e.activation(out=cacc[:, 0, 0:NTt], in_=pt.h[:, 0:NTt], func=AF.Ln, scale=1.0 / D, bias=EPS), reads=[pt], writes=[cacc])
        act(lambda e: e.activation(out=cacc[:, 0, 0:NTt], in_=cacc[:, 0, 0:NTt], func=AF.Exp, scale=-0.5), reads=[cacc], writes=[cacc])
        for b in range(nb):
            prs = nps()
            pe(lambda e, b=b, prs=prs: e.transpose(prs.h[:, 0:128], cacc[:, 0, b * 128:(b + 1) * 128], identF[:]), reads=[cacc, identF], writes=[prs])
            act(lambda e, prs=prs: e.copy(out=bmv[:, 3:4], in_=prs.h[:, 0:1]), reads=[prs], writes=[bmv])
            for hf in range(2):
                pt = nps()
                for j in range(4):
                    k = hf * 4 + j
                    pe(lambda e, pt=pt, j=j, k=k, b=b: e.transpose(pt.h[:, j * 128:(j + 1) * 128], xT[:, k, b * 128:(b + 1) * 128], identF[:]),
                       reads=[xT, identF], writes=[pt])
                dve(lambda e, pt=pt, hf=hf: e.scalar_tensor_tensor(out=vtk[:, 1, hf * 512:(hf + 1) * 512], in0=pt.h[:, 0:512], scalar=bmv[:, 3:4],
                                                                   in1=wfin[:, hf * 512:(hf + 1) * 512], op0=ALU.mult, op1=ALU.mult),
                    reads=[pt, bmv, wfin], writes=[(vtk, 1)])
            P.dma(dst.h.ap()[tok0 + b * 128: tok0 + (b + 1) * 128, :], vtk[:, 1, 0:1024], reads=[(vtk, 1)], writes=[dst], eng="act")

    if n_ptiles:
        setup_sgu_kind(0)
    for ti in range(n_state):
        run_tile(0, ti * NT, NT, ti == 0, False, None, mode="state", need_q_halo=(ti == n_state - 1))
    for ti in range(n_ptiles):
        run_tile(0, ti * NT, NT, ti == 0, ti == n_ptiles - 1, None, apply_flag=(ti == 0 and n_state > 0))
    if NS:
        setup_sgu_kind(1)
        run_tile(1, 0, NSAMP, True, True, (0,))
    stats = P.emit()
    return nc, stats


WNAMES = ["norm_mix", "norm_ffn", "norm_final", "sgu_w_in", "sgu_ln_g", "sgu_ln_b", "sgu_w_s", "sgu_b_s", "sgu_w_out",
          "gdn_w_in", "gdn_w_conv", "gdn_a_log", "gdn_dt_bias", "gdn_w_onorm", "gdn_w_out", "ffn_w_gate", "ffn_w_up", "ffn_w_down"]
SQUEEZE0 = {"sgu_w_in", "sgu_ln_g", "sgu_ln_b", "sgu_w_s", "sgu_b_s", "sgu_w_out", "gdn_w_in", "gdn_w_conv", "gdn_a_log",
            "gdn_dt_bias", "gdn_w_onorm", "gdn_w_out"}

_CACHE = {}


def kernel(**inputs):
    x_prompt = np.ascontiguousarray(inputs["x_prompt"], dtype=np.float32)
    x_sample = np.ascontiguousarray(inputs["x_sample"], dtype=np.float32)
    state_gdn = np.ascontiguousarray(inputs["state_gdn"], dtype=np.float32)
    state_conv = np.ascontiguousarray(inputs["state_conv"], dtype=np.float32)
    B, SEQ, _ = x_prompt.shape
    DB, DS, _ = x_sample.shape
    n_cores = 8
    NT = 256
    NS = DB // n_cores
    HALF = SEQ // 2
    n_half = HALF // NT
    key = (n_half, NT, NS)
    if key not in _CACHE:
        _CACHE[key] = build_program(n_half, NT, NS, n_state=n_half)[0]
    nc = _CACHE[key]
    wd = {}
    for n in WNAMES:
        a = np.ascontiguousarray(inputs[n], dtype=np.float32)
        wd[n] = a[0] if n in SQUEEZE0 else a
    in_maps = []
    for c in range(n_cores):
        m = dict(wd)
        sq, second = c % B, c // B
        m["xq"] = x_prompt[sq, :HALF]
        m["xp"] = x_prompt[sq, HALF:] if second else x_prompt[sq, :HALF]
        m["flag"] = np.array([1.0 if second else 0.0], dtype=np.float32)
        m["xs"] = x_sample[c * NS:(c + 1) * NS].reshape(NS * DS, D)
        m["sg"] = state_gdn[0, c * NS:(c + 1) * NS]
        m["sc"] = state_conv[0, c * NS:(c + 1) * NS]
        in_maps.append(m)
    res = run_bass_kernel_spmd(nc, in_maps, core_ids=list(range(n_cores)))
    r = res.results
    y_prompt = np.stack([np.concatenate([r[c]["yp"], r[c + B]["yp"]], axis=0) for c in range(B)]).astype(np.float32)
    y_sample = np.concatenate([r[c]["ys"].reshape(NS, DS, D) for c in range(n_cores)]).astype(np.float32)
    ns_gdn_p = np.stack([r[c + B]["sgp"] for c in range(B)])[None].astype(np.float32)
    ns_conv_p = np.stack([r[c + B]["scp"] for c in range(B)])[None].astype(np.float32)
    ns_gdn_s = np.concatenate([r[c]["sgs"] for c in range(n_cores)])[None].astype(np.float32)
    ns_conv_s = np.concatenate([r[c]["scs"] for c in range(n_cores)])[None].astype(np.float32)
    ns_v_s = np.concatenate([r[c]["svs"].reshape(NS, DS, DSGU) for c in range(n_cores)])[None].astype(np.float32)
    return (y_prompt, y_sample, ns_gdn_p, ns_conv_p, ns_gdn_s, ns_conv_s, ns_v_s)
```

```python
import numpy as np
import concourse.bass as bass
import concourse.mybir as mybir
from concourse.bass_utils import run_bass_kernel_spmd

F32 = mybir.dt.float32
BF16 = mybir.dt.bfloat16
AF = mybir.ActivationFunctionType
ALU = mybir.AluOpType
AX = mybir.AxisListType

COMPUTE = ("pe", "act", "dve", "pool")


class T:
    __slots__ = ("h", "name", "last_w", "readers", "dma_sem", "kind")

    def __init__(self, h, name, kind="sb"):
        self.h = h
        self.name = name
        self.kind = kind
        self.last_w = {}
        self.readers = {}
        self.dma_sem = None

    def __getitem__(self, k):
        return self.h[k]


class Op:
    __slots__ = ("eng", "fn", "deps", "idx", "signal", "sigval", "is_dma", "dsem", "dval")

    def __init__(self, eng, fn):
        self.eng = eng
        self.fn = fn
        self.deps = []
        self.signal = False
        self.sigval = None
        self.is_dma = False
        self.dsem = None
        self.dval = None


class _Rec:
    def __getattr__(self, name):
        def f(*a, **k):
            self.call = (name, a, k)
            return self
        return f


def _replay(name, a, k):
    return lambda e: getattr(e, name)(*a, **k)


class Prog:
    def __init__(self, nc):
        self.nc = nc
        self.q = {e: [] for e in ("pe", "act", "dve", "pool", "sp")}
        self.tiles = []
        self._sems = []

    def sb(self, name, shape, dtype):
        t = T(self.nc.alloc_sbuf_tensor(name, list(shape), dtype), name)
        self.tiles.append(t)
        return t

    def ps(self, name, shape, dtype=F32):
        t = T(self.nc.alloc_psum_tensor(name, list(shape), dtype), name, "ps")
        self.tiles.append(t)
        return t

    def dram(self, name, shape, dtype, kind="Internal"):
        t = T(self.nc.dram_tensor(name, list(shape), dtype, kind=kind), name, "dram")
        self.tiles.append(t)
        return t

    def ext(self, h, name):
        t = T(h, name, "dram")
        self.tiles.append(t)
        return t

    def new_sem(self, name):
        cm = self.nc.semaphore(name)
        s = cm.__enter__()
        self._sems.append(cm)
        return s

    @staticmethod
    def _conf(k1, k2):
        return k1 is None or k2 is None or k1 == k2

    @staticmethod
    def _rk(x):
        return (x, None) if isinstance(x, T) else x

    def _add_deps(self, op, reads, writes, extra=()):
        deps = list(extra)
        for (t, k) in reads:
            for kk, w in t.last_w.items():
                if self._conf(k, kk):
                    deps.append(w)
        for (t, k) in writes:
            for kk, w in t.last_w.items():
                if self._conf(k, kk):
                    deps.append(w)
            for kk, rs in t.readers.items():
                if self._conf(k, kk):
                    deps.extend(rs)
        best = {}
        for d in deps:
            if d is op:
                continue
            if d.is_dma:
                key = ("dma", id(d.dsem))
                if key not in best or best[key].dval < d.dval:
                    best[key] = d
            else:
                if d.eng == op.eng and op.eng == "pe":
                    continue
                key = d.eng
                if key not in best or best[key].idx < d.idx:
                    best[key] = d
        for d in best.values():
            op.deps.append(d)
            if not d.is_dma:
                d.signal = True
        for (t, k) in reads:
            t.readers.setdefault(k, []).append(op)
        for (t, k) in writes:
            if k is None:
                t.last_w = {None: op}
                t.readers = {}
            else:
                t.last_w[k] = op
                t.readers[k] = []

    def op(self, eng, fn, reads=(), writes=(), extra=()):
        rec = _Rec()
        fn(rec)
        o = Op(eng, _replay(*rec.call))
        o.idx = len(self.q[eng])
        self._add_deps(o, [self._rk(r) for r in reads], [self._rk(w) for w in writes], extra)
        self.q[eng].append(o)
        return o

    def dma(self, out_ap, in_ap, reads=(), writes=(), sem_of=None, extra=(), eng="sp"):
        reads = [self._rk(r) for r in reads]
        writes = [self._rk(w) for w in writes]
        if sem_of is None:
            cands = [x for x in list(writes) + list(reads) if x[0].kind == "sb"]
            sem_of = cands[0] if cands else (list(writes) + list(reads))[0]
        st, sk = self._rk(sem_of)
        if st.dma_sem is None:
            st.dma_sem = {}
        if sk not in st.dma_sem:
            st.dma_sem[sk] = [self.new_sem("d%d" % len(self._sems)), 0]
        ent = st.dma_sem[sk]
        ent[1] += 16
        o = Op(eng, lambda e: e.dma_start(out=out_ap, in_=in_ap))
        o.is_dma = True
        o.dsem = ent[0]
        o.dval = ent[1]
        o.idx = len(self.q[eng])
        self._add_deps(o, reads, writes, extra)
        self.q[eng].append(o)
        return o

    def emit(self):
        nc = self.nc
        sems = {e: self.new_sem("s_" + e) for e in COMPUTE}
        for e in COMPUTE:
            c = 0
            for o in self.q[e]:
                if (not o.is_dma) and o.signal:
                    c += 1
                    o.sigval = c
        engmap = {"pe": "tensor", "act": "scalar", "dve": "vector", "pool": "gpsimd", "sp": "sync"}
        stats = {"waits": 0, "instr": 0}

        def emit_queue(ename, engine):
            known = {}
            for o in self.q[ename]:
                for d in o.deps:
                    if d.is_dma:
                        key, val, sem = ("dma", id(d.dsem)), d.dval, d.dsem
                    else:
                        key, val, sem = d.eng, d.sigval, sems[d.eng]
                    if known.get(key, 0) >= val:
                        continue
                    engine.wait_ge(sem, val)
                    stats["waits"] += 1
                    known[key] = val
                ins = o.fn(engine)
                stats["instr"] += 1
                if o.is_dma:
                    ins.then_inc(o.dsem, 16)
                elif o.signal:
                    ins.then_inc(sems[ename], 1)
            if ename == "sp":
                for t in self.tiles:
                    if t.dma_sem:
                        for ent in t.dma_sem.values():
                            engine.wait_ge(ent[0], ent[1])

        with nc.Block() as block:
            for ename in ["sp", "pool", "act", "dve", "pe"]:
                getattr(block, engmap[ename])(lambda engine, _n=ename: emit_queue(_n, engine))
        return stats


D = 1024
DSGU = 2048
DFF = 2816
GIN = 6176
EPS = 1e-6
NEG = -30000.0
WSLOT = 4096


DBG = []
NGROUPS = 2


def build_program(n_ptiles, NT, NS, n_state=0):
    assert NT % 128 == 0 and NS % 2 == 0
    nc = bass.Bass("TRN2", target_bir_lowering=False)
    P = Prog(nc)
    NP = n_ptiles * NT
    NSAMP = NS * 64
    NTMAX = max(NT, NSAMP) if NS else NT
    assert NSAMP <= NT or n_ptiles == 0
    NBMAX = NTMAX // 128

    def din(name, shape):
        return P.ext(nc.dram_tensor(name, list(shape), F32, kind="ExternalInput"), name)

    def dout(name, shape):
        return P.ext(nc.dram_tensor(name, list(shape), F32, kind="ExternalOutput"), name)

    xp = din("xp", [max(NP, 1), D])
    xq = din("xq", [max(n_state * NT, 1), D])
    flag = din("flag", [1])
    xs = din("xs", [max(NSAMP, 1), D])
    sg = din("sg", [max(NS, 1), 16, 128, 128])
    sc = din("sc", [max(NS, 1), 3, 4096])
    norm_mix = din("norm_mix", [2, D])
    norm_ffn = din("norm_ffn", [2, D])
    norm_final = din("norm_final", [D])
    sgu_w_in = din("sgu_w_in", [D, 4096])
    sgu_ln_g = din("sgu_ln_g", [DSGU])
    sgu_ln_b = din("sgu_ln_b", [DSGU])
    sgu_w_s = din("sgu_w_s", [8, 128, 128])
    sgu_b_s = din("sgu_b_s", [8, 128])
    sgu_w_out = din("sgu_w_out", [DSGU, D])
    gdn_w_in = din("gdn_w_in", [D, GIN])
    gdn_w_conv = din("gdn_w_conv", [4, 4096])
    gdn_a_log = din("gdn_a_log", [16])
    gdn_dt_bias = din("gdn_dt_bias", [16])
    gdn_w_onorm = din("gdn_w_onorm", [128])
    gdn_w_out = din("gdn_w_out", [DSGU, D])
    ffn_w_gate = din("ffn_w_gate", [2, D, DFF])
    ffn_w_up = din("ffn_w_up", [2, D, DFF])
    ffn_w_down = din("ffn_w_down", [2, DFF, D])

    yp = dout("yp", [max(NP, 1), D])
    ys = dout("ys", [max(NSAMP, 1), D])
    sgp = dout("sgp", [16, 128, 128])
    scp = dout("scp", [3, 4096])
    sgs = dout("sgs", [max(NS, 1), 16, 128, 128])
    scs = dout("scs", [max(NS, 1), 3, 4096])
    svs = dout("svs", [max(NSAMP, 1), DSGU])

    xT = P.sb("xT", [128, 8, NTMAX], F32)
    hT = P.sb("hT", [128, 8, NTMAX], BF16)
    A1 = P.sb("A1", [128, 22, NTMAX], BF16)
    kqT = P.sb("kqT", [128, 16, NTMAX], BF16)
    wsl = [P.sb("wsl%d" % i, [128, WSLOT], BF16) for i in range(4)]
    vtk = P.sb("vtk", [128, NBMAX, 2048], F32)
    vnb = P.sb("vnb", [128, 2048], BF16)
    vtok = P.sb("vtok", [128, NBMAX, 2048], BF16)
    ktok = P.sb("ktok", [128, NBMAX, 1024], BF16)
    grep = P.sb("grep", [128, 2048], F32)
    brep = P.sb("brep", [128, 2048], F32)
    wfin = P.sb("wfin", [128, 1024], F32)
    identF = P.sb("identF", [128, 128], F32)
    identB = P.sb("identB", [128, 128], BF16)
    onesF = P.sb("onesF", [128, 128], F32)
    onesB = P.sb("onesB", [128, 128], BF16)
    tril = P.sb("tril", [128, 128], F32)
    incl = P.sb("incl", [128, 128], F32)
    strict = P.sb("strict", [128, 128], F32)
    negm = P.sb("negm", [128, 128], BF16)
    selA = P.sb("selA", [128, 128], F32)
    selB = P.sb("selB", [128, 128], F32)
    blk1 = P.sb("blk1", [128, 128], F32)
    wsT1 = P.sb("wsT", [128, 8, 128], BF16)
    bsrow1 = P.sb("bsrow", [1, 1024], F32)
    wsT = [wsT1, wsT1]
    bsrow = [bsrow1, bsrow1]
    cw = P.sb("cw", [128, 4, 32], F32)
    nsc = P.sb("nsc", [128, 40], F32)
    alr = P.sb("alr", [128, 16], F32)
    dtr = P.sb("dtr", [128, 16], F32)
    hal = P.sb("hal", [128, 32, 4, 3], F32)
    stg = P.sb("stg", [128, 128], F32)
    flg = P.sb("flg", [128, 1], F32)
    xc = P.sb("xc", [128, 4, NTMAX + 12], F32)
    cacc = P.sb("cacc", [128, 4, NTMAX], F32)
    vTg = P.sb("vTg", [128, 4, NTMAX], BF16)
    Sf = [P.sb("Sf%d" % i, [128, 16, 128], F32) for i in range(2)]
    Sb = [P.sb("Sb%d" % i, [128, 16, 128], BF16) for i in range(2)]
    gts_ = [P.sb("gts%d" % i, [128, 32], F32) for i in range(NBMAX)]
    gsm_ = [P.sb("gsm%d" % i, [128, 12, 16], F32) for i in range(NBMAX)]
    gsm_unused = None
    Gtri = P.sb("Gtri", [128, 8, 128], F32)
    E1 = P.sb("E1", [128, 8, 128], F32)
    Es = P.sb("Es", [128, 8, 128], F32)
    PT = P.sb("PT", [128, 8, 128], BF16)
    Xa = [P.sb("Xa%d" % i, [128, 8, 128], BF16) for i in range(2)]
    Xt = [P.sb("Xt%d" % i, [128, 8, 128], BF16) for i in range(2)]
    Pm = [P.sb("Pm%d" % i, [128, 8, 128], BF16) for i in range(2)]
    P32 = P.sb("P32", [128, 8, 128], F32)
    kg = P.sb("kg", [128, 8, 128], BF16)
    kd = P.sb("kd", [128, 8, 128], BF16)
    nWT = P.sb("nWT", [128, 8, 128], BF16)
    qdT = P.sb("qdT", [128, 8, 128], BF16)
    dlt = P.sb("dlt", [128, 8, 128], BF16)
    onb = P.sb("onb", [128, 8, 128], BF16)
    oss = P.sb("oss", [128, 3, 8], F32)
    bst = P.sb("bst", [128, 4, 6], F32)
    bmv = P.sb("bmv", [128, 4], F32)

    pp = [P.ps("pp%d" % i, [128, 1024], F32) for i in range(4)]
    ppc = [0]

    def nps():
        t = pp[ppc[0] % 4]
        ppc[0] += 1
        return t

    class PH:
        def __init__(self, t, half):
            self.t, self.off, self.key = t, half * 512, (t, ("h", half))

        def ap(self, a, b, rows=slice(None)):
            return self.t.h[rows, self.off + a:self.off + b]

        def apb(self, a, b):
            return self.t.h[:, :].bitcast(BF16)[:, 2 * self.off + a:2 * self.off + b]

    p1c = [0]

    def nps1():
        i = p1c[0]
        p1c[0] += 1
        return PH(pp[(i // 2) % 4], i % 2)

    def pbf(t):
        return t.h[:, :].bitcast(BF16)

    wsc = [0]

    def nws():
        t = wsl[wsc[0] % 4]
        wsc[0] += 1
        return t

    dbg_done = set()

    def dbg(name, t, ap, shape, dtype=F32):
        if name not in DBG or name in dbg_done:
            return
        dbg_done.add(name)
        o = P.ext(nc.dram_tensor("dbg_" + name, list(shape), dtype, kind="ExternalOutput"), "dbg_" + name)
        P.dma(o.h.ap(), ap, reads=[t], writes=[o])

    def pool(fn, reads=(), writes=()):
        return P.op("pool", fn, reads, writes)

    def dve(fn, reads=(), writes=()):
        return P.op("dve", fn, reads, writes)

    def act(fn, reads=(), writes=()):
        return P.op("act", fn, reads, writes)

    def pe(fn, reads=(), writes=()):
        return P.op("pe", fn, reads, writes)

    pool(lambda e: e.memset(onesF[:], 1.0), writes=[onesF])
    pool(lambda e: e.memset(onesB[:], 1.0), writes=[onesB])
    pool(lambda e: e.memset(identF[:], 0.0), writes=[identF])
    pool(lambda e: e.affine_select(out=identF[:], in_=onesF[:], pattern=[[-1, 128]], base=0, channel_multiplier=1,
                                   compare_op=ALU.is_equal, fill=0.0), reads=[onesF], writes=[identF])
    pool(lambda e: e.tensor_copy(out=identB[:], in_=identF[:]), reads=[identF], writes=[identB])
    pool(lambda e: e.memset(tril[:], 0.0), writes=[tril])
    pool(lambda e: e.affine_select(out=tril[:], in_=onesF[:], pattern=[[1, 128]], base=0, channel_multiplier=-1,
                                   compare_op=ALU.is_ge, fill=0.0), reads=[onesF], writes=[tril])
    pool(lambda e: e.tensor_copy(out=incl[:], in_=tril[:]), reads=[tril], writes=[incl])
    pool(lambda e: e.memset(incl[0:64, 64:128], 0.0), writes=[incl])
    pool(lambda e: e.memset(strict[:], 0.0), writes=[strict])
    pool(lambda e: e.affine_select(out=strict[:], in_=onesF[:], pattern=[[1, 128]], base=0, channel_multiplier=-1,
                                   compare_op=ALU.is_gt, fill=0.0), reads=[onesF], writes=[strict])
    pool(lambda e: e.memset(strict[0:64, 64:128], 0.0), writes=[strict])
    pool(lambda e: e.tensor_scalar(out=negm[:], in0=incl[:], scalar1=-1.0, scalar2=-NEG, op0=ALU.add, op1=ALU.mult),
         reads=[incl], writes=[negm])
    pool(lambda e: e.memset(selA[:], 0.0), writes=[selA])
    pool(lambda e: e.memset(selA[0:64, :], 1.0), writes=[selA])
    pool(lambda e: e.memset(selB[:], 0.0), writes=[selB])
    pool(lambda e: e.memset(selB[64:128, :], 1.0), writes=[selB])
    pool(lambda e: e.memset(blk1[:], 0.0), writes=[blk1])
    pool(lambda e: e.memset(blk1[0:64, 0:64], 1.0), writes=[blk1])
    pool(lambda e: e.memset(blk1[64:128, 64:128], 1.0), writes=[blk1])
    pool(lambda e: e.memset(hal[:], 0.0), writes=[hal])

    def bcast_rows(ap1d, n):
        return ap1d.partition_broadcast(128)

    P.dma(grep[:], sgu_ln_g.h.ap().partition_broadcast(128), reads=[sgu_ln_g], writes=[grep])
    P.dma(brep[:], sgu_ln_b.h.ap().partition_broadcast(128), reads=[sgu_ln_b], writes=[brep])
    P.dma(wfin[:], norm_final.h.ap().partition_broadcast(128), reads=[norm_final], writes=[wfin])
    P.dma(alr[:], gdn_a_log.h.ap().partition_broadcast(128), reads=[gdn_a_log], writes=[alr])
    P.dma(dtr[:], gdn_dt_bias.h.ap().partition_broadcast(128), reads=[gdn_dt_bias], writes=[dtr])
    P.dma(flg[:], flag.h.ap().partition_broadcast(128), reads=[flag], writes=[flg])
    act(lambda e: e.activation(out=alr[:], in_=alr[:], func=AF.Exp), reads=[alr], writes=[alr])
    dve(lambda e: e.tensor_scalar(out=alr[:], in0=alr[:], scalar1=-1.0, scalar2=None, op0=ALU.mult), reads=[alr], writes=[alr])

    def small_T(dst_ap, dst_t, src_ap, src_t, R):
        P.dma(stg[0:R, :], src_ap, reads=[src_t], writes=[stg])
        pt = nps()
        pe(lambda e: e.transpose(pt.h[:, 0:R], stg[0:R, :], identF[0:R, 0:R]), reads=[stg, identF], writes=[pt])
        act(lambda e: e.copy(out=dst_ap, in_=pt.h[:, 0:R]), reads=[pt], writes=[dst_t])

    small_T(cw[:].rearrange("p i c -> p (i c)"), cw, gdn_w_conv.h.ap().rearrange("i (c p) -> (i c) p", p=128), gdn_w_conv, 128)
    small_T(nsc[:, 0:16], nsc, norm_mix.h.ap().rearrange("l (k p) -> (l k) p", p=128), norm_mix, 16)
    small_T(nsc[:, 16:32], nsc, norm_ffn.h.ap().rearrange("l (k p) -> (l k) p", p=128), norm_ffn, 16)
    small_T(nsc[:, 32:33], nsc, gdn_w_onorm.h.ap().rearrange("(o p) -> o p", o=1), gdn_w_onorm, 1)

    def setup_sgu_kind(kind):
        if kind == 0:
            P.dma(vtk[:, 0, 0:1024].rearrange("p (g s) -> p g s", g=8), sgu_w_s.h.ap().rearrange("g t s -> t g s"),
                  reads=[sgu_w_s], writes=[(vtk, 0)])
            P.dma(bsrow[0][:], sgu_b_s.h.ap().rearrange("(o g) t -> o (g t)", o=1), reads=[sgu_b_s], writes=[bsrow[0]])
        else:
            pool(lambda e: e.memset(vtk[:, 0, 0:1024], 0.0), writes=[(vtk, 0)])
            v3 = vtk[:, 0, 0:1024].rearrange("p (g s) -> p g s", g=8)
            P.dma(v3[0:64, :, 0:64], sgu_w_s.h.ap().rearrange("g t s -> t g s")[0:64, :, 0:64], reads=[sgu_w_s], writes=[(vtk, 0)])
            P.dma(v3[64:128, :, 64:128], sgu_w_s.h.ap().rearrange("g t s -> t g s")[0:64, :, 0:64], reads=[sgu_w_s], writes=[(vtk, 0)])
            b4 = bsrow[1][:].rearrange("o (g h t) -> o g h t", g=8, h=2)
            for hh in range(2):
                P.dma(b4[:, :, hh, :], sgu_b_s.h.ap().rearrange("(o g) t -> o g t", o=1)[:, :, 0:64], reads=[sgu_b_s], writes=[bsrow[1]])
        msk = tril if kind == 0 else incl
        for g4 in range(2):
            pt = nps()
            for j in range(4):
                g = g4 * 4 + j
                pe(lambda e, g=g, j=j, pt=pt: e.transpose(pt.h[:, j * 128:(j + 1) * 128], vtk[:, 0, g * 128:(g + 1) * 128], identF[:]),
                   reads=[(vtk, 0), identF], writes=[pt])
            dve(lambda e, g4=g4, pt=pt, kind=kind, msk=msk: e.tensor_tensor(
                out=wsT[kind][:, g4 * 4:(g4 + 1) * 4, :], in0=pt.h[:, 0:512].rearrange("p (g t) -> p g t", g=4),
                in1=msk[:].unsqueeze(1).to_broadcast([128, 4, 128]), op=ALU.mult), reads=[pt, msk], writes=[wsT[kind]])

    wdefs = [
        ("w_sgu_in", sgu_w_in, sgu_w_in.h.ap(), D, 4096, (0, 0)),
        ("w_sgu_out", sgu_w_out, sgu_w_out.h.ap(), DSGU, D, None),
        ("w_gate0", ffn_w_gate, ffn_w_gate.h.ap()[0], D, DFF, (16, 0)),
        ("w_up0", ffn_w_up, ffn_w_up.h.ap()[0], D, DFF, (16, 0)),
        ("w_down0", ffn_w_down, ffn_w_down.h.ap()[0], DFF, D, None),
        ("w_gdn_in", gdn_w_in, gdn_w_in.h.ap(), D, GIN, (8, 0)),
        ("w_gdn_out", gdn_w_out, gdn_w_out.h.ap(), DSGU, D, (32, 1)),
        ("w_gate1", ffn_w_gate, ffn_w_gate.h.ap()[1], D, DFF, (24, 0)),
        ("w_up1", ffn_w_up, ffn_w_up.h.ap()[1], D, DFF, (24, 0)),
        ("w_down1", ffn_w_down, ffn_w_down.h.ap()[1], DFF, D, None),
    ]
    WS = {}
    pro_ops = []
    cnt = 0
    engs = ["act", "dve"]
    stf = [(vtk, 0, lambda w: vtk[:, 0, 0:w]), (vtk, 1, lambda w: vtk[:, 1, 0:w]),
           (Sf[0], None, lambda w: Sf[0][:].rearrange("p h d -> p (h d)")[:, 0:w]), (Sf[1], None, lambda w: Sf[1][:].rearrange("p h d -> p (h d)")[:, 0:w])]
    stb = [(vtok, 0, lambda w: vtok[:, 0, 0:w]), (vtok, 1, lambda w: vtok[:, 1, 0:w]),
           (Sb[0], None, lambda w: Sb[0][:].rearrange("p h d -> p (h d)")[:, 0:w]), (Sb[1], None, lambda w: Sb[1][:].rearrange("p h d -> p (h d)")[:, 0:w])]
    for (name, srct, srcap, K, C, fold) in wdefs:
        scr = P.dram(name, [K, C], BF16)
        WS[name] = (scr, K // 128, C)
        for kc in range(K // 128):
            for c0 in range(0, C, 2048):
                cwid = min(2048, C - c0)
                i = cnt % 4
                cnt += 1
                ft, fk, fap = stf[i]
                bt, bk, bap = stb[i]
                fkey = (ft, fk) if fk is not None else ft
                bkey = (bt, bk) if bk is not None else bt
                P.dma(fap(cwid), srcap[kc * 128:(kc + 1) * 128, c0:c0 + cwid], reads=[srct], writes=[fkey])
                en = engs[cnt % 2]
                if fold is None:
                    if en == "act":
                        act(lambda e: e.copy(out=bap(cwid), in_=fap(cwid)), reads=[fkey], writes=[bkey])
                    else:
                        dve(lambda e: e.tensor_copy(out=bap(cwid), in_=fap(cwid)), reads=[fkey], writes=[bkey])
                else:
                    col = fold[0] + (kc if fold[1] == 0 else 0)
                    if en == "act":
                        act(lambda e: e.activation(out=bap(cwid), in_=fap(cwid), func=AF.Copy, scale=nsc[:, col:col + 1]), reads=[fkey, nsc], writes=[bkey])
                    else:
                        dve(lambda e: e.tensor_scalar(out=bap(cwid), in0=fap(cwid), scalar1=nsc[:, col:col + 1], scalar2=None, op0=ALU.mult),
                            reads=[fkey, nsc], writes=[bkey])
                o = P.dma(scr.h.ap()[kc * 128:(kc + 1) * 128, c0:c0 + cwid], bap(cwid), reads=[bkey], writes=[(scr, (kc, c0))], eng="act")
                pro_ops.append(o)
    fence = P.dma(stg[0:1, 0:4], onesF[0:1, 0:4], reads=[onesF], writes=[stg], extra=pro_ops)
    for name in WS:
        WS[name][0].last_w = {None: fence}
        WS[name][0].readers = {}

    def wpieces(name, c_lo, c_hi, KC):
        scr = WS[name][0]
        pcmax = (WSLOT // KC) // 128 * 128
        c0 = c_lo
        while c0 < c_hi:
            pc = min(pcmax, c_hi - c0)
            slot = nws()
            view = slot[:, 0:KC * pc].rearrange("p (k c) -> p k c", k=KC)
            P.dma(view, scr.h.ap().rearrange("(k p) c -> p k c", p=128)[:, :, c0:c0 + pc], reads=[scr], writes=[slot])
            yield slot, view, c0, pc
            c0 += pc

    def linear_fm(name, c_lo, c_hi, KC, rhs_t, rhs_ap, NTt, evac, oc_base=0, piped=False):
        gmax = min(4, max(1, 1024 // NTt))
        per_bank = max(1, 512 // NTt)
        pend = {"b1": None, "b2": None, "b2n": None}
        for slot, view, c0, pc in wpieces(name, c_lo, c_hi, KC):
            nch = pc // 128
            j0 = 0
            while j0 < nch:
                n = min(gmax, nch - j0)
                pt = nps()
                for j in range(n):
                    off = (j // per_bank) * 512 + (j % per_bank) * NTt
                    for k in range(KC):
                        pe(lambda e, pt=pt, off=off, view=view, k=k, jj=j0 + j: e.matmul(
                            pt.h[:, off:off + NTt], lhsT=view[:, k, jj * 128:(jj + 1) * 128], rhs=rhs_ap(k),
                            start=(k == 0), stop=(k == KC - 1)), reads=[slot, rhs_t], writes=[pt])
                if per_bank * NTt == 512 or n <= per_bank:
                    ap3 = pt.h[:, 0:n * NTt].rearrange("p (j t) -> p j t", j=n)
                else:
                    raise NotImplementedError
                called = [False]

                def mid():
                    called[0] = True
                    if pend["b1"] is not None:
                        pend["b1"]()
                    if pend["b2"] is not None:
                        pend["b2"]()
                if piped:
                    d_ = evac(oc_base + (c0 - c_lo) // 128 + j0, n, pt, ap3, mid)
                else:
                    d_ = evac(oc_base + (c0 - c_lo) // 128 + j0, n, pt, ap3)
                if not called[0]:
                    mid()
                pend["b2"] = pend["b2n"]
                pend["b1"], pend["b2n"] = d_ if d_ is not None else (None, None)
                j0 += n
        for k_ in ("b1", "b2"):
            if pend[k_] is not None:
                pend[k_]()
        if pend["b2n"] is not None:
            pend["b2n"]()

    def rmsnorm_to_hT(NTt):
        act(lambda e: e.activation(out=hT[:, :, 0:NTt], in_=xT[:, :, 0:NTt], func=AF.Square), reads=[xT], writes=[hT])
        pt = nps()
        for k in range(8):
            pe(lambda e, k=k, pt=pt: e.matmul(pt.h[:, 0:NTt], lhsT=onesB[:], rhs=hT[:, k, 0:NTt], start=(k == 0), stop=(k == 7)),
               reads=[onesB, hT], writes=[pt])
        act(lambda e, pt=pt: e.activation(out=cacc[:, 0, 0:NTt], in_=pt.h[:, 0:NTt], func=AF.Ln, scale=1.0 / D, bias=EPS), reads=[pt], writes=[cacc])
        act(lambda e: e.activation(out=cacc[:, 0, 0:NTt], in_=cacc[:, 0, 0:NTt], func=AF.Exp, scale=-0.5), reads=[cacc], writes=[cacc])
        dve(lambda e: e.tensor_tensor(out=hT[:, :, 0:NTt], in0=xT[:, :, 0:NTt], in1=cacc[:, 0, 0:NTt].unsqueeze(1).to_broadcast([128, 8, NTt]),
                                      op=ALU.mult), reads=[xT, cacc], writes=[hT])

    def resid_evac(NTt):
        def ev(oc0, n, pt, ap3):
            dve(lambda e: e.tensor_tensor(out=xT[:, oc0:oc0 + n, 0:NTt], in0=ap3, in1=xT[:, oc0:oc0 + n, 0:NTt], op=ALU.add),
                reads=[pt, xT], writes=[xT])
        return ev

    def ffn(layer, NTt):
        rmsnorm_to_hT(NTt)
        hrhs = lambda k: hT[:, k, 0:NTt]

        def ev_gate(oc0, n, pt, ap3):
            act(lambda e: e.activation(out=A1[:, oc0:oc0 + n, 0:NTt], in_=ap3, func=AF.Silu), reads=[pt], writes=[(A1, c) for c in range(oc0, oc0 + n)])

        def ev_up(oc0, n, pt, ap3):
            dve(lambda e: e.tensor_tensor(out=A1[:, oc0:oc0 + n, 0:NTt], in0=ap3, in1=A1[:, oc0:oc0 + n, 0:NTt], op=ALU.mult),
                reads=[pt] + [(A1, c) for c in range(oc0, oc0 + n)], writes=[(A1, c) for c in range(oc0, oc0 + n)])
        linear_fm("w_gate%d" % layer, 0, DFF, 8, hT, hrhs, NTt, ev_gate)
        linear_fm("w_up%d" % layer, 0, DFF, 8, hT, hrhs, NTt, ev_up)
        linear_fm("w_down%d" % layer, 0, D, 22, A1, lambda k: A1[:, k, 0:NTt], NTt, resid_evac(NTt))

    XW = NTMAX + 12
    convsets = [
        (xc.h[:, :, :], xc, cacc.h[:, :, :], cacc, vTg.h[:, :, :], vTg),
        (vtk.h[:, 0, 0:4 * XW].rearrange("p (j w) -> p j w", j=4), (vtk, 0),
         vnb.h[:, :].bitcast(F32)[:, 0:4 * NTMAX].rearrange("p (j t) -> p j t", j=4), vnb,
         vtk.h[:, 0, 4 * XW + 16:4 * XW + 16 + 2 * NTMAX].bitcast(BF16).rearrange("p (j t) -> p j t", j=4), (vtk, 0)),
    ]
    convc = [0]

    def run_tile(kind, tok0, NTt, first, last, seqs, mode="full", apply_flag=False, need_q_halo=False):
        full = mode == "full"
        nb = NTt // 128
        src = (xp if full else xq) if kind == 0 else xs
        dst = yp if kind == 0 else ys
        nseg = 1 if kind == 0 else NTt // 64
        L = NTt // nseg
        if apply_flag:
            dve(lambda e: e.tensor_scalar(out=Sf[0][:], in0=Sf[0][:], scalar1=flg[:, 0:1], scalar2=None, op0=ALU.mult), reads=[Sf[0], flg], writes=[Sf[0]])
            act(lambda e: e.copy(out=Sb[0][:], in_=Sf[0][:]), reads=[Sf[0]], writes=[Sb[0]])
            dve(lambda e: e.tensor_scalar(out=hal[:], in0=hal[:], scalar1=flg[:, 0:1], scalar2=None, op0=ALU.mult), reads=[hal, flg], writes=[hal])
        for b in range(nb):
            xb = b % 2
            P.dma(vtk[:, xb, 0:1024], src.h.ap()[tok0 + b * 128: tok0 + (b + 1) * 128, :], reads=[src], writes=[(vtk, xb)])
            for hf in range(2):
                pt = nps()
                for j in range(4):
                    k = hf * 4 + j
                    pe(lambda e, pt=pt, j=j, k=k, xb=xb: e.transpose(pt.h[:, j * 128:(j + 1) * 128], vtk[:, xb, k * 128:(k + 1) * 128], identF[:]),
                       reads=[(vtk, xb), identF], writes=[pt])
                act(lambda e, pt=pt, hf=hf, b=b: e.copy(out=xT[:, hf * 4:(hf + 1) * 4, b * 128:(b + 1) * 128],
                                                        in_=pt.h[:, 0:512].rearrange("p (j t) -> p j t", j=4)), reads=[pt], writes=[xT])
        rmsnorm_to_hT(NTt)
        hrhs = lambda k: hT[:, k, 0:NTt]

        def ev_u(oc0, n, pt, ap3):
            act(lambda e: e.activation(out=A1[:, oc0:oc0 + n, 0:NTt], in_=ap3, func=AF.Gelu), reads=[pt], writes=[(A1, c) for c in range(oc0, oc0 + n)])
        for slot, view, c0, pc in wpieces("w_sgu_in", 2048, 4096, 8):
            for b in range(nb):
                for q in range(pc // 512):
                    pt = nps()
                    for k in range(8):
                        pe(lambda e, pt=pt, k=k, b=b, q=q, view=view: e.matmul(pt.h[:, 0:512], lhsT=hT[:, k, b * 128:(b + 1) * 128],
                                                                               rhs=view[:, k, q * 512:(q + 1) * 512], start=(k == 0), stop=(k == 7)),
                           reads=[slot, hT], writes=[pt])
                    cc = c0 - 2048 + q * 512
                    act(lambda e, pt=pt, b=b, cc=cc: e.activation(out=vtk[:, b, cc:cc + 512], in_=pt.h[:, 0:512], func=AF.Gelu),
                        reads=[pt], writes=[(vtk, b)])
        for b in range(nb):
            for q in range(4):
                dve(lambda e, b=b, q=q: e.bn_stats(out=bst[:, q, :], in_=vtk[:, b, q * 512:(q + 1) * 512]), reads=[(vtk, b)], writes=[bst])
            dve(lambda e: e.bn_aggr(out=bmv[:, 0:2], in_=bst[:].rearrange("p q s -> p (q s)")), reads=[bst], writes=[bmv])
            act(lambda e: e.activation(out=bmv[:, 2:3], in_=bmv[:, 1:2], func=AF.Ln, bias=1e-5), reads=[bmv], writes=[bmv])
            act(lambda e: e.activation(out=bmv[:, 2:3], in_=bmv[:, 2:3], func=AF.Exp, scale=-0.5), reads=[bmv], writes=[bmv])
            dve(lambda e, b=b: e.scalar_tensor_tensor(out=vtk[:, b, :], in0=vtk[:, b, :], scalar=bmv[:, 0:1], in1=grep[:],
                                                      op0=ALU.subtract, op1=ALU.mult), reads=[(vtk, b), bmv, grep], writes=[(vtk, b)])
            dve(lambda e, b=b: e.scalar_tensor_tensor(out=vtk[:, b, :], in0=vtk[:, b, :], scalar=bmv[:, 2:3], in1=brep[:],
                                                      op0=ALU.mult, op1=ALU.add), reads=[(vtk, b), bmv, brep], writes=[(vtk, b)])
            if kind == 1:
                P.dma(svs.h.ap()[tok0 + b * 128: tok0 + (b + 1) * 128, :], vtk[:, b, :], reads=[(vtk, b)], writes=[svs])
        linear_fm("w_sgu_in", 0, 2048, 8, hT, hrhs, NTt, ev_u)
        for b in range(nb):
            act(lambda e, b=b: e.copy(out=vnb[:], in_=vtk[:, b, :]), reads=[(vtk, b)], writes=[vnb])
            for c4 in range(4):
                pt = nps()
                brow = bsrow[kind][0:1, c4 * 256:(c4 + 1) * 256].rearrange("o (g t) -> o g t", g=2).unsqueeze(2).to_broadcast([1, 2, 2, 128])
                pe(lambda e, pt=pt, brow=brow: e.matmul(pt.h[:, 0:512], lhsT=onesF[0:1, :], rhs=brow, start=True, stop=False),
                   reads=[onesF, bsrow[kind]], writes=[pt])
                for j in range(4):
                    ci = c4 * 4 + j
                    g = ci // 2
                    pe(lambda e, pt=pt, j=j, ci=ci, g=g: e.matmul(pt.h[:, j * 128:(j + 1) * 128], lhsT=vnb[:, ci * 128:(ci + 1) * 128],
                                                                 rhs=wsT[kind][:, g, :], start=False, stop=(j == 3)), reads=[vnb, wsT[kind]], writes=[pt])
                dve(lambda e, pt=pt, c4=c4, b=b: e.tensor_tensor(out=A1[:, c4 * 4:(c4 + 1) * 4, b * 128:(b + 1) * 128],
                                                                 in0=pt.h[:, 0:512].rearrange("p (j t) -> p j t", j=4),
                                                                 in1=A1[:, c4 * 4:(c4 + 1) * 4, b * 128:(b + 1) * 128], op=ALU.mult),
                    reads=[pt] + [(A1, c) for c in range(c4 * 4, c4 * 4 + 4)], writes=[(A1, c) for c in range(c4 * 4, c4 * 4 + 4)])
        linear_fm("w_sgu_out", 0, D, 16, A1, lambda k: A1[:, k, 0:NTt], NTt, resid_evac(NTt))
        ffn(0, NTt)
        rmsnorm_to_hT(NTt)
        if kind == 1:
            for c4 in range(8):
                P.dma(vtk[0:nseg * 3, 1, 0:512], sc.h.ap().rearrange("s i c -> (s i) c")[seqs[0] * 3:(seqs[0] + nseg) * 3, c4 * 512:(c4 + 1) * 512],
                      reads=[sc], writes=[(vtk, 1)])
                pt = nps()
                for j in range(4):
                    pe(lambda e, pt=pt, j=j: e.transpose(pt.h[:, j * 16:j * 16 + nseg * 3], vtk[0:nseg * 3, 1, j * 128:(j + 1) * 128],
                                                       identF[0:nseg * 3, 0:nseg * 3]), reads=[(vtk, 1), identF], writes=[pt])
                act(lambda e, pt=pt, c4=c4: e.copy(out=hal[:, c4 * 4:(c4 + 1) * 4, 0:nseg, :],
                                                   in_=pt.h[:, 0:64].rearrange("p (j x) -> p j x", j=4)[:, :, 0:nseg * 3].rearrange("p j (s i) -> p j s i", i=3)),
                    reads=[pt], writes=[hal])
        W = L + 3

        def ev_qkv(oc0, n, pt, ap3, mid):
            XC, XCK, CA, CAK, VT, VTK = convsets[convc[0] % 2]
            convc[0] += 1
            xv = XC[:, 0:n, 0:nseg * W].rearrange("p j (s w) -> p j s w", s=nseg)
            act(lambda e: e.copy(out=xv[:, :, :, 3:3 + L], in_=ap3.rearrange("p j (s l) -> p j s l", s=nseg)), reads=[pt], writes=[XCK])
            pool(lambda e: e.tensor_copy(out=xv[:, :, :, 0:3], in_=hal[:, oc0:oc0 + n, 0:nseg, :]), reads=[hal], writes=[XCK])
            pool(lambda e: e.tensor_copy(out=hal[:, oc0:oc0 + n, 0:nseg, :], in_=xv[:, :, :, L:L + 3]), reads=[XCK], writes=[hal])
            mid()
            avs = [CA[:, j, 0:NTt].rearrange("p (s l) -> p s l", s=nseg) for j in range(n)]
            CK = [(CAK if isinstance(CAK, tuple) else (CAK, None)) for j in range(n)]
            for j in range(n):
                c = oc0 + j
                dve(lambda e, j=j, c=c: e.tensor_scalar(out=avs[j], in0=xv[:, j, :, 0:L], scalar1=cw[:, 0, c:c + 1], scalar2=None, op0=ALU.mult),
                    reads=[XCK, cw], writes=[(CK[j][0], ("c", j))])
            for i in range(1, 4):
                for j in range(n):
                    c = oc0 + j
                    dve(lambda e, j=j, c=c, i=i: e.scalar_tensor_tensor(out=avs[j], in0=xv[:, j, :, i:i + L], scalar=cw[:, i, c:c + 1], in1=avs[j],
                                                                      op0=ALU.mult, op1=ALU.add), reads=[XCK, cw, (CK[j][0], ("c", j))], writes=[(CK[j][0], ("c", j))])
            KQK = [(kqT, c) for c in range(oc0, oc0 + n)]
            if oc0 < 16:
                act(lambda e: e.activation(out=kqT[:, oc0:oc0 + n, 0:NTt], in_=CA[:, 0:n, 0:NTt], func=AF.Silu), reads=[CAK], writes=KQK)
                act(lambda e: e.activation(out=VT[:, 0:n, 0:NTt], in_=kqT[:, oc0:oc0 + n, 0:NTt], func=AF.Square), reads=KQK, writes=[VTK])

                def partB1():
                    p2 = nps()
                    per_bank = max(1, 512 // NTt)
                    for j in range(n):
                        off = (j // per_bank) * 512 + (j % per_bank) * NTt
                        pe(lambda e, j=j, off=off, p2=p2: e.matmul(p2.h[:, off:off + NTt], lhsT=onesB[:], rhs=VT[:, j, 0:NTt], start=True, stop=True),
                           reads=[onesB, VTK], writes=[p2])
                    a3 = p2.h[:, 0:n * NTt].rearrange("p (j t) -> p j t", j=n)
                    act(lambda e: e.activation(out=CA[:, 0:n, 0:NTt], in_=a3, func=AF.Ln, bias=EPS), reads=[p2], writes=[CAK])
                    qb = -0.5 * float(np.log(128.0)) if oc0 < 8 else 0.0
                    act(lambda e: e.activation(out=CA[:, 0:n, 0:NTt], in_=CA[:, 0:n, 0:NTt], func=AF.Exp, scale=-0.5, bias=qb), reads=[CAK], writes=[CAK])

                def partB2():
                    dve(lambda e: e.tensor_tensor(out=kqT[:, oc0:oc0 + n, 0:NTt], in0=kqT[:, oc0:oc0 + n, 0:NTt], in1=CA[:, 0:n, 0:NTt], op=ALU.mult),
                        reads=KQK + [CAK], writes=KQK)
                    if oc0 >= 8:
                        for b in range(nb):
                            p3 = nps()
                            for j in range(n):
                                pe(lambda e, j=j, b=b, p3=p3: e.transpose(pbf(p3)[:, j * 128:(j + 1) * 128], kqT[:, oc0 + j, b * 128:(b + 1) * 128], identB[:]),
                                   reads=KQK + [identB], writes=[p3])
                            act(lambda e, b=b, p3=p3: e.copy(out=ktok[:, b, (oc0 - 8) * 128:(oc0 - 8 + n) * 128], in_=pbf(p3)[:, 0:n * 128]),
                                reads=[p3], writes=[(ktok, b)])
                return (partB1, partB2)
            else:
                act(lambda e: e.activation(out=VT[:, 0:n, 0:NTt], in_=CA[:, 0:n, 0:NTt], func=AF.Silu), reads=[CAK], writes=[VTK])

                def partB2():
                    for b in range(nb):
                        p3 = nps()
                        for j in range(n):
                            pe(lambda e, j=j, b=b, p3=p3: e.transpose(pbf(p3)[:, j * 128:(j + 1) * 128], VT[:, j, b * 128:(b + 1) * 128], identB[:]),
                               reads=[VTK, identB], writes=[p3])
                        act(lambda e, b=b, p3=p3: e.copy(out=vtok[:, b, (oc0 - 16) * 128:(oc0 - 16 + n) * 128], in_=pbf(p3)[:, 0:n * 128]),
                            reads=[p3], writes=[(vtok, b)])
                return (None, partB2)
        if full or need_q_halo:
            linear_fm("w_gdn_in", 0, 4096, 8, hT, hrhs, NTt, ev_qkv, piped=True)
        else:
            linear_fm("w_gdn_in", 1024, 4096, 8, hT, hrhs, NTt, ev_qkv, oc_base=8, piped=True)
        if (kind == 1 or last) and full:
            odst = scs if kind == 1 else scp
            orow = odst.h.ap().rearrange("s i c -> (s i) c") if kind == 1 else odst.h.ap()
            r0 = seqs[0] * 3 if kind == 1 else 0
            for c4 in range(8):
                pt = nps()
                for j in range(4):
                    pe(lambda e, pt=pt, j=j, c4=c4: e.transpose(pt.h[0:nseg * 3, j * 128:(j + 1) * 128],
                                                              hal[:, c4 * 4 + j, 0:nseg, :].rearrange("p s i -> p (s i)"), identF[:]),
                       reads=[hal, identF], writes=[pt])
                act(lambda e, pt=pt: e.copy(out=vtk[0:nseg * 3, 1, 0:512], in_=pt.h[0:nseg * 3, 0:512]), reads=[pt], writes=[(vtk, 1)])
                P.dma(orow[r0:r0 + nseg * 3, c4 * 512:(c4 + 1) * 512], vtk[0:nseg * 3, 1, 0:512], reads=[(vtk, 1)], writes=[odst])

        gslot, gview, _, _ = next(wpieces("w_gdn_in", 6144, 6176, 8))
        if kind == 0 and first and not apply_flag:
            pool(lambda e: e.memset(Sf[0][:], 0.0), writes=[Sf[0]])
            pool(lambda e: e.memset(Sb[0][:], 0.0), writes=[Sb[0]])
        for b in range(nb):
            pg = nps()
            for k in range(8):
                pe(lambda e, k=k, b=b, pg=pg: e.matmul(pg.h[:, 0:32], lhsT=hT[:, k, b * 128:(b + 1) * 128], rhs=gview[:, k, 0:32],
                                                       start=(k == 0), stop=(k == 7)), reads=[hT, gslot], writes=[pg])
            act(lambda e, pg=pg: e.copy(out=gts_[b][:], in_=pg.h[:, 0:32]), reads=[pg], writes=[gts_[b]])
            G = lambda i, b=b: gsm_[b][:, i, :]
            act(lambda e: e.activation(out=G(0), in_=gts_[b][:, 0:16], func=AF.Exp, scale=-1.0), reads=[gts_[b]], writes=[gsm_[b]])
            dve(lambda e: e.tensor_scalar(out=G(0), in0=G(0), scalar1=1.0, scalar2=None, op0=ALU.add), reads=[gsm_[b]], writes=[gsm_[b]])
            dve(lambda e: e.reciprocal(out=G(1), in_=G(0)), reads=[gsm_[b]], writes=[gsm_[b]])
            dve(lambda e: e.tensor_scalar(out=G(2), in0=G(1), scalar1=-1.0, scalar2=None, op0=ALU.mult), reads=[gsm_[b]], writes=[gsm_[b]])
            dve(lambda e: e.tensor_tensor(out=G(9), in0=gts_[b][:, 16:32], in1=dtr[:], op=ALU.add), reads=[gts_[b], dtr], writes=[gsm_[b]])
            act(lambda e: e.activation(out=G(9), in_=G(9), func=AF.Exp), reads=[gsm_[b]], writes=[gsm_[b]])
            act(lambda e: e.activation(out=G(9), in_=G(9), func=AF.Ln, bias=1.0), reads=[gsm_[b]], writes=[gsm_[b]])
            dve(lambda e: e.tensor_tensor(out=G(3), in0=G(9), in1=alr[:], op=ALU.mult), reads=[gsm_[b], alr], writes=[gsm_[b]])
            pc_ = nps()
            pe(lambda e, pc_=pc_: e.matmul(pc_.h[:, 0:16], lhsT=incl[:], rhs=G(3), start=True, stop=True), reads=[incl, gsm_[b]], writes=[pc_])
            pe(lambda e, pc_=pc_: e.matmul(pc_.h[:, 16:32], lhsT=blk1[:], rhs=G(3), start=True, stop=True), reads=[blk1, gsm_[b]], writes=[pc_])
            pe(lambda e, pc_=pc_: e.matmul(pc_.h[:, 32:48], lhsT=selA[:], rhs=G(3), start=True, stop=True), reads=[selA, gsm_[b]], writes=[pc_])
            pe(lambda e, pc_=pc_: e.matmul(pc_.h[:, 48:64], lhsT=selB[:], rhs=G(3), start=True, stop=True), reads=[selB, gsm_[b]], writes=[pc_])
            act(lambda e, pc_=pc_: e.copy(out=G(4), in_=pc_.h[:, 0:16]), reads=[pc_], writes=[gsm_[b]])
            act(lambda e, pc_=pc_: e.activation(out=G(5), in_=pc_.h[:, 0:16], func=AF.Exp), reads=[pc_], writes=[gsm_[b]])
            dve(lambda e, pc_=pc_: e.tensor_tensor(out=G(6), in0=pc_.h[:, 16:32], in1=G(4), op=ALU.subtract), reads=[pc_, gsm_[b]], writes=[gsm_[b]])
            act(lambda e: e.activation(out=G(6), in_=G(6), func=AF.Exp), reads=[gsm_[b]], writes=[gsm_[b]])
            act(lambda e, pc_=pc_: e.activation(out=G(7), in_=pc_.h[:, 32:48], func=AF.Exp), reads=[pc_], writes=[gsm_[b]])
            act(lambda e, pc_=pc_: e.activation(out=G(8), in_=pc_.h[:, 48:64], func=AF.Exp), reads=[pc_], writes=[gsm_[b]])
            dbg("gts", gts_[b], gts_[b][:], [128, 32])
            dbg("gsm", gsm_[b], gsm_[b][:], [128, 12, 16])
        def ev_z(oc0, n, pt, ap3):
            act(lambda e: e.activation(out=A1[:, oc0:oc0 + n, 0:NTt], in_=ap3, func=AF.Silu), reads=[pt], writes=[(A1, c) for c in range(oc0, oc0 + n)])
        if full:
            linear_fm("w_gdn_in", 4096, 6144, 8, hT, hrhs, NTt, ev_z)
        for b in range(nb):
            gsm = gsm_[b]
            G = lambda i, b=b: gsm_[b][:, i, :]
            if kind == 1:
                for ch in range(2):
                    sq_ = seqs[0] + b * 2 + ch
                    P.dma(Sf[ch][:], sg.h.ap()[sq_].rearrange("h k v -> k h v"), reads=[sg], writes=[Sf[ch]])
                    act(lambda e, ch=ch: e.copy(out=Sb[ch][:], in_=Sf[ch][:]), reads=[Sf[ch]], writes=[Sb[ch]])
            bs_ = slice(b * 128, (b + 1) * 128)
            NG = NGROUPS
            HG = 8 // NG
            GW = HG * 128
            grp = list(range(NG))
            for hf in range(2):
                h0 = hf * 8
                hs = [slice(g * HG, (g + 1) * HG) for g in grp]
                hgl = [slice(h0 + g * HG, h0 + (g + 1) * HG) for g in grp]
                K_ = lambda t, g: (t, ("g", g))
                X032, XT32, X132 = Gtri, E1, Es
                fl = lambda t, g: t[:, hs[g], :].rearrange("p h t -> p (h t)")
                prt_, pgr_, pkq_ = {}, {}, {}
                for g in grp:
                    dve(lambda e, g=g: e.tensor_tensor(out=Gtri[:, hs[g], :], in0=G(3)[:, hgl[g]].unsqueeze(2).to_broadcast([128, HG, 128]),
                                                       in1=incl[:].unsqueeze(1).to_broadcast([128, HG, 128]), op=ALU.mult),
                        reads=[gsm, incl], writes=[K_(Gtri, g)])
                for g in grp:
                    prt = nps1()
                    prt_[g] = prt
                    pe(lambda e, g=g, prt=prt: e.matmul(prt.ap(0, GW), lhsT=onesF[:], rhs=fl(Gtri, g), start=True, stop=False),
                       reads=[onesF, K_(Gtri, g)], writes=[prt.key])
                    pe(lambda e, g=g, prt=prt: e.matmul(prt.ap(0, GW), lhsT=identB[:], rhs=negm[:].unsqueeze(1).to_broadcast([128, HG, 128]),
                                                        start=False, stop=True), reads=[identB, negm], writes=[prt.key])
                    if full:
                        pgr = nps1()
                        pgr_[g] = pgr
                        pe(lambda e, g=g, pgr=pgr: e.matmul(pgr.ap(0, GW), lhsT=onesF[:], rhs=fl(Gtri, g), start=True, stop=True),
                           reads=[onesF, K_(Gtri, g)], writes=[pgr.key])
                v3 = lambda ph: ph.ap(0, GW).rearrange("p (h t) -> p h t", h=HG)
                for g in grp:
                    dve(lambda e, g=g: e.tensor_tensor(out=E1[:, hs[g], :], in0=v3(prt_[g]), in1=G(4)[:, hgl[g]].unsqueeze(2).to_broadcast([128, HG, 128]),
                                                       op=ALU.subtract), reads=[prt_[g].key, gsm], writes=[K_(E1, g)])
                for g in grp:
                    act(lambda e, g=g: e.activation(out=E1[:, hs[g], :], in_=E1[:, hs[g], :], func=AF.Exp), reads=[K_(E1, g)], writes=[K_(E1, g)])
                for g in grp:
                    pool(lambda e, g=g: e.tensor_tensor(out=Es[:, hs[g], :], in0=E1[:, hs[g], :], in1=strict[:].unsqueeze(1).to_broadcast([128, HG, 128]), op=ALU.mult),
                         reads=[K_(E1, g), strict], writes=[K_(Es, g)])
                for g in grp:
                    pkq = nps1()
                    pkq_[g] = pkq
                    for j in range(HG // 2):
                        hk = hf * 4 + g * (HG // 2) + j
                        if full:
                            pe(lambda e, j=j, hk=hk, pkq=pkq: e.matmul(pkq.ap(j * 256, (j + 1) * 256), lhsT=kqT[:, 8 + hk, bs_], rhs=kqT[:, hk:hk + 9:8, bs_],
                                                                       start=True, stop=True), reads=[kqT], writes=[pkq.key])
                        else:
                            pe(lambda e, j=j, hk=hk, pkq=pkq: e.matmul(pkq.ap(j * 256 + 128, (j + 1) * 256), lhsT=kqT[:, 8 + hk, bs_], rhs=kqT[:, 8 + hk, bs_],
                                                                       start=True, stop=True), reads=[(kqT, 8 + hk)], writes=[pkq.key])
                kq4 = lambda ph: ph.ap(0, GW).rearrange("p (j c t) -> p j c t", j=HG // 2, c=2)
                if full:
                    for g in grp:
                        dve(lambda e, g=g: e.tensor_tensor(out=PT[:, hs[g], :].rearrange("p (j r) t -> p j r t", r=2),
                                                           in0=kq4(pkq_[g])[:, :, 0:1, :].to_broadcast([128, HG // 2, 2, 128]),
                                                           in1=E1[:, hs[g], :].rearrange("p (j r) t -> p j r t", r=2), op=ALU.mult),
                            reads=[pkq_[g].key, K_(E1, g)], writes=[K_(PT, g)])
                for g in grp:
                    for hh in range(HG):
                        hw = g * HG + hh
                        dve(lambda e, g=g, hh=hh, hw=hw: e.scalar_tensor_tensor(out=X032[:, hw, :], in0=kq4(pkq_[g])[:, hh // 2, 1, :], scalar=G(2)[:, h0 + hw:h0 + hw + 1],
                                                                               in1=Es[:, hw, :], op0=ALU.mult, op1=ALU.mult),
                            reads=[pkq_[g].key, gsm, K_(Es, g)], writes=[K_(X032, g)])
                if full:
                    for g in grp:
                        act(lambda e, g=g: e.activation(out=Es[:, hs[g], :], in_=v3(pgr_[g]), func=AF.Exp), reads=[pgr_[g].key], writes=[K_(Es, g)])
                    for g in grp:
                        q4 = kqT[:, hf * 4 + g * (HG // 2):hf * 4 + (g + 1) * (HG // 2), bs_].unsqueeze(2).to_broadcast([128, HG // 2, 2, 128])
                        dve(lambda e, g=g, q4=q4: e.tensor_tensor(out=qdT[:, hs[g], :].rearrange("p (j r) t -> p j r t", r=2), in0=q4,
                                                                  in1=Es[:, hs[g], :].rearrange("p (j r) t -> p j r t", r=2), op=ALU.mult),
                            reads=[kqT, K_(Es, g)], writes=[K_(qdT, g)])
                for g in grp:
                    pxt = nps1()
                    for hh in range(HG):
                        pe(lambda e, g=g, hh=hh, pxt=pxt: e.transpose(pxt.ap(hh * 128, (hh + 1) * 128), X032[:, g * HG + hh, :], identF[:]),
                           reads=[K_(X032, g), identF], writes=[pxt.key])
                    act(lambda e, g=g, pxt=pxt: e.copy(out=fl(XT32, g), in_=pxt.ap(0, GW)), reads=[pxt.key], writes=[K_(XT32, g)])
                for g in grp:
                    dve(lambda e, g=g: e.tensor_tensor(out=P32[:, hs[g], :], in0=X032[:, hs[g], :], in1=identF[:].unsqueeze(1).to_broadcast([128, HG, 128]), op=ALU.add),
                        reads=[K_(X032, g), identF], writes=[K_(P32, g)])
                for g in grp:
                    px = nps1()
                    for hh in range(HG):
                        hw = g * HG + hh
                        pe(lambda e, hh=hh, hw=hw, px=px: e.matmul(px.ap(hh * 128, (hh + 1) * 128), lhsT=XT32[:, hw, :], rhs=X032[:, hw, :], start=True, stop=True),
                           reads=[K_(XT32, g), K_(X032, g)], writes=[px.key])
                    act(lambda e, g=g, px=px: e.copy(out=fl(X132, g), in_=px.ap(0, GW)), reads=[px.key], writes=[K_(X132, g)])
                    act(lambda e, g=g: e.copy(out=Xa[1][:, hs[g], :], in_=X132[:, hs[g], :]), reads=[K_(X132, g)], writes=[K_(Xa[1], g)])
                for g in grp:
                    pxT = nps1()
                    for hh in range(HG):
                        hw = g * HG + hh
                        pe(lambda e, hh=hh, hw=hw, pxT=pxT: e.matmul(pxT.ap(hh * 128, (hh + 1) * 128), lhsT=X032[:, hw, :], rhs=XT32[:, hw, :], start=True, stop=True),
                           reads=[K_(XT32, g), K_(X032, g)], writes=[pxT.key])
                    dve(lambda e, g=g, pxT=pxT: e.tensor_copy(out=fl(Xt[1], g), in_=pxT.ap(0, GW)), reads=[pxT.key], writes=[K_(Xt[1], g)])
                for g in grp:
                    ppm = nps1()
                    for hh in range(HG):
                        hw = g * HG + hh
                        pe(lambda e, hh=hh, hw=hw, ppm=ppm: e.matmul(ppm.ap(hh * 128, (hh + 1) * 128), lhsT=XT32[:, hw, :], rhs=X132[:, hw, :], start=True, stop=True),
                           reads=[K_(XT32, g), K_(X132, g)], writes=[ppm.key])
                    dve(lambda e, g=g: e.tensor_tensor(out=P32[:, hs[g], :], in0=P32[:, hs[g], :], in1=X132[:, hs[g], :], op=ALU.add),
                        reads=[K_(P32, g), K_(X132, g)], writes=[K_(P32, g)])
                    dve(lambda e, g=g, ppm=ppm: e.tensor_tensor(out=fl(P32, g), in0=ppm.ap(0, GW), in1=fl(P32, g), op=ALU.add),
                        reads=[ppm.key, K_(P32, g)], writes=[K_(P32, g)])
                    act(lambda e, g=g: e.copy(out=Pm[1][:, hs[g], :], in_=P32[:, hs[g], :]), reads=[K_(P32, g)], writes=[K_(Pm[1], g)])
                cur = 1
                for lev in range(2, 6):
                    nxt = 1 - cur
                    if lev < 5:
                        for g in grp:
                            px = nps1()
                            for hh in range(HG):
                                hw = g * HG + hh
                                pe(lambda e, hh=hh, hw=hw, px=px, cur=cur: e.matmul(px.ap(hh * 128, (hh + 1) * 128), lhsT=Xt[cur][:, hw, :], rhs=Xa[cur][:, hw, :],
                                                                                    start=True, stop=True), reads=[K_(Xt[cur], g), K_(Xa[cur], g)], writes=[px.key])
                            act(lambda e, g=g, px=px, nxt=nxt: e.copy(out=fl(Xa[nxt], g), in_=px.ap(0, GW)), reads=[px.key], writes=[K_(Xa[nxt], g)])
                    pxTs = {}
                    for g in grp:
                        pxT = nps1()
                        for hh in range(HG):
                            hw = g * HG + hh
                            pe(lambda e, hh=hh, hw=hw, pxT=pxT, cur=cur: e.matmul(pxT.ap(hh * 128, (hh + 1) * 128), lhsT=Xa[cur][:, hw, :], rhs=Xt[cur][:, hw, :],
                                                                                  start=True, stop=True), reads=[K_(Xt[cur], g), K_(Xa[cur], g)], writes=[pxT.key])
                        dve(lambda e, g=g, pxT=pxT, nxt=nxt: e.tensor_copy(out=fl(Xt[nxt], g), in_=pxT.ap(0, GW)), reads=[pxT.key], writes=[K_(Xt[nxt], g)])
                    for g in grp:
                        ppm = nps1()
                        for hh in range(HG):
                            hw = g * HG + hh
                            pe(lambda e, hh=hh, hw=hw, ppm=ppm, nxt=nxt, cur=cur: e.matmul(ppm.ap(hh * 128, (hh + 1) * 128), lhsT=Xt[nxt][:, hw, :], rhs=Pm[cur][:, hw, :],
                                                                                           start=True, stop=True), reads=[K_(Xt[nxt], g), K_(Pm[cur], g)], writes=[ppm.key])
                        dve(lambda e, g=g, ppm=ppm: e.tensor_tensor(out=fl(P32, g), in0=ppm.ap(0, GW), in1=fl(P32, g), op=ALU.add),
                            reads=[ppm.key, K_(P32, g)], writes=[K_(P32, g)])
                        act(lambda e, g=g, nxt=nxt: e.copy(out=Pm[nxt][:, hs[g], :], in_=P32[:, hs[g], :]), reads=[K_(P32, g)], writes=[K_(Pm[nxt], g)])
                    cur = nxt
                Tt = Pm[cur]
                dbg("Tt", Tt, Tt[:], [128, 8, 128], BF16)
                for g in grp:
                    kc0 = hf * 512 + g * (HG // 2) * 128
                    k4 = ktok[:, b, kc0:kc0 + (HG // 2) * 128].rearrange("p (j d) -> p j d", j=HG // 2).unsqueeze(2).to_broadcast([128, HG // 2, 2, 128])
                    for dst_, gi in ((kg, 5), (kd, 6)):
                        pool(lambda e, g=g, k4=k4, dst_=dst_, gi=gi: e.tensor_tensor(
                            out=dst_[:, hs[g], :].rearrange("p (j r) d -> p j r d", r=2), in0=k4,
                            in1=G(gi)[:, hgl[g]].rearrange("p (j r) -> p j r", r=2).unsqueeze(3).to_broadcast([128, HG // 2, 2, 128]), op=ALU.mult),
                            reads=[(ktok, b), gsm], writes=[K_(dst_, g)])
                for g in grp:
                    pw = nps1()
                    for hh in range(HG):
                        hw = g * HG + hh
                        pe(lambda e, hh=hh, hw=hw, pw=pw, Tt=Tt: e.matmul(pw.ap(hh * 128, (hh + 1) * 128), lhsT=kg[:, hw, :], rhs=Tt[:, hw, :], start=True, stop=True),
                           reads=[K_(kg, g), K_(Tt, g)], writes=[pw.key])
                    act(lambda e, g=g, pw=pw: e.mul(out=fl(nWT, g), in_=pw.ap(0, GW), mul=-1.0), reads=[pw.key], writes=[K_(nWT, g)])
                for ch in range(2):
                    si = ch if kind == 1 else 0
                    r_ = slice(ch * 64, ch * 64 + 64)
                    SK = lambda t, g: (t, ("s", hf, g))
                    for g in grp:
                        pd = nps1()
                        for hh in range(HG):
                            hw = g * HG + hh
                            hg = h0 + hw
                            pe(lambda e, hh=hh, hw=hw, hg=hg, pd=pd, Tt=Tt, r_=r_: e.matmul(pd.ap(hh * 128, (hh + 1) * 128, r_), lhsT=Tt[r_, hw, r_], rhs=vtok[r_, b, hg * 128:(hg + 1) * 128],
                                                                                           start=True, stop=False), reads=[K_(Tt, g), (vtok, b)], writes=[pd.key])
                            pe(lambda e, hh=hh, hw=hw, hg=hg, pd=pd, r_=r_, si=si: e.matmul(pd.ap(hh * 128, (hh + 1) * 128, r_), lhsT=nWT[:, hw, r_], rhs=Sb[si][:, hg, :],
                                                                                           start=False, stop=True), reads=[K_(nWT, g), SK(Sb[si], g)], writes=[pd.key])
                        dve(lambda e, g=g, pd=pd, r_=r_: e.tensor_tensor(out=dlt[r_, hs[g], :], in0=pd.ap(0, GW, r_).rearrange("p (h d) -> p h d", h=HG),
                                                                         in1=G(1)[r_, hgl[g]].unsqueeze(2).to_broadcast([64, HG, 128]), op=ALU.mult),
                            reads=[pd.key, gsm], writes=[(dlt, (ch, g))])
                    pos = {}
                    if full:
                        for g in grp:
                            po = nps1()
                            pos[g] = po
                            for hh in range(HG):
                                hw = g * HG + hh
                                hg = h0 + hw
                                pe(lambda e, hh=hh, hw=hw, hg=hg, po=po, r_=r_, si=si: e.matmul(po.ap(hh * 128, (hh + 1) * 128, r_), lhsT=qdT[:, hw, r_], rhs=Sb[si][:, hg, :],
                                                                                               start=True, stop=False), reads=[K_(qdT, g), SK(Sb[si], g)], writes=[po.key])
                                pe(lambda e, hh=hh, hw=hw, po=po, r_=r_: e.matmul(po.ap(hh * 128, (hh + 1) * 128, r_), lhsT=PT[r_, hw, r_], rhs=dlt[r_, hw, :],
                                                                                 start=False, stop=True), reads=[K_(PT, g), (dlt, (ch, g))], writes=[po.key])
                    for g in grp:
                        psu = nps1()
                        for hh in range(HG):
                            hw = g * HG + hh
                            pe(lambda e, hh=hh, hw=hw, psu=psu, r_=r_: e.matmul(psu.ap(hh * 128, (hh + 1) * 128), lhsT=kd[r_, hw, :], rhs=dlt[r_, hw, :], start=True, stop=True),
                               reads=[K_(kd, g), (dlt, (ch, g))], writes=[psu.key])
                        for hh in range(HG):
                            hg = h0 + g * HG + hh
                            dve(lambda e, hh=hh, hg=hg, psu=psu, si=si, ch=ch: e.scalar_tensor_tensor(out=Sf[si][:, hg, :], in0=Sf[si][:, hg, :],
                                                                                                     scalar=G(7 + ch)[:, hg:hg + 1], in1=psu.ap(hh * 128, (hh + 1) * 128),
                                                                                                     op0=ALU.mult, op1=ALU.add),
                                reads=[psu.key, gsm, SK(Sf[si], g)], writes=[SK(Sf[si], g)])
                        act(lambda e, g=g, si=si: e.copy(out=Sb[si][:, hgl[g], :], in_=Sf[si][:, hgl[g], :]), reads=[SK(Sf[si], g)], writes=[SK(Sb[si], g)])
                    if full:
                        for g in grp:
                            po = pos[g]
                            o3 = po.ap(0, GW, r_).rearrange("p (h d) -> p h d", h=HG)
                            OK_ = (oss, (ch, g))
                            act(lambda e, g=g, o3=o3: e.activation(out=E1[r_, hs[g], :], in_=o3, func=AF.Square), reads=[po.key], writes=[(E1, ("o", ch, g))])
                            dve(lambda e, g=g: e.tensor_reduce(out=oss[r_, 0, hs[g]], in_=E1[r_, hs[g], :], axis=AX.X, op=ALU.add), reads=[(E1, ("o", ch, g))], writes=[OK_])
                            act(lambda e, g=g: e.activation(out=oss[r_, 1, hs[g]], in_=oss[r_, 0, hs[g]], func=AF.Ln, scale=1.0 / 128, bias=EPS), reads=[OK_], writes=[OK_])
                            act(lambda e, g=g: e.activation(out=oss[r_, 2, hs[g]], in_=oss[r_, 1, hs[g]], func=AF.Exp, scale=-0.5), reads=[OK_], writes=[OK_])
                            dve(lambda e, g=g, o3=o3: e.tensor_tensor(out=onb[r_, hs[g], :], in0=o3, in1=oss[r_, 2, hs[g]].unsqueeze(2).to_broadcast([64, HG, 128]),
                                                                      op=ALU.mult), reads=[po.key, OK_], writes=[(onb, (ch, g))])
                    if kind == 1 and hf == 1:
                        sq_ = seqs[0] + b * 2 + ch
                        P.dma(sgs.h.ap()[sq_].rearrange("h k v -> k h v"), Sf[si][:], reads=[Sf[si]], writes=[sgs])
                dbg("onb", onb, onb[:], [128, 8, 128], BF16)
                if full:
                    for g in grp:
                        pot = nps1()
                        for hh in range(HG):
                            pe(lambda e, g=g, hh=hh, pot=pot: e.transpose(pot.apb(hh * 128, (hh + 1) * 128), onb[:, g * HG + hh, :], identB[:]),
                               reads=[(onb, (0, g)), (onb, (1, g)), identB], writes=[pot.key])
                        a0 = h0 + g * HG
                        dve(lambda e, g=g, pot=pot, a0=a0: e.tensor_tensor(out=A1[:, a0:a0 + HG, bs_], in0=pot.apb(0, GW).rearrange("p (h t) -> p h t", h=HG),
                                                                           in1=A1[:, a0:a0 + HG, bs_], op=ALU.mult),
                            reads=[pot.key] + [(A1, a0 + i) for i in range(HG)], writes=[(A1, a0 + i) for i in range(HG)])
        if kind == 0 and last and full:
            P.dma(sgp.h.ap().rearrange("h k v -> k h v"), Sf[0][:], reads=[Sf[0]], writes=[sgp])
        if not full:
            return
        linear_fm("w_gdn_out", 0, D, 16, A1, lambda k: A1[:, k, 0:NTt], NTt, resid_evac(NTt))
        ffn(1, NTt)
        act(lambda e: e.activation(out=hT[:, :, 0:NTt], in_=xT[:, :, 0:NTt], func=AF.Square), reads=[xT], writes=[hT])
        pt = nps()
        for k in range(8):
            pe(lambda e, k=k, pt=pt: e.matmul(pt.h[:, 0:NTt], lhsT=onesB[:], rhs=hT[:, k, 0:NTt], start=(k == 0), stop=(k == 7)), reads=[onesB, hT], writes=[pt])
        act(lambda e, pt=pt: e.activation(out=cacc[:, 0, 0:NTt], in_=pt.h[:, 0:NTt], func=AF.Ln, scale=1.0 / D, bias=EPS), reads=[pt], writes=[cacc])
        act(lambda e: e.activation(out=cacc[:, 0, 0:NTt], in_=cacc[:, 0, 0:NTt], func=AF.Exp, scale=-0.5), reads=[cacc], writes=[cacc])
        for b in range(nb):
            prs = nps()
            pe(lambda e, b=b, prs=prs: e.transpose(prs.h[:, 0:128], cacc[:, 0, b * 128:(b + 1) * 128], identF[:]), reads=[cacc, identF], writes=[prs])
            act(lambda e, prs=prs: e.copy(out=bmv[:, 3:4], in_=prs.h[:, 0:1]), reads=[prs], writes=[bmv])
            for hf in range(2):
                pt = nps()
                for j in range(4):
                    k = hf * 4 + j
                    pe(lambda e, pt=pt, j=j, k=k, b=b: e.transpose(pt.h[:, j * 128:(j + 1) * 128], xT[:, k, b * 128:(b + 1) * 128], identF[:]),
                       reads=[xT, identF], writes=[pt])
                dve(lambda e, pt=pt, hf=hf: e.scalar_tensor_tensor(out=vtk[:, 1, hf * 512:(hf + 1) * 512], in0=pt.h[:, 0:512], scalar=bmv[:, 3:4],
                                                                   in1=wfin[:, hf * 512:(hf + 1) * 512], op0=ALU.mult, op1=ALU.mult),
                    reads=[pt, bmv, wfin], writes=[(vtk, 1)])
            P.dma(dst.h.ap()[tok0 + b * 128: tok0 + (b + 1) * 128, :], vtk[:, 1, 0:1024], reads=[(vtk, 1)], writes=[dst], eng="act")

    if n_ptiles:
        setup_sgu_kind(0)
    for ti in range(n_state):
        run_tile(0, ti * NT, NT, ti == 0, False, None, mode="state", need_q_halo=(ti == n_state - 1))
    for ti in range(n_ptiles):
        run_tile(0, ti * NT, NT, ti == 0, ti == n_ptiles - 1, None, apply_flag=(ti == 0 and n_state > 0))
    if NS:
        setup_sgu_kind(1)
        run_tile(1, 0, NSAMP, True, True, (0,))
    stats = P.emit()
    return nc, stats


WNAMES = ["norm_mix", "norm_ffn", "norm_final", "sgu_w_in", "sgu_ln_g", "sgu_ln_b", "sgu_w_s", "sgu_b_s", "sgu_w_out",
          "gdn_w_in", "gdn_w_conv", "gdn_a_log", "gdn_dt_bias", "gdn_w_onorm", "gdn_w_out", "ffn_w_gate", "ffn_w_up", "ffn_w_down"]
SQUEEZE0 = {"sgu_w_in", "sgu_ln_g", "sgu_ln_b", "sgu_w_s", "sgu_b_s", "sgu_w_out", "gdn_w_in", "gdn_w_conv", "gdn_a_log",
            "gdn_dt_bias", "gdn_w_onorm", "gdn_w_out"}

_CACHE = {}


def kernel(**inputs):
    x_prompt = np.ascontiguousarray(inputs["x_prompt"], dtype=np.float32)
    x_sample = np.ascontiguousarray(inputs["x_sample"], dtype=np.float32)
    state_gdn = np.ascontiguousarray(inputs["state_gdn"], dtype=np.float32)
    state_conv = np.ascontiguousarray(inputs["state_conv"], dtype=np.float32)
    B, SEQ, _ = x_prompt.shape
    DB, DS, _ = x_sample.shape
    n_cores = 8
    NT = 256
    NS = DB // n_cores
    HALF = SEQ // 2
    n_half = HALF // NT
    key = (n_half, NT, NS)
    if key not in _CACHE:
        _CACHE[key] = build_program(n_half, NT, NS, n_state=n_half)[0]
    nc = _CACHE[key]
    wd = {}
    for n in WNAMES:
        a = np.ascontiguousarray(inputs[n], dtype=np.float32)
        wd[n] = a[0] if n in SQUEEZE0 else a
    in_maps = []
    for c in range(n_cores):
        m = dict(wd)
        sq, second = c % B, c // B
        m["xq"] = x_prompt[sq, :HALF]
        m["xp"] = x_prompt[sq, HALF:] if second else x_prompt[sq, :HALF]
        m["flag"] = np.array([1.0 if second else 0.0], dtype=np.float32)
        m["xs"] = x_sample[c * NS:(c + 1) * NS].reshape(NS * DS, D)
        m["sg"] = state_gdn[0, c * NS:(c + 1) * NS]
        m["sc"] = state_conv[0, c * NS:(c + 1) * NS]
        in_maps.append(m)
    res = run_bass_kernel_spmd(nc, in_maps, core_ids=list(range(n_cores)))
    r = res.results
    y_prompt = np.stack([np.concatenate([r[c]["yp"], r[c + B]["yp"]], axis=0) for c in range(B)]).astype(np.float32)
    y_sample = np.concatenate([r[c]["ys"].reshape(NS, DS, D) for c in range(n_cores)]).astype(np.float32)
    ns_gdn_p = np.stack([r[c + B]["sgp"] for c in range(B)])[None].astype(np.float32)
    ns_conv_p = np.stack([r[c + B]["scp"] for c in range(B)])[None].astype(np.float32)
    ns_gdn_s = np.concatenate([r[c]["sgs"] for c in range(n_cores)])[None].astype(np.float32)
    ns_conv_s = np.concatenate([r[c]["scs"] for c in range(n_cores)])[None].astype(np.float32)
    ns_v_s = np.concatenate([r[c]["svs"].reshape(NS, DS, DSGU) for c in range(n_cores)])[None].astype(np.float32)
    return (y_prompt, y_sample, ns_gdn_p, ns_conv_p, ns_gdn_s, ns_conv_s, ns_v_s)
```

```python
import numpy as np
import concourse.bass as bass
import concourse.mybir as mybir
from concourse.bass_utils import run_bass_kernel_spmd

F32 = mybir.dt.float32
BF16 = mybir.dt.bfloat16
AF = mybir.ActivationFunctionType
ALU = mybir.AluOpType
AX = mybir.AxisListType

COMPUTE = ("pe", "act", "dve", "pool")


class T:
    __slots__ = ("h", "name", "last_w", "readers", "dma_sem", "kind")

    def __init__(self, h, name, kind="sb"):
        self.h = h
        self.name = name
        self.kind = kind
        self.last_w = {}
        self.readers = {}
        self.dma_sem = None

    def __getitem__(self, k):
        return self.h[k]


class Op:
    __slots__ = ("eng", "fn", "deps", "idx", "signal", "sigval", "is_dma", "dsem", "dval")

    def __init__(self, eng, fn):
        self.eng = eng
        self.fn = fn
        self.deps = []
        self.signal = False
        self.sigval = None
        self.is_dma = False
        self.dsem = None
        self.dval = None


class _Rec:
    def __getattr__(self, name):
        def f(*a, **k):
            self.call = (name, a, k)
            return self
        return f


def _replay(name, a, k):
    return lambda e: getattr(e, name)(*a, **k)


class Prog:
    def __init__(self, nc):
        self.nc = nc
        self.q = {e: [] for e in ("pe", "act", "dve", "pool", "sp")}
        self.tiles = []
        self._sems = []

    def sb(self, name, shape, dtype):
        t = T(self.nc.alloc_sbuf_tensor(name, list(shape), dtype), name)
        self.tiles.append(t)
        return t

    def ps(self, name, shape, dtype=F32):
        t = T(self.nc.alloc_psum_tensor(name, list(shape), dtype), name, "ps")
        self.tiles.append(t)
        return t

    def dram(self, name, shape, dtype, kind="Internal"):
        t = T(self.nc.dram_tensor(name, list(shape), dtype, kind=kind), name, "dram")
        self.tiles.append(t)
        return t

    def ext(self, h, name):
        t = T(h, name, "dram")
        self.tiles.append(t)
        return t

    def new_sem(self, name):
        cm = self.nc.semaphore(name)
        s = cm.__enter__()
        self._sems.append(cm)
        return s

    @staticmethod
    def _conf(k1, k2):
        if k1 is None or k2 is None or k1 == k2:
            return True
        if isinstance(k1, tuple) and isinstance(k2, tuple):
            n = min(len(k1), len(k2))
            return k1[:n] == k2[:n]
        return False

    @staticmethod
    def _rk(x):
        return (x, None) if isinstance(x, T) else x

    def _add_deps(self, op, reads, writes, extra=()):
        deps = list(extra)
        for (t, k) in reads:
            for kk, w in t.last_w.items():
                if self._conf(k, kk):
                    deps.append(w)
        for (t, k) in writes:
            for kk, w in t.last_w.items():
                if self._conf(k, kk):
                    deps.append(w)
            for kk, rs in t.readers.items():
                if self._conf(k, kk):
                    deps.extend(rs)
        best = {}
        for d in deps:
            if d is op:
                continue
            if d.is_dma:
                key = ("dma", id(d.dsem))
                if key not in best or best[key].dval < d.dval:
                    best[key] = d
            else:
                if d.eng == op.eng and op.eng == "pe":
                    continue
                key = d.eng
                if key not in best or best[key].idx < d.idx:
                    best[key] = d
        for d in best.values():
            op.deps.append(d)
            if not d.is_dma:
                d.signal = True
        for (t, k) in reads:
            t.readers.setdefault(k, []).append(op)
        for (t, k) in writes:
            if k is None:
                t.last_w = {None: op}
                t.readers = {}
            else:
                t.last_w[k] = op
                t.readers[k] = []

    def op(self, eng, fn, reads=(), writes=(), extra=()):
        rec = _Rec()
        fn(rec)
        o = Op(eng, _replay(*rec.call))
        o.idx = len(self.q[eng])
        self._add_deps(o, [self._rk(r) for r in reads], [self._rk(w) for w in writes], extra)
        self.q[eng].append(o)
        return o

    def dma(self, out_ap, in_ap, reads=(), writes=(), sem_of=None, extra=(), eng="sp"):
        reads = [self._rk(r) for r in reads]
        writes = [self._rk(w) for w in writes]
        if sem_of is None:
            cands = [x for x in list(writes) + list(reads) if x[0].kind == "sb"]
            sem_of = cands[0] if cands else (list(writes) + list(reads))[0]
        st, sk = self._rk(sem_of)
        if st.dma_sem is None:
            st.dma_sem = {}
        if sk not in st.dma_sem:
            st.dma_sem[sk] = [self.new_sem("d%d" % len(self._sems)), 0]
        ent = st.dma_sem[sk]
        ent[1] += 16
        o = Op(eng, lambda e: e.dma_start(out=out_ap, in_=in_ap))
        o.is_dma = True
        o.dsem = ent[0]
        o.dval = ent[1]
        o.idx = len(self.q[eng])
        self._add_deps(o, reads, writes, extra)
        self.q[eng].append(o)
        return o

    def emit(self):
        nc = self.nc
        sems = {e: self.new_sem("s_" + e) for e in COMPUTE}
        for e in COMPUTE:
            c = 0
            for o in self.q[e]:
                if (not o.is_dma) and o.signal:
                    c += 1
                    o.sigval = c
        engmap = {"pe": "tensor", "act": "scalar", "dve": "vector", "pool": "gpsimd", "sp": "sync"}
        stats = {"waits": 0, "instr": 0}

        def emit_queue(ename, engine):
            known = {}
            for o in self.q[ename]:
                for d in o.deps:
                    if d.is_dma:
                        key, val, sem = ("dma", id(d.dsem)), d.dval, d.dsem
                    else:
                        key, val, sem = d.eng, d.sigval, sems[d.eng]
                    if known.get(key, 0) >= val:
                        continue
                    engine.wait_ge(sem, val)
                    stats["waits"] += 1
                    known[key] = val
                ins = o.fn(engine)
                stats["instr"] += 1
                if o.is_dma:
                    ins.then_inc(o.dsem, 16)
                elif o.signal:
                    ins.then_inc(sems[ename], 1)
            if ename == "sp":
                for t in self.tiles:
                    if t.dma_sem:
                        for ent in t.dma_sem.values():
                            engine.wait_ge(ent[0], ent[1])

        with nc.Block() as block:
            for ename in ["sp", "pool", "act", "dve", "pe"]:
                getattr(block, engmap[ename])(lambda engine, _n=ename: emit_queue(_n, engine))
        return stats


D = 1024
DSGU = 2048
DFF = 2816
GIN = 6176
EPS = 1e-6
NEG = -30000.0
WSLOT = 4096


DBG = []
NGROUPS = 2


def build_program(n_ptiles, NT, NS, n_state=0):
    assert NT % 128 == 0 and NS % 2 == 0
    nc = bass.Bass("TRN2", target_bir_lowering=False)
    P = Prog(nc)
    NP = n_ptiles * NT
    NSAMP = NS * 64
    NTMAX = max(NT, NSAMP) if NS else NT
    assert NSAMP <= NT or n_ptiles == 0
    NBMAX = NTMAX // 128

    def din(name, shape):
        return P.ext(nc.dram_tensor(name, list(shape), F32, kind="ExternalInput"), name)

    def dout(name, shape):
        return P.ext(nc.dram_tensor(name, list(shape), F32, kind="ExternalOutput"), name)

    xp = din("xp", [max(NP, 1), D])
    xq = din("xq", [max(n_state * NT, 1), D])
    flag = din("flag", [1])
    xs = din("xs", [max(NSAMP, 1), D])
    sg = din("sg", [max(NS, 1), 16, 128, 128])
    sc = din("sc", [max(NS, 1), 3, 4096])
    norm_mix = din("norm_mix", [2, D])
    norm_ffn = din("norm_ffn", [2, D])
    norm_final = din("norm_final", [D])
    sgu_w_in = din("sgu_w_in", [D, 4096])
    sgu_ln_g = din("sgu_ln_g", [DSGU])
    sgu_ln_b = din("sgu_ln_b", [DSGU])
    sgu_w_s = din("sgu_w_s", [8, 128, 128])
    sgu_b_s = din("sgu_b_s", [8, 128])
    sgu_w_out = din("sgu_w_out", [DSGU, D])
    gdn_w_in = din("gdn_w_in", [D, GIN])
    gdn_w_conv = din("gdn_w_conv", [4, 4096])
    gdn_a_log = din("gdn_a_log", [16])
    gdn_dt_bias = din("gdn_dt_bias", [16])
    gdn_w_onorm = din("gdn_w_onorm", [128])
    gdn_w_out = din("gdn_w_out", [DSGU, D])
    ffn_w_gate = din("ffn_w_gate", [2, D, DFF])
    ffn_w_up = din("ffn_w_up", [2, D, DFF])
    ffn_w_down = din("ffn_w_down", [2, DFF, D])

    yp = dout("yp", [max(NP, 1), D])
    ys = dout("ys", [max(NSAMP, 1), D])
    sgp = dout("sgp", [16, 128, 128])
    scp = dout("scp", [3, 4096])
    sgs = dout("sgs", [max(NS, 1), 16, 128, 128])
    scs = dout("scs", [max(NS, 1), 3, 4096])
    svs = dout("svs", [max(NSAMP, 1), DSGU])

    xT = P.sb("xT", [128, 8, NTMAX], F32)
    hT = P.sb("hT", [128, 8, NTMAX], BF16)
    A1 = P.sb("A1", [128, 22, NTMAX], BF16)
    kqT = P.sb("kqT", [128, 16, NTMAX], BF16)
    wsl = [P.sb("wsl%d" % i, [128, WSLOT], BF16) for i in range(4)]
    vtk = P.sb("vtk", [128, NBMAX, 2048], F32)
    vnb = P.sb("vnb", [128, 2048], BF16)
    vtok = P.sb("vtok", [128, NBMAX, 2048], BF16)
    ktok = P.sb("ktok", [128, NBMAX, 1024], BF16)
    grep = P.sb("grep", [128, 2048], F32)
    brep = P.sb("brep", [128, 2048], F32)
    wfin = P.sb("wfin", [128, 1024], F32)
    identF = P.sb("identF", [128, 128], F32)
    identB = P.sb("identB", [128, 128], BF16)
    onesF = P.sb("onesF", [128, 128], F32)
    onesB = P.sb("onesB", [128, 128], BF16)
    tril = P.sb("tril", [128, 128], F32)
    incl = P.sb("incl", [128, 128], F32)
    strict = P.sb("strict", [128, 128], F32)
    negm = P.sb("negm", [128, 128], BF16)
    selA = P.sb("selA", [128, 128], F32)
    selB = P.sb("selB", [128, 128], F32)
    blk1 = P.sb("blk1", [128, 128], F32)
    wsT1 = P.sb("wsT", [128, 8, 128], BF16)
    bsrow1 = P.sb("bsrow", [1, 1024], F32)
    wsT = [wsT1, wsT1]
    bsrow = [bsrow1, bsrow1]
    cw = P.sb("cw", [128, 4, 32], F32)
    nsc = P.sb("nsc", [128, 40], F32)
    alr = P.sb("alr", [128, 16], F32)
    dtr = P.sb("dtr", [128, 16], F32)
    hal = P.sb("hal", [128, 32, 4, 3], F32)
    stg = P.sb("stg", [128, 128], F32)
    flg = P.sb("flg", [128, 1], F32)
    xc = P.sb("xc", [128, 4, NTMAX + 12], F32)
    cacc = P.sb("cacc", [128, 4, NTMAX], F32)
    vTg = P.sb("vTg", [128, 4, NTMAX], BF16)
    Sf = [P.sb("Sf%d" % i, [128, 16, 128], F32) for i in range(2)]
    Sb = [P.sb("Sb%d" % i, [128, 16, 128], BF16) for i in range(2)]
    gts_ = [P.sb("gts%d" % i, [128, 32], F32) for i in range(NBMAX)]
    gsm_ = [P.sb("gsm%d" % i, [128, 12, 16], F32) for i in range(NBMAX)]
    gsm_unused = None
    Gtri = P.sb("Gtri", [128, 8, 128], F32)
    E1 = P.sb("E1", [128, 8, 128], F32)
    Es = P.sb("Es", [128, 8, 128], F32)
    PT = P.sb("PT", [128, 8, 128], BF16)
    Xa = [P.sb("Xa%d" % i, [128, 8, 128], BF16) for i in range(2)]
    Xt = [P.sb("Xt%d" % i, [128, 8, 128], BF16) for i in range(2)]
    Pm = [P.sb("Pm%d" % i, [128, 8, 128], BF16) for i in range(2)]
    P32 = P.sb("P32", [128, 8, 128], F32)
    kg = P.sb("kg", [128, 8, 128], BF16)
    kd = P.sb("kd", [128, 8, 128], BF16)
    nWT = P.sb("nWT", [128, 8, 128], BF16)
    qdT = P.sb("qdT", [128, 8, 128], BF16)
    dlt = P.sb("dlt", [128, 8, 128], BF16)
    onb = P.sb("onb", [128, 8, 128], BF16)
    oss = P.sb("oss", [128, 3, 8], F32)
    bst = P.sb("bst", [128, 4, 6], F32)
    bmv = P.sb("bmv", [128, 4], F32)

    pp = [P.ps("pp%d" % i, [128, 1024], F32) for i in range(4)]
    ppc = [0]

    def nps():
        t = pp[ppc[0] % 4]
        ppc[0] += 1
        return t

    class PH:
        def __init__(self, t, half):
            self.t, self.off, self.key = t, half * 512, (t, ("h", half))

        def ap(self, a, b, rows=slice(None)):
            return self.t.h[rows, self.off + a:self.off + b]

        def apb(self, a, b):
            return self.t.h[:, :].bitcast(BF16)[:, 2 * self.off + a:2 * self.off + b]

    p1c = [0]

    def nps1():
        i = p1c[0]
        p1c[0] += 1
        return PH(pp[(i // 2) % 4], i % 2)

    def pbf(t):
        return t.h[:, :].bitcast(BF16)

    wsc = [0]

    def nws():
        t = wsl[wsc[0] % 4]
        wsc[0] += 1
        return t

    dbg_done = set()

    def dbg(name, t, ap, shape, dtype=F32):
        if name not in DBG or name in dbg_done:
            return
        dbg_done.add(name)
        o = P.ext(nc.dram_tensor("dbg_" + name, list(shape), dtype, kind="ExternalOutput"), "dbg_" + name)
        P.dma(o.h.ap(), ap, reads=[t], writes=[o])

    def pool(fn, reads=(), writes=()):
        return P.op("pool", fn, reads, writes)

    def dve(fn, reads=(), writes=()):
        return P.op("dve", fn, reads, writes)

    def act(fn, reads=(), writes=()):
        return P.op("act", fn, reads, writes)

    def pe(fn, reads=(), writes=()):
        return P.op("pe", fn, reads, writes)

    pool(lambda e: e.memset(onesF[:], 1.0), writes=[onesF])
    pool(lambda e: e.memset(onesB[:], 1.0), writes=[onesB])
    pool(lambda e: e.memset(identF[:], 0.0), writes=[identF])
    pool(lambda e: e.affine_select(out=identF[:], in_=onesF[:], pattern=[[-1, 128]], base=0, channel_multiplier=1,
                                   compare_op=ALU.is_equal, fill=0.0), reads=[onesF], writes=[identF])
    pool(lambda e: e.tensor_copy(out=identB[:], in_=identF[:]), reads=[identF], writes=[identB])
    pool(lambda e: e.memset(tril[:], 0.0), writes=[tril])
    pool(lambda e: e.affine_select(out=tril[:], in_=onesF[:], pattern=[[1, 128]], base=0, channel_multiplier=-1,
                                   compare_op=ALU.is_ge, fill=0.0), reads=[onesF], writes=[tril])
    pool(lambda e: e.tensor_copy(out=incl[:], in_=tril[:]), reads=[tril], writes=[incl])
    pool(lambda e: e.memset(incl[0:64, 64:128], 0.0), writes=[incl])
    pool(lambda e: e.memset(strict[:], 0.0), writes=[strict])
    pool(lambda e: e.affine_select(out=strict[:], in_=onesF[:], pattern=[[1, 128]], base=0, channel_multiplier=-1,
                                   compare_op=ALU.is_gt, fill=0.0), reads=[onesF], writes=[strict])
    pool(lambda e: e.memset(strict[0:64, 64:128], 0.0), writes=[strict])
    pool(lambda e: e.tensor_scalar(out=negm[:], in0=incl[:], scalar1=-1.0, scalar2=-NEG, op0=ALU.add, op1=ALU.mult),
         reads=[incl], writes=[negm])
    pool(lambda e: e.memset(selA[:], 0.0), writes=[selA])
    pool(lambda e: e.memset(selA[0:64, :], 1.0), writes=[selA])
    pool(lambda e: e.memset(selB[:], 0.0), writes=[selB])
    pool(lambda e: e.memset(selB[64:128, :], 1.0), writes=[selB])
    pool(lambda e: e.memset(blk1[:], 0.0), writes=[blk1])
    pool(lambda e: e.memset(blk1[0:64, 0:64], 1.0), writes=[blk1])
    pool(lambda e: e.memset(blk1[64:128, 64:128], 1.0), writes=[blk1])
    pool(lambda e: e.memset(hal[:], 0.0), writes=[hal])

    def bcast_rows(ap1d, n):
        return ap1d.partition_broadcast(128)

    P.dma(grep[:], sgu_ln_g.h.ap().partition_broadcast(128), reads=[sgu_ln_g], writes=[grep])
    P.dma(brep[:], sgu_ln_b.h.ap().partition_broadcast(128), reads=[sgu_ln_b], writes=[brep])
    P.dma(wfin[:], norm_final.h.ap().partition_broadcast(128), reads=[norm_final], writes=[wfin])
    P.dma(alr[:], gdn_a_log.h.ap().partition_broadcast(128), reads=[gdn_a_log], writes=[alr])
    P.dma(dtr[:], gdn_dt_bias.h.ap().partition_broadcast(128), reads=[gdn_dt_bias], writes=[dtr])
    P.dma(flg[:], flag.h.ap().partition_broadcast(128), reads=[flag], writes=[flg])
    act(lambda e: e.activation(out=alr[:], in_=alr[:], func=AF.Exp), reads=[alr], writes=[alr])
    dve(lambda e: e.tensor_scalar(out=alr[:], in0=alr[:], scalar1=-1.0, scalar2=None, op0=ALU.mult), reads=[alr], writes=[alr])

    def small_T(dst_ap, dst_t, src_ap, src_t, R):
        P.dma(stg[0:R, :], src_ap, reads=[src_t], writes=[stg])
        pt = nps()
        pe(lambda e: e.transpose(pt.h[:, 0:R], stg[0:R, :], identF[0:R, 0:R]), reads=[stg, identF], writes=[pt])
        act(lambda e: e.copy(out=dst_ap, in_=pt.h[:, 0:R]), reads=[pt], writes=[dst_t])

    small_T(cw[:].rearrange("p i c -> p (i c)"), cw, gdn_w_conv.h.ap().rearrange("i (c p) -> (i c) p", p=128), gdn_w_conv, 128)
    small_T(nsc[:, 0:16], nsc, norm_mix.h.ap().rearrange("l (k p) -> (l k) p", p=128), norm_mix, 16)
    small_T(nsc[:, 16:32], nsc, norm_ffn.h.ap().rearrange("l (k p) -> (l k) p", p=128), norm_ffn, 16)
    small_T(nsc[:, 32:33], nsc, gdn_w_onorm.h.ap().rearrange("(o p) -> o p", o=1), gdn_w_onorm, 1)

    def setup_sgu_kind(kind):
        if kind == 0:
            P.dma(vtk[:, 0, 0:1024].rearrange("p (g s) -> p g s", g=8), sgu_w_s.h.ap().rearrange("g t s -> t g s"),
                  reads=[sgu_w_s], writes=[(vtk, 0)])
            P.dma(bsrow[0][:], sgu_b_s.h.ap().rearrange("(o g) t -> o (g t)", o=1), reads=[sgu_b_s], writes=[bsrow[0]])
        else:
            pool(lambda e: e.memset(vtk[:, 0, 0:1024], 0.0), writes=[(vtk, 0)])
            v3 = vtk[:, 0, 0:1024].rearrange("p (g s) -> p g s", g=8)
            P.dma(v3[0:64, :, 0:64], sgu_w_s.h.ap().rearrange("g t s -> t g s")[0:64, :, 0:64], reads=[sgu_w_s], writes=[(vtk, 0)])
            P.dma(v3[64:128, :, 64:128], sgu_w_s.h.ap().rearrange("g t s -> t g s")[0:64, :, 0:64], reads=[sgu_w_s], writes=[(vtk, 0)])
            b4 = bsrow[1][:].rearrange("o (g h t) -> o g h t", g=8, h=2)
            for hh in range(2):
                P.dma(b4[:, :, hh, :], sgu_b_s.h.ap().rearrange("(o g) t -> o g t", o=1)[:, :, 0:64], reads=[sgu_b_s], writes=[bsrow[1]])
        msk = tril if kind == 0 else incl
        for g4 in range(2):
            pt = nps()
            for j in range(4):
                g = g4 * 4 + j
                pe(lambda e, g=g, j=j, pt=pt: e.transpose(pt.h[:, j * 128:(j + 1) * 128], vtk[:, 0, g * 128:(g + 1) * 128], identF[:]),
                   reads=[(vtk, 0), identF], writes=[pt])
            dve(lambda e, g4=g4, pt=pt, kind=kind, msk=msk: e.tensor_tensor(
                out=wsT[kind][:, g4 * 4:(g4 + 1) * 4, :], in0=pt.h[:, 0:512].rearrange("p (g t) -> p g t", g=4),
                in1=msk[:].unsqueeze(1).to_broadcast([128, 4, 128]), op=ALU.mult), reads=[pt, msk], writes=[wsT[kind]])

    wdefs = [
        ("w_sgu_in", sgu_w_in, sgu_w_in.h.ap(), D, 4096, (0, 0)),
        ("w_sgu_out", sgu_w_out, sgu_w_out.h.ap(), DSGU, D, None),
        ("w_gate0", ffn_w_gate, ffn_w_gate.h.ap()[0], D, DFF, (16, 0)),
        ("w_up0", ffn_w_up, ffn_w_up.h.ap()[0], D, DFF, (16, 0)),
        ("w_down0", ffn_w_down, ffn_w_down.h.ap()[0], DFF, D, None),
        ("w_gdn_in", gdn_w_in, gdn_w_in.h.ap(), D, GIN, (8, 0)),
        ("w_gdn_out", gdn_w_out, gdn_w_out.h.ap(), DSGU, D, (32, 1)),
        ("w_gate1", ffn_w_gate, ffn_w_gate.h.ap()[1], D, DFF, (24, 0)),
        ("w_up1", ffn_w_up, ffn_w_up.h.ap()[1], D, DFF, (24, 0)),
        ("w_down1", ffn_w_down, ffn_w_down.h.ap()[1], DFF, D, None),
    ]
    WS = {}
    pro_ops = []
    cnt = 0
    engs = ["act", "dve"]
    stf = [(vtk, 0, lambda w: vtk[:, 0, 0:w]), (vtk, 1, lambda w: vtk[:, 1, 0:w]),
           (Sf[0], None, lambda w: Sf[0][:].rearrange("p h d -> p (h d)")[:, 0:w]), (Sf[1], None, lambda w: Sf[1][:].rearrange("p h d -> p (h d)")[:, 0:w])]
    stb = [(vtok, 0, lambda w: vtok[:, 0, 0:w]), (vtok, 1, lambda w: vtok[:, 1, 0:w]),
           (Sb[0], None, lambda w: Sb[0][:].rearrange("p h d -> p (h d)")[:, 0:w]), (Sb[1], None, lambda w: Sb[1][:].rearrange("p h d -> p (h d)")[:, 0:w])]
    for (name, srct, srcap, K, C, fold) in wdefs:
        scr = P.dram(name, [K, C], BF16)
        WS[name] = (scr, K // 128, C)
        for kc in range(K // 128):
            for c0 in range(0, C, 2048):
                cwid = min(2048, C - c0)
                i = cnt % 4
                cnt += 1
                ft, fk, fap = stf[i]
                bt, bk, bap = stb[i]
                fkey = (ft, fk) if fk is not None else ft
                bkey = (bt, bk) if bk is not None else bt
                P.dma(fap(cwid), srcap[kc * 128:(kc + 1) * 128, c0:c0 + cwid], reads=[srct], writes=[fkey])
                en = engs[cnt % 2]
                if fold is None:
                    if en == "act":
                        act(lambda e: e.copy(out=bap(cwid), in_=fap(cwid)), reads=[fkey], writes=[bkey])
                    else:
                        dve(lambda e: e.tensor_copy(out=bap(cwid), in_=fap(cwid)), reads=[fkey], writes=[bkey])
                else:
                    col = fold[0] + (kc if fold[1] == 0 else 0)
                    if en == "act":
                        act(lambda e: e.activation(out=bap(cwid), in_=fap(cwid), func=AF.Copy, scale=nsc[:, col:col + 1]), reads=[fkey, nsc], writes=[bkey])
                    else:
                        dve(lambda e: e.tensor_scalar(out=bap(cwid), in0=fap(cwid), scalar1=nsc[:, col:col + 1], scalar2=None, op0=ALU.mult),
                            reads=[fkey, nsc], writes=[bkey])
                o = P.dma(scr.h.ap()[kc * 128:(kc + 1) * 128, c0:c0 + cwid], bap(cwid), reads=[bkey], writes=[(scr, (kc, c0))], eng="act")
                pro_ops.append(o)
    fence = P.dma(stg[0:1, 0:4], onesF[0:1, 0:4], reads=[onesF], writes=[stg], extra=pro_ops)
    for name in WS:
        WS[name][0].last_w = {None: fence}
        WS[name][0].readers = {}

    def wpieces(name, c_lo, c_hi, KC):
        scr = WS[name][0]
        pcmax = (WSLOT // KC) // 128 * 128
        c0 = c_lo
        while c0 < c_hi:
            pc = min(pcmax, c_hi - c0)
            slot = nws()
            view = slot[:, 0:KC * pc].rearrange("p (k c) -> p k c", k=KC)
            P.dma(view, scr.h.ap().rearrange("(k p) c -> p k c", p=128)[:, :, c0:c0 + pc], reads=[scr], writes=[slot])
            yield slot, view, c0, pc
            c0 += pc

    def linear_fm(name, c_lo, c_hi, KC, rhs_t, rhs_ap, NTt, evac, oc_base=0, piped=False):
        gmax = min(4, max(1, 1024 // NTt))
        per_bank = max(1, 512 // NTt)
        pend = {"b1": None, "b2": None, "b2n": None}
        for slot, view, c0, pc in wpieces(name, c_lo, c_hi, KC):
            nch = pc // 128
            j0 = 0
            while j0 < nch:
                n = min(gmax, nch - j0)
                pt = nps()
                for j in range(n):
                    off = (j // per_bank) * 512 + (j % per_bank) * NTt
                    for k in range(KC):
                        pe(lambda e, pt=pt, off=off, view=view, k=k, jj=j0 + j: e.matmul(
                            pt.h[:, off:off + NTt], lhsT=view[:, k, jj * 128:(jj + 1) * 128], rhs=rhs_ap(k),
                            start=(k == 0), stop=(k == KC - 1)), reads=[slot, rhs_t], writes=[pt])
                if per_bank * NTt == 512 or n <= per_bank:
                    ap3 = pt.h[:, 0:n * NTt].rearrange("p (j t) -> p j t", j=n)
                else:
                    raise NotImplementedError
                called = [False]

                def mid():
                    called[0] = True
                    if pend["b1"] is not None:
                        pend["b1"]()
                    if pend["b2"] is not None:
                        pend["b2"]()
                if piped:
                    d_ = evac(oc_base + (c0 - c_lo) // 128 + j0, n, pt, ap3, mid)
                else:
                    d_ = evac(oc_base + (c0 - c_lo) // 128 + j0, n, pt, ap3)
                if not called[0]:
                    mid()
                pend["b2"] = pend["b2n"]
                pend["b1"], pend["b2n"] = d_ if d_ is not None else (None, None)
                j0 += n
        for k_ in ("b1", "b2"):
            if pend[k_] is not None:
                pend[k_]()
        if pend["b2n"] is not None:
            pend["b2n"]()

    def rmsnorm_to_hT(NTt):
        act(lambda e: e.activation(out=hT[:, :, 0:NTt], in_=xT[:, :, 0:NTt], func=AF.Square), reads=[xT], writes=[hT])
        pt = nps()
        for k in range(8):
            pe(lambda e, k=k, pt=pt: e.matmul(pt.h[:, 0:NTt], lhsT=onesB[:], rhs=hT[:, k, 0:NTt], start=(k == 0), stop=(k == 7)),
               reads=[onesB, hT], writes=[pt])
        act(lambda e, pt=pt: e.activation(out=cacc[:, 0, 0:NTt], in_=pt.h[:, 0:NTt], func=AF.Ln, scale=1.0 / D, bias=EPS), reads=[pt], writes=[cacc])
        act(lambda e: e.activation(out=cacc[:, 0, 0:NTt], in_=cacc[:, 0, 0:NTt], func=AF.Exp, scale=-0.5), reads=[cacc], writes=[cacc])
        dve(lambda e: e.tensor_tensor(out=hT[:, :, 0:NTt], in0=xT[:, :, 0:NTt], in1=cacc[:, 0, 0:NTt].unsqueeze(1).to_broadcast([128, 8, NTt]),
                                      op=ALU.mult), reads=[xT, cacc], writes=[hT])

    def resid_evac(NTt):
        def ev(oc0, n, pt, ap3):
            dve(lambda e: e.tensor_tensor(out=xT[:, oc0:oc0 + n, 0:NTt], in0=ap3, in1=xT[:, oc0:oc0 + n, 0:NTt], op=ALU.add),
                reads=[pt, xT], writes=[xT])
        return ev

    def ffn(layer, NTt):
        rmsnorm_to_hT(NTt)
        hrhs = lambda k: hT[:, k, 0:NTt]

        def ev_gate(oc0, n, pt, ap3):
            act(lambda e: e.activation(out=A1[:, oc0:oc0 + n, 0:NTt], in_=ap3, func=AF.Silu), reads=[pt], writes=[(A1, c) for c in range(oc0, oc0 + n)])

        def ev_up(oc0, n, pt, ap3):
            dve(lambda e: e.tensor_tensor(out=A1[:, oc0:oc0 + n, 0:NTt], in0=ap3, in1=A1[:, oc0:oc0 + n, 0:NTt], op=ALU.mult),
                reads=[pt] + [(A1, c) for c in range(oc0, oc0 + n)], writes=[(A1, c) for c in range(oc0, oc0 + n)])
        linear_fm("w_gate%d" % layer, 0, DFF, 8, hT, hrhs, NTt, ev_gate)
        linear_fm("w_up%d" % layer, 0, DFF, 8, hT, hrhs, NTt, ev_up)
        linear_fm("w_down%d" % layer, 0, D, 22, A1, lambda k: A1[:, k, 0:NTt], NTt, resid_evac(NTt))

    XW = NTMAX + 12
    convsets = [
        (xc.h[:, :, :], xc, cacc.h[:, :, :], cacc, vTg.h[:, :, :], vTg),
        (vtk.h[:, 0, 0:4 * XW].rearrange("p (j w) -> p j w", j=4), (vtk, 0),
         vnb.h[:, :].bitcast(F32)[:, 0:4 * NTMAX].rearrange("p (j t) -> p j t", j=4), vnb,
         vtk.h[:, 0, 4 * XW + 16:4 * XW + 16 + 2 * NTMAX].bitcast(BF16).rearrange("p (j t) -> p j t", j=4), (vtk, 0)),
    ]
    convc = [0]

    def run_tile(kind, tok0, NTt, first, last, seqs, mode="full", apply_flag=False, need_q_halo=False):
        full = mode == "full"
        nb = NTt // 128
        src = (xp if full else xq) if kind == 0 else xs
        dst = yp if kind == 0 else ys
        nseg = 1 if kind == 0 else NTt // 64
        L = NTt // nseg
        if apply_flag:
            dve(lambda e: e.tensor_scalar(out=Sf[0][:], in0=Sf[0][:], scalar1=flg[:, 0:1], scalar2=None, op0=ALU.mult), reads=[Sf[0], flg], writes=[Sf[0]])
            act(lambda e: e.copy(out=Sb[0][:], in_=Sf[0][:]), reads=[Sf[0]], writes=[Sb[0]])
            dve(lambda e: e.tensor_scalar(out=hal[:], in0=hal[:], scalar1=flg[:, 0:1], scalar2=None, op0=ALU.mult), reads=[hal, flg], writes=[hal])
        for b in range(nb):
            xb = b % 2
            P.dma(vtk[:, xb, 0:1024], src.h.ap()[tok0 + b * 128: tok0 + (b + 1) * 128, :], reads=[src], writes=[(vtk, xb)])
            for hf in range(2):
                pt = nps()
                for j in range(4):
                    k = hf * 4 + j
                    pe(lambda e, pt=pt, j=j, k=k, xb=xb: e.transpose(pt.h[:, j * 128:(j + 1) * 128], vtk[:, xb, k * 128:(k + 1) * 128], identF[:]),
                       reads=[(vtk, xb), identF], writes=[pt])
                act(lambda e, pt=pt, hf=hf, b=b: e.copy(out=xT[:, hf * 4:(hf + 1) * 4, b * 128:(b + 1) * 128],
                                                        in_=pt.h[:, 0:512].rearrange("p (j t) -> p j t", j=4)), reads=[pt], writes=[xT])
        rmsnorm_to_hT(NTt)
        hrhs = lambda k: hT[:, k, 0:NTt]

        def ev_u(oc0, n, pt, ap3):
            act(lambda e: e.activation(out=A1[:, oc0:oc0 + n, 0:NTt], in_=ap3, func=AF.Gelu), reads=[pt], writes=[(A1, c) for c in range(oc0, oc0 + n)])
        for slot, view, c0, pc in wpieces("w_sgu_in", 2048, 4096, 8):
            for b in range(nb):
                for q in range(pc // 512):
                    pt = nps()
                    for k in range(8):
                        pe(lambda e, pt=pt, k=k, b=b, q=q, view=view: e.matmul(pt.h[:, 0:512], lhsT=hT[:, k, b * 128:(b + 1) * 128],
                                                                               rhs=view[:, k, q * 512:(q + 1) * 512], start=(k == 0), stop=(k == 7)),
                           reads=[slot, hT], writes=[pt])
                    cc = c0 - 2048 + q * 512
                    act(lambda e, pt=pt, b=b, cc=cc: e.activation(out=vtk[:, b, cc:cc + 512], in_=pt.h[:, 0:512], func=AF.Gelu),
                        reads=[pt], writes=[(vtk, b)])
        for b in range(nb):
            for q in range(4):
                dve(lambda e, b=b, q=q: e.bn_stats(out=bst[:, q, :], in_=vtk[:, b, q * 512:(q + 1) * 512]), reads=[(vtk, b)], writes=[bst])
            dve(lambda e: e.bn_aggr(out=bmv[:, 0:2], in_=bst[:].rearrange("p q s -> p (q s)")), reads=[bst], writes=[bmv])
            act(lambda e: e.activation(out=bmv[:, 2:3], in_=bmv[:, 1:2], func=AF.Ln, bias=1e-5), reads=[bmv], writes=[bmv])
            act(lambda e: e.activation(out=bmv[:, 2:3], in_=bmv[:, 2:3], func=AF.Exp, scale=-0.5), reads=[bmv], writes=[bmv])
            dve(lambda e, b=b: e.scalar_tensor_tensor(out=vtk[:, b, :], in0=vtk[:, b, :], scalar=bmv[:, 0:1], in1=grep[:],
                                                      op0=ALU.subtract, op1=ALU.mult), reads=[(vtk, b), bmv, grep], writes=[(vtk, b)])
            dve(lambda e, b=b: e.scalar_tensor_tensor(out=vtk[:, b, :], in0=vtk[:, b, :], scalar=bmv[:, 2:3], in1=brep[:],
                                                      op0=ALU.mult, op1=ALU.add), reads=[(vtk, b), bmv, brep], writes=[(vtk, b)])
            if kind == 1:
                P.dma(svs.h.ap()[tok0 + b * 128: tok0 + (b + 1) * 128, :], vtk[:, b, :], reads=[(vtk, b)], writes=[svs])
        linear_fm("w_sgu_in", 0, 2048, 8, hT, hrhs, NTt, ev_u)
        for b in range(nb):
            act(lambda e, b=b: e.copy(out=vnb[:], in_=vtk[:, b, :]), reads=[(vtk, b)], writes=[vnb])
            for c4 in range(4):
                pt = nps()
                brow = bsrow[kind][0:1, c4 * 256:(c4 + 1) * 256].rearrange("o (g t) -> o g t", g=2).unsqueeze(2).to_broadcast([1, 2, 2, 128])
                pe(lambda e, pt=pt, brow=brow: e.matmul(pt.h[:, 0:512], lhsT=onesF[0:1, :], rhs=brow, start=True, stop=False),
                   reads=[onesF, bsrow[kind]], writes=[pt])
                for j in range(4):
                    ci = c4 * 4 + j
                    g = ci // 2
                    pe(lambda e, pt=pt, j=j, ci=ci, g=g: e.matmul(pt.h[:, j * 128:(j + 1) * 128], lhsT=vnb[:, ci * 128:(ci + 1) * 128],
                                                                 rhs=wsT[kind][:, g, :], start=False, stop=(j == 3)), reads=[vnb, wsT[kind]], writes=[pt])
                dve(lambda e, pt=pt, c4=c4, b=b: e.tensor_tensor(out=A1[:, c4 * 4:(c4 + 1) * 4, b * 128:(b + 1) * 128],
                                                                 in0=pt.h[:, 0:512].rearrange("p (j t) -> p j t", j=4),
                                                                 in1=A1[:, c4 * 4:(c4 + 1) * 4, b * 128:(b + 1) * 128], op=ALU.mult),
                    reads=[pt] + [(A1, c) for c in range(c4 * 4, c4 * 4 + 4)], writes=[(A1, c) for c in range(c4 * 4, c4 * 4 + 4)])
        linear_fm("w_sgu_out", 0, D, 16, A1, lambda k: A1[:, k, 0:NTt], NTt, resid_evac(NTt))
        ffn(0, NTt)
        rmsnorm_to_hT(NTt)
        if kind == 1:
            for c4 in range(8):
                P.dma(vtk[0:nseg * 3, 1, 0:512], sc.h.ap().rearrange("s i c -> (s i) c")[seqs[0] * 3:(seqs[0] + nseg) * 3, c4 * 512:(c4 + 1) * 512],
                      reads=[sc], writes=[(vtk, 1)])
                pt = nps()
                for j in range(4):
                    pe(lambda e, pt=pt, j=j: e.transpose(pt.h[:, j * 16:j * 16 + nseg * 3], vtk[0:nseg * 3, 1, j * 128:(j + 1) * 128],
                                                       identF[0:nseg * 3, 0:nseg * 3]), reads=[(vtk, 1), identF], writes=[pt])
                act(lambda e, pt=pt, c4=c4: e.copy(out=hal[:, c4 * 4:(c4 + 1) * 4, 0:nseg, :],
                                                   in_=pt.h[:, 0:64].rearrange("p (j x) -> p j x", j=4)[:, :, 0:nseg * 3].rearrange("p j (s i) -> p j s i", i=3)),
                    reads=[pt], writes=[hal])
        W = L + 3

        def ev_qkv(oc0, n, pt, ap3, mid):
            XC, XCK, CA, CAK, VT, VTK = convsets[convc[0] % 2]
            convc[0] += 1
            xv = XC[:, 0:n, 0:nseg * W].rearrange("p j (s w) -> p j s w", s=nseg)
            act(lambda e: e.copy(out=xv[:, :, :, 3:3 + L], in_=ap3.rearrange("p j (s l) -> p j s l", s=nseg)), reads=[pt], writes=[XCK])
            pool(lambda e: e.tensor_copy(out=xv[:, :, :, 0:3], in_=hal[:, oc0:oc0 + n, 0:nseg, :]), reads=[hal], writes=[XCK])
            pool(lambda e: e.tensor_copy(out=hal[:, oc0:oc0 + n, 0:nseg, :], in_=xv[:, :, :, L:L + 3]), reads=[XCK], writes=[hal])
            mid()
            avs = [CA[:, j, 0:NTt].rearrange("p (s l) -> p s l", s=nseg) for j in range(n)]
            CK = [(CAK if isinstance(CAK, tuple) else (CAK, None)) for j in range(n)]
            for j in range(n):
                c = oc0 + j
                dve(lambda e, j=j, c=c: e.tensor_scalar(out=avs[j], in0=xv[:, j, :, 0:L], scalar1=cw[:, 0, c:c + 1], scalar2=None, op0=ALU.mult),
                    reads=[XCK, cw], writes=[(CK[j][0], ("c", j))])
            for i in range(1, 4):
                for j in range(n):
                    c = oc0 + j
                    dve(lambda e, j=j, c=c, i=i: e.scalar_tensor_tensor(out=avs[j], in0=xv[:, j, :, i:i + L], scalar=cw[:, i, c:c + 1], in1=avs[j],
                                                                      op0=ALU.mult, op1=ALU.add), reads=[XCK, cw, (CK[j][0], ("c", j))], writes=[(CK[j][0], ("c", j))])
            KQK = [(kqT, c) for c in range(oc0, oc0 + n)]
            if oc0 < 16:
                act(lambda e: e.activation(out=kqT[:, oc0:oc0 + n, 0:NTt], in_=CA[:, 0:n, 0:NTt], func=AF.Silu), reads=[CAK], writes=KQK)
                act(lambda e: e.activation(out=VT[:, 0:n, 0:NTt], in_=kqT[:, oc0:oc0 + n, 0:NTt], func=AF.Square), reads=KQK, writes=[VTK])

                def partB1():
                    p2 = nps()
                    per_bank = max(1, 512 // NTt)
                    for j in range(n):
                        off = (j // per_bank) * 512 + (j % per_bank) * NTt
                        pe(lambda e, j=j, off=off, p2=p2: e.matmul(p2.h[:, off:off + NTt], lhsT=onesB[:], rhs=VT[:, j, 0:NTt], start=True, stop=True),
                           reads=[onesB, VTK], writes=[p2])
                    a3 = p2.h[:, 0:n * NTt].rearrange("p (j t) -> p j t", j=n)
                    act(lambda e: e.activation(out=CA[:, 0:n, 0:NTt], in_=a3, func=AF.Ln, bias=EPS), reads=[p2], writes=[CAK])
                    qb = -0.5 * float(np.log(128.0)) if oc0 < 8 else 0.0
                    act(lambda e: e.activation(out=CA[:, 0:n, 0:NTt], in_=CA[:, 0:n, 0:NTt], func=AF.Exp, scale=-0.5, bias=qb), reads=[CAK], writes=[CAK])

                def partB2():
                    dve(lambda e: e.tensor_tensor(out=kqT[:, oc0:oc0 + n, 0:NTt], in0=kqT[:, oc0:oc0 + n, 0:NTt], in1=CA[:, 0:n, 0:NTt], op=ALU.mult),
                        reads=KQK + [CAK], writes=KQK)
                    if oc0 >= 8:
                        for b in range(nb):
                            p3 = nps()
                            for j in range(n):
                                pe(lambda e, j=j, b=b, p3=p3: e.transpose(pbf(p3)[:, j * 128:(j + 1) * 128], kqT[:, oc0 + j, b * 128:(b + 1) * 128], identB[:]),
                                   reads=KQK + [identB], writes=[p3])
                            act(lambda e, b=b, p3=p3: e.copy(out=ktok[:, b, (oc0 - 8) * 128:(oc0 - 8 + n) * 128], in_=pbf(p3)[:, 0:n * 128]),
                                reads=[p3], writes=[(ktok, b)])
                return (partB1, partB2)
            else:
                act(lambda e: e.activation(out=VT[:, 0:n, 0:NTt], in_=CA[:, 0:n, 0:NTt], func=AF.Silu), reads=[CAK], writes=[VTK])

                def partB2():
                    for b in range(nb):
                        p3 = nps()
                        for j in range(n):
                            pe(lambda e, j=j, b=b, p3=p3: e.transpose(pbf(p3)[:, j * 128:(j + 1) * 128], VT[:, j, b * 128:(b + 1) * 128], identB[:]),
                               reads=[VTK, identB], writes=[p3])
                        act(lambda e, b=b, p3=p3: e.copy(out=vtok[:, b, (oc0 - 16) * 128:(oc0 - 16 + n) * 128], in_=pbf(p3)[:, 0:n * 128]),
                            reads=[p3], writes=[(vtok, b)])
                return (None, partB2)
        if full or need_q_halo:
            linear_fm("w_gdn_in", 0, 4096, 8, hT, hrhs, NTt, ev_qkv, piped=True)
        else:
            linear_fm("w_gdn_in", 1024, 4096, 8, hT, hrhs, NTt, ev_qkv, oc_base=8, piped=True)
        if (kind == 1 or last) and full:
            odst = scs if kind == 1 else scp
            orow = odst.h.ap().rearrange("s i c -> (s i) c") if kind == 1 else odst.h.ap()
            r0 = seqs[0] * 3 if kind == 1 else 0
            for c4 in range(8):
                pt = nps()
                for j in range(4):
                    pe(lambda e, pt=pt, j=j, c4=c4: e.transpose(pt.h[0:nseg * 3, j * 128:(j + 1) * 128],
                                                              hal[:, c4 * 4 + j, 0:nseg, :].rearrange("p s i -> p (s i)"), identF[:]),
                       reads=[hal, identF], writes=[pt])
                act(lambda e, pt=pt: e.copy(out=vtk[0:nseg * 3, 1, 0:512], in_=pt.h[0:nseg * 3, 0:512]), reads=[pt], writes=[(vtk, 1)])
                P.dma(orow[r0:r0 + nseg * 3, c4 * 512:(c4 + 1) * 512], vtk[0:nseg * 3, 1, 0:512], reads=[(vtk, 1)], writes=[odst])

        gslot, gview, _, _ = next(wpieces("w_gdn_in", 6144, 6176, 8))
        if kind == 0 and first and not apply_flag:
            pool(lambda e: e.memset(Sf[0][:], 0.0), writes=[Sf[0]])
            pool(lambda e: e.memset(Sb[0][:], 0.0), writes=[Sb[0]])
        for b in range(nb):
            pg = nps()
            for k in range(8):
                pe(lambda e, k=k, b=b, pg=pg: e.matmul(pg.h[:, 0:32], lhsT=hT[:, k, b * 128:(b + 1) * 128], rhs=gview[:, k, 0:32],
                                                       start=(k == 0), stop=(k == 7)), reads=[hT, gslot], writes=[pg])
            act(lambda e, pg=pg: e.copy(out=gts_[b][:], in_=pg.h[:, 0:32]), reads=[pg], writes=[gts_[b]])
            G = lambda i, b=b: gsm_[b][:, i, :]
            act(lambda e: e.activation(out=G(0), in_=gts_[b][:, 0:16], func=AF.Exp, scale=-1.0), reads=[gts_[b]], writes=[gsm_[b]])
            dve(lambda e: e.tensor_scalar(out=G(0), in0=G(0), scalar1=1.0, scalar2=None, op0=ALU.add), reads=[gsm_[b]], writes=[gsm_[b]])
            dve(lambda e: e.reciprocal(out=G(1), in_=G(0)), reads=[gsm_[b]], writes=[gsm_[b]])
            dve(lambda e: e.tensor_scalar(out=G(2), in0=G(1), scalar1=-1.0, scalar2=None, op0=ALU.mult), reads=[gsm_[b]], writes=[gsm_[b]])
            dve(lambda e: e.tensor_tensor(out=G(9), in0=gts_[b][:, 16:32], in1=dtr[:], op=ALU.add), reads=[gts_[b], dtr], writes=[gsm_[b]])
            act(lambda e: e.activation(out=G(9), in_=G(9), func=AF.Exp), reads=[gsm_[b]], writes=[gsm_[b]])
            act(lambda e: e.activation(out=G(9), in_=G(9), func=AF.Ln, bias=1.0), reads=[gsm_[b]], writes=[gsm_[b]])
            dve(lambda e: e.tensor_tensor(out=G(3), in0=G(9), in1=alr[:], op=ALU.mult), reads=[gsm_[b], alr], writes=[gsm_[b]])
            pc_ = nps()
            pe(lambda e, pc_=pc_: e.matmul(pc_.h[:, 0:16], lhsT=incl[:], rhs=G(3), start=True, stop=True), reads=[incl, gsm_[b]], writes=[pc_])
            pe(lambda e, pc_=pc_: e.matmul(pc_.h[:, 16:32], lhsT=blk1[:], rhs=G(3), start=True, stop=True), reads=[blk1, gsm_[b]], writes=[pc_])
            pe(lambda e, pc_=pc_: e.matmul(pc_.h[:, 32:48], lhsT=selA[:], rhs=G(3), start=True, stop=True), reads=[selA, gsm_[b]], writes=[pc_])
            pe(lambda e, pc_=pc_: e.matmul(pc_.h[:, 48:64], lhsT=selB[:], rhs=G(3), start=True, stop=True), reads=[selB, gsm_[b]], writes=[pc_])
            act(lambda e, pc_=pc_: e.copy(out=G(4), in_=pc_.h[:, 0:16]), reads=[pc_], writes=[gsm_[b]])
            act(lambda e, pc_=pc_: e.activation(out=G(5), in_=pc_.h[:, 0:16], func=AF.Exp), reads=[pc_], writes=[gsm_[b]])
            dve(lambda e, pc_=pc_: e.tensor_tensor(out=G(6), in0=pc_.h[:, 16:32], in1=G(4), op=ALU.subtract), reads=[pc_, gsm_[b]], writes=[gsm_[b]])
            act(lambda e: e.activation(out=G(6), in_=G(6), func=AF.Exp), reads=[gsm_[b]], writes=[gsm_[b]])
            act(lambda e, pc_=pc_: e.activation(out=G(7), in_=pc_.h[:, 32:48], func=AF.Exp), reads=[pc_], writes=[gsm_[b]])
            act(lambda e, pc_=pc_: e.activation(out=G(8), in_=pc_.h[:, 48:64], func=AF.Exp), reads=[pc_], writes=[gsm_[b]])
            dbg("gts", gts_[b], gts_[b][:], [128, 32])
            dbg("gsm", gsm_[b], gsm_[b][:], [128, 12, 16])
        def ev_z(oc0, n, pt, ap3):
            act(lambda e: e.activation(out=A1[:, oc0:oc0 + n, 0:NTt], in_=ap3, func=AF.Silu), reads=[pt], writes=[(A1, c) for c in range(oc0, oc0 + n)])
        if full:
            linear_fm("w_gdn_in", 4096, 6144, 8, hT, hrhs, NTt, ev_z)
        for b in range(nb):
            gsm = gsm_[b]
            G = lambda i, b=b: gsm_[b][:, i, :]
            if kind == 1:
                for ch in range(2):
                    sq_ = seqs[0] + b * 2 + ch
                    P.dma(Sf[ch][:], sg.h.ap()[sq_].rearrange("h k v -> k h v"), reads=[sg], writes=[Sf[ch]])
                    act(lambda e, ch=ch: e.copy(out=Sb[ch][:], in_=Sf[ch][:]), reads=[Sf[ch]], writes=[Sb[ch]])
            bs_ = slice(b * 128, (b + 1) * 128)
            NG = NGROUPS
            HG = 8 // NG
            GW = HG * 128
            grp = list(range(NG))
            for hf in range(2):
                h0 = hf * 8
                hs = [slice(g * HG, (g + 1) * HG) for g in grp]
                hgl = [slice(h0 + g * HG, h0 + (g + 1) * HG) for g in grp]
                K_ = lambda t, g: (t, ("g", g))
                X032, XT32, X132 = Gtri, E1, Es
                fl = lambda t, g: t[:, hs[g], :].rearrange("p h t -> p (h t)")
                prt_, pgr_, pkq_ = {}, {}, {}
                for g in grp:
                    dve(lambda e, g=g: e.tensor_tensor(out=Gtri[:, hs[g], :], in0=G(3)[:, hgl[g]].unsqueeze(2).to_broadcast([128, HG, 128]),
                                                       in1=incl[:].unsqueeze(1).to_broadcast([128, HG, 128]), op=ALU.mult),
                        reads=[gsm, incl], writes=[K_(Gtri, g)])
                for g in grp:
                    prt = nps1()
                    prt_[g] = prt
                    pe(lambda e, g=g, prt=prt: e.matmul(prt.ap(0, GW), lhsT=onesF[:], rhs=fl(Gtri, g), start=True, stop=False),
                       reads=[onesF, K_(Gtri, g)], writes=[prt.key])
                    pe(lambda e, g=g, prt=prt: e.matmul(prt.ap(0, GW), lhsT=identB[:], rhs=negm[:].unsqueeze(1).to_broadcast([128, HG, 128]),
                                                        start=False, stop=True), reads=[identB, negm], writes=[prt.key])
                    if full:
                        pgr = nps1()
                        pgr_[g] = pgr
                        pe(lambda e, g=g, pgr=pgr: e.matmul(pgr.ap(0, GW), lhsT=onesF[:], rhs=fl(Gtri, g), start=True, stop=True),
                           reads=[onesF, K_(Gtri, g)], writes=[pgr.key])
                v3 = lambda ph: ph.ap(0, GW).rearrange("p (h t) -> p h t", h=HG)
                for g in grp:
                    dve(lambda e, g=g: e.tensor_tensor(out=E1[:, hs[g], :], in0=v3(prt_[g]), in1=G(4)[:, hgl[g]].unsqueeze(2).to_broadcast([128, HG, 128]),
                                                       op=ALU.subtract), reads=[prt_[g].key, gsm], writes=[K_(E1, g)])
                for g in grp:
                    act(lambda e, g=g: e.activation(out=E1[:, hs[g], :], in_=E1[:, hs[g], :], func=AF.Exp), reads=[K_(E1, g)], writes=[K_(E1, g)])
                for g in grp:
                    pool(lambda e, g=g: e.tensor_tensor(out=Es[:, hs[g], :], in0=E1[:, hs[g], :], in1=strict[:].unsqueeze(1).to_broadcast([128, HG, 128]), op=ALU.mult),
                         reads=[K_(E1, g), strict], writes=[K_(Es, g)])
                for g in grp:
                    pkq = nps1()
                    pkq_[g] = pkq
                    for j in range(HG // 2):
                        hk = hf * 4 + g * (HG // 2) + j
                        if full:
                            pe(lambda e, j=j, hk=hk, pkq=pkq: e.matmul(pkq.ap(j * 256, (j + 1) * 256), lhsT=kqT[:, 8 + hk, bs_], rhs=kqT[:, hk:hk + 9:8, bs_],
                                                                       start=True, stop=True), reads=[kqT], writes=[pkq.key])
                        else:
                            pe(lambda e, j=j, hk=hk, pkq=pkq: e.matmul(pkq.ap(j * 256 + 128, (j + 1) * 256), lhsT=kqT[:, 8 + hk, bs_], rhs=kqT[:, 8 + hk, bs_],
                                                                       start=True, stop=True), reads=[(kqT, 8 + hk)], writes=[pkq.key])
                kq4 = lambda ph: ph.ap(0, GW).rearrange("p (j c t) -> p j c t", j=HG // 2, c=2)
                if full:
                    for g in grp:
                        dve(lambda e, g=g: e.tensor_tensor(out=PT[:, hs[g], :].rearrange("p (j r) t -> p j r t", r=2),
                                                           in0=kq4(pkq_[g])[:, :, 0:1, :].to_broadcast([128, HG // 2, 2, 128]),
                                                           in1=E1[:, hs[g], :].rearrange("p (j r) t -> p j r t", r=2), op=ALU.mult),
                            reads=[pkq_[g].key, K_(E1, g)], writes=[K_(PT, g)])
                for g in grp:
                    for hh in range(HG):
                        hw = g * HG + hh
                        dve(lambda e, g=g, hh=hh, hw=hw: e.scalar_tensor_tensor(out=X032[:, hw, :], in0=kq4(pkq_[g])[:, hh // 2, 1, :], scalar=G(2)[:, h0 + hw:h0 + hw + 1],
                                                                               in1=Es[:, hw, :], op0=ALU.mult, op1=ALU.mult),
                            reads=[pkq_[g].key, gsm, K_(Es, g)], writes=[(X032, ("g", g, hh))])
                if full:
                    for g in grp:
                        act(lambda e, g=g: e.activation(out=Es[:, hs[g], :], in_=v3(pgr_[g]), func=AF.Exp), reads=[pgr_[g].key], writes=[K_(Es, g)])
                    for g in grp:
                        q4 = kqT[:, hf * 4 + g * (HG // 2):hf * 4 + (g + 1) * (HG // 2), bs_].unsqueeze(2).to_broadcast([128, HG // 2, 2, 128])
                        dve(lambda e, g=g, q4=q4: e.tensor_tensor(out=qdT[:, hs[g], :].rearrange("p (j r) t -> p j r t", r=2), in0=q4,
                                                                  in1=Es[:, hs[g], :].rearrange("p (j r) t -> p j r t", r=2), op=ALU.mult),
                            reads=[kqT, K_(Es, g)], writes=[K_(qdT, g)])
                for g in grp:
                    pxt = nps1()
                    for hh in range(HG):
                        pe(lambda e, g=g, hh=hh, pxt=pxt: e.transpose(pxt.ap(hh * 128, (hh + 1) * 128), X032[:, g * HG + hh, :], identF[:]),
                           reads=[K_(X032, g), identF], writes=[pxt.key])
                    act(lambda e, g=g, pxt=pxt: e.copy(out=fl(XT32, g), in_=pxt.ap(0, GW)), reads=[pxt.key], writes=[K_(XT32, g)])
                for g in grp:
                    dve(lambda e, g=g: e.tensor_tensor(out=P32[:, hs[g], :], in0=X032[:, hs[g], :], in1=identF[:].unsqueeze(1).to_broadcast([128, HG, 128]), op=ALU.add),
                        reads=[K_(X032, g), identF], writes=[K_(P32, g)])
                for g in grp:
                    px = nps1()
                    for hh in range(HG):
                        hw = g * HG + hh
                        pe(lambda e, hh=hh, hw=hw, px=px: e.matmul(px.ap(hh * 128, (hh + 1) * 128), lhsT=XT32[:, hw, :], rhs=X032[:, hw, :], start=True, stop=True),
                           reads=[K_(XT32, g), K_(X032, g)], writes=[px.key])
                    act(lambda e, g=g, px=px: e.copy(out=fl(X132, g), in_=px.ap(0, GW)), reads=[px.key], writes=[K_(X132, g)])
                    act(lambda e, g=g: e.copy(out=Xa[1][:, hs[g], :], in_=X132[:, hs[g], :]), reads=[K_(X132, g)], writes=[K_(Xa[1], g)])
                for g in grp:
                    pxT = nps1()
                    for hh in range(HG):
                        hw = g * HG + hh
                        pe(lambda e, hh=hh, hw=hw, pxT=pxT: e.matmul(pxT.ap(hh * 128, (hh + 1) * 128), lhsT=X032[:, hw, :], rhs=XT32[:, hw, :], start=True, stop=True),
                           reads=[K_(XT32, g), K_(X032, g)], writes=[pxT.key])
                    dve(lambda e, g=g, pxT=pxT: e.tensor_copy(out=fl(Xt[1], g), in_=pxT.ap(0, GW)), reads=[pxT.key], writes=[K_(Xt[1], g)])
                for g in grp:
                    ppm = nps1()
                    for hh in range(HG):
                        hw = g * HG + hh
                        pe(lambda e, hh=hh, hw=hw, ppm=ppm: e.matmul(ppm.ap(hh * 128, (hh + 1) * 128), lhsT=XT32[:, hw, :], rhs=X132[:, hw, :], start=True, stop=True),
                           reads=[K_(XT32, g), K_(X132, g)], writes=[ppm.key])
                    dve(lambda e, g=g: e.tensor_tensor(out=P32[:, hs[g], :], in0=P32[:, hs[g], :], in1=X132[:, hs[g], :], op=ALU.add),
                        reads=[K_(P32, g), K_(X132, g)], writes=[K_(P32, g)])
                    dve(lambda e, g=g, ppm=ppm: e.tensor_tensor(out=fl(P32, g), in0=ppm.ap(0, GW), in1=fl(P32, g), op=ALU.add),
                        reads=[ppm.key, K_(P32, g)], writes=[K_(P32, g)])
                    act(lambda e, g=g: e.copy(out=Pm[1][:, hs[g], :], in_=P32[:, hs[g], :]), reads=[K_(P32, g)], writes=[K_(Pm[1], g)])
                cur = 1
                for lev in range(2, 6):
                    nxt = 1 - cur
                    if lev < 5:
                        for g in grp:
                            px = nps1()
                            for hh in range(HG):
                                hw = g * HG + hh
                                pe(lambda e, hh=hh, hw=hw, px=px, cur=cur: e.matmul(px.ap(hh * 128, (hh + 1) * 128), lhsT=Xt[cur][:, hw, :], rhs=Xa[cur][:, hw, :],
                                                                                    start=True, stop=True), reads=[K_(Xt[cur], g), K_(Xa[cur], g)], writes=[px.key])
                            act(lambda e, g=g, px=px, nxt=nxt: e.copy(out=fl(Xa[nxt], g), in_=px.ap(0, GW)), reads=[px.key], writes=[K_(Xa[nxt], g)])
                    pxTs = {}
                    for g in grp:
                        pxT = nps1()
                        for hh in range(HG):
                            hw = g * HG + hh
                            pe(lambda e, hh=hh, hw=hw, pxT=pxT, cur=cur: e.matmul(pxT.ap(hh * 128, (hh + 1) * 128), lhsT=Xa[cur][:, hw, :], rhs=Xt[cur][:, hw, :],
                                                                                  start=True, stop=True), reads=[K_(Xt[cur], g), K_(Xa[cur], g)], writes=[pxT.key])
                        dve(lambda e, g=g, pxT=pxT, nxt=nxt: e.tensor_copy(out=fl(Xt[nxt], g), in_=pxT.ap(0, GW)), reads=[pxT.key], writes=[K_(Xt[nxt], g)])
                    for g in grp:
                        ppm = nps1()
                        for hh in range(HG):
                            hw = g * HG + hh
                            pe(lambda e, hh=hh, hw=hw, ppm=ppm, nxt=nxt, cur=cur: e.matmul(ppm.ap(hh * 128, (hh + 1) * 128), lhsT=Xt[nxt][:, hw, :], rhs=Pm[cur][:, hw, :],
                                                                                           start=True, stop=True), reads=[K_(Xt[nxt], g), K_(Pm[cur], g)], writes=[ppm.key])
                        dve(lambda e, g=g, ppm=ppm: e.tensor_tensor(out=fl(P32, g), in0=ppm.ap(0, GW), in1=fl(P32, g), op=ALU.add),
                            reads=[ppm.key, K_(P32, g)], writes=[K_(P32, g)])
                        act(lambda e, g=g, nxt=nxt: e.copy(out=Pm[nxt][:, hs[g], :], in_=P32[:, hs[g], :]), reads=[K_(P32, g)], writes=[K_(Pm[nxt], g)])
                    cur = nxt
                Tt = Pm[cur]
                dbg("Tt", Tt, Tt[:], [128, 8, 128], BF16)
                for g in grp:
                    kc0 = hf * 512 + g * (HG // 2) * 128
                    k4 = ktok[:, b, kc0:kc0 + (HG // 2) * 128].rearrange("p (j d) -> p j d", j=HG // 2).unsqueeze(2).to_broadcast([128, HG // 2, 2, 128])
                    for dst_, gi in ((kg, 5), (kd, 6)):
                        pool(lambda e, g=g, k4=k4, dst_=dst_, gi=gi: e.tensor_tensor(
                            out=dst_[:, hs[g], :].rearrange("p (j r) d -> p j r d", r=2), in0=k4,
                            in1=G(gi)[:, hgl[g]].rearrange("p (j r) -> p j r", r=2).unsqueeze(3).to_broadcast([128, HG // 2, 2, 128]), op=ALU.mult),
                            reads=[(ktok, b), gsm], writes=[K_(dst_, g)])
                for g in grp:
                    pw = nps1()
                    for hh in range(HG):
                        hw = g * HG + hh
                        pe(lambda e, hh=hh, hw=hw, pw=pw, Tt=Tt: e.matmul(pw.ap(hh * 128, (hh + 1) * 128), lhsT=kg[:, hw, :], rhs=Tt[:, hw, :], start=True, stop=True),
                           reads=[K_(kg, g), K_(Tt, g)], writes=[pw.key])
                    act(lambda e, g=g, pw=pw: e.mul(out=fl(nWT, g), in_=pw.ap(0, GW), mul=-1.0), reads=[pw.key], writes=[K_(nWT, g)])
                for ch in range(2):
                    si = ch if kind == 1 else 0
                    r_ = slice(ch * 64, ch * 64 + 64)
                    SK = lambda t, g: (t, ("s", hf, g))
                    for g in grp:
                        pd = nps1()
                        for hh in range(HG):
                            hw = g * HG + hh
                            hg = h0 + hw
                            pe(lambda e, hh=hh, hw=hw, hg=hg, pd=pd, Tt=Tt, r_=r_: e.matmul(pd.ap(hh * 128, (hh + 1) * 128, r_), lhsT=Tt[r_, hw, r_], rhs=vtok[r_, b, hg * 128:(hg + 1) * 128],
                                                                                           start=True, stop=False), reads=[K_(Tt, g), (vtok, b)], writes=[pd.key])
                            pe(lambda e, hh=hh, hw=hw, hg=hg, pd=pd, r_=r_, si=si: e.matmul(pd.ap(hh * 128, (hh + 1) * 128, r_), lhsT=nWT[:, hw, r_], rhs=Sb[si][:, hg, :],
                                                                                           start=False, stop=True), reads=[K_(nWT, g), SK(Sb[si], g)], writes=[pd.key])
                        dve(lambda e, g=g, pd=pd, r_=r_: e.tensor_tensor(out=dlt[r_, hs[g], :], in0=pd.ap(0, GW, r_).rearrange("p (h d) -> p h d", h=HG),
                                                                         in1=G(1)[r_, hgl[g]].unsqueeze(2).to_broadcast([64, HG, 128]), op=ALU.mult),
                            reads=[pd.key, gsm], writes=[(dlt, (ch, g))])
                    pos = {}
                    if full:
                        for g in grp:
                            po = nps1()
                            pos[g] = po
                            for hh in range(HG):
                                hw = g * HG + hh
                                hg = h0 + hw
                                pe(lambda e, hh=hh, hw=hw, hg=hg, po=po, r_=r_, si=si: e.matmul(po.ap(hh * 128, (hh + 1) * 128, r_), lhsT=qdT[:, hw, r_], rhs=Sb[si][:, hg, :],
                                                                                               start=True, stop=False), reads=[K_(qdT, g), SK(Sb[si], g)], writes=[po.key])
                                pe(lambda e, hh=hh, hw=hw, po=po, r_=r_: e.matmul(po.ap(hh * 128, (hh + 1) * 128, r_), lhsT=PT[r_, hw, r_], rhs=dlt[r_, hw, :],
                                                                                 start=False, stop=True), reads=[K_(PT, g), (dlt, (ch, g))], writes=[po.key])
                    for g in grp:
                        psu = nps1()
                        for hh in range(HG):
                            hw = g * HG + hh
                            pe(lambda e, hh=hh, hw=hw, psu=psu, r_=r_: e.matmul(psu.ap(hh * 128, (hh + 1) * 128), lhsT=kd[r_, hw, :], rhs=dlt[r_, hw, :], start=True, stop=True),
                               reads=[K_(kd, g), (dlt, (ch, g))], writes=[psu.key])
                        for hh in range(HG):
                            hg = h0 + g * HG + hh
                            dve(lambda e, hh=hh, hg=hg, psu=psu, si=si, ch=ch: e.scalar_tensor_tensor(out=Sf[si][:, hg, :], in0=Sf[si][:, hg, :],
                                                                                                     scalar=G(7 + ch)[:, hg:hg + 1], in1=psu.ap(hh * 128, (hh + 1) * 128),
                                                                                                     op0=ALU.mult, op1=ALU.add),
                                reads=[psu.key, gsm, (Sf[si], ("s", hf, g, hh))], writes=[(Sf[si], ("s", hf, g, hh))])
                        act(lambda e, g=g, si=si: e.copy(out=Sb[si][:, hgl[g], :], in_=Sf[si][:, hgl[g], :]), reads=[SK(Sf[si], g)], writes=[SK(Sb[si], g)])
                    if full:
                        for g in grp:
                            po = pos[g]
                            o3 = po.ap(0, GW, r_).rearrange("p (h d) -> p h d", h=HG)
                            OK_ = (oss, (ch, g))
                            act(lambda e, g=g, o3=o3: e.activation(out=E1[r_, hs[g], :], in_=o3, func=AF.Square), reads=[po.key], writes=[(E1, ("o", ch, g))])
                            dve(lambda e, g=g: e.tensor_reduce(out=oss[r_, 0, hs[g]], in_=E1[r_, hs[g], :], axis=AX.X, op=ALU.add), reads=[(E1, ("o", ch, g))], writes=[OK_])
                            act(lambda e, g=g: e.activation(out=oss[r_, 1, hs[g]], in_=oss[r_, 0, hs[g]], func=AF.Ln, scale=1.0 / 128, bias=EPS), reads=[OK_], writes=[OK_])
                            act(lambda e, g=g: e.activation(out=oss[r_, 2, hs[g]], in_=oss[r_, 1, hs[g]], func=AF.Exp, scale=-0.5), reads=[OK_], writes=[OK_])
                            dve(lambda e, g=g, o3=o3: e.tensor_tensor(out=onb[r_, hs[g], :], in0=o3, in1=oss[r_, 2, hs[g]].unsqueeze(2).to_broadcast([64, HG, 128]),
                                                                      op=ALU.mult), reads=[po.key, OK_], writes=[(onb, (ch, g))])
                    if kind == 1 and hf == 1:
                        sq_ = seqs[0] + b * 2 + ch
                        P.dma(sgs.h.ap()[sq_].rearrange("h k v -> k h v"), Sf[si][:], reads=[Sf[si]], writes=[sgs])
                dbg("onb", onb, onb[:], [128, 8, 128], BF16)
                if full:
                    for g in grp:
                        pot = nps1()
                        for hh in range(HG):
                            pe(lambda e, g=g, hh=hh, pot=pot: e.transpose(pot.apb(hh * 128, (hh + 1) * 128), onb[:, g * HG + hh, :], identB[:]),
                               reads=[(onb, (0, g)), (onb, (1, g)), identB], writes=[pot.key])
                        a0 = h0 + g * HG
                        dve(lambda e, g=g, pot=pot, a0=a0: e.tensor_tensor(out=A1[:, a0:a0 + HG, bs_], in0=pot.apb(0, GW).rearrange("p (h t) -> p h t", h=HG),
                                                                           in1=A1[:, a0:a0 + HG, bs_], op=ALU.mult),
                            reads=[pot.key] + [(A1, a0 + i) for i in range(HG)], writes=[(A1, a0 + i) for i in range(HG)])
        if kind == 0 and last and full:
            P.dma(sgp.h.ap().rearrange("h k v -> k h v"), Sf[0][:], reads=[Sf[0]], writes=[sgp])
        if not full:
            return
        linear_fm("w_gdn_out", 0, D, 16, A1, lambda k: A1[:, k, 0:NTt], NTt, resid_evac(NTt))
        ffn(1, NTt)
        act(lambda e: e.activation(out=hT[:, :, 0:NTt], in_=xT[:, :, 0:NTt], func=AF.Square), reads=[xT], writes=[hT])
        pt = nps()
        for k in range(8):
            pe(lambda e, k=k, pt=pt: e.matmul(pt.h[:, 0:NTt], lhsT=onesB[:], rhs=hT[:, k, 0:NTt], start=(k == 0), stop=(k == 7)), reads=[onesB, hT], writes=[pt])
        act(lambda e, pt=pt: e.activation(out=cacc[:, 0, 0:NTt], in_=pt.h[:, 0:NTt], func=AF.Ln, scale=1.0 / D, bias=EPS), reads=[pt], writes=[cacc])
        act(lambda e: e.activation(out=cacc[:, 0, 0:NTt], in_=cacc[:, 0, 0:NTt], func=AF.Exp, scale=-0.5), reads=[cacc], writes=[cacc])
        for b in range(nb):
            prs = nps()
            pe(lambda e, b=b, prs=prs: e.transpose(prs.h[:, 0:128], cacc[:, 0, b * 128:(b + 1) * 128], identF[:]), reads=[cacc, identF], writes=[prs])
            act(lambda e, prs=prs: e.copy(out=bmv[:, 3:4], in_=prs.h[:, 0:1]), reads=[prs], writes=[bmv])
            for hf in range(2):
                pt = nps()
                for j in range(4):
                    k = hf * 4 + j
                    pe(lambda e, pt=pt, j=j, k=k, b=b: e.transpose(pt.h[:, j * 128:(j + 1) * 128], xT[:, k, b * 128:(b + 1) * 128], identF[:]),
                       reads=[xT, identF], writes=[pt])
                dve(lambda e, pt=pt, hf=hf: e.scalar_tensor_tensor(out=vtk[:, 1, hf * 512:(hf + 1) * 512], in0=pt.h[:, 0:512], scalar=bmv[:, 3:4],
                                                                   in1=wfin[:, hf * 512:(hf + 1) * 512], op0=ALU.mult, op1=ALU.mult),
                    reads=[pt, bmv, wfin], writes=[(vtk, 1)])
            P.dma(dst.h.ap()[tok0 + b * 128: tok0 + (b + 1) * 128, :], vtk[:, 1, 0:1024], reads=[(vtk, 1)], writes=[dst], eng="act")

    if n_ptiles:
        setup_sgu_kind(0)
    for ti in range(n_state):
        run_tile(0, ti * NT, NT, ti == 0, False, None, mode="state", need_q_halo=(ti == n_state - 1))
    for ti in range(n_ptiles):
        run_tile(0, ti * NT, NT, ti == 0, ti == n_ptiles - 1, None, apply_flag=(ti == 0 and n_state > 0))
    if NS:
        setup_sgu_kind(1)
        run_tile(1, 0, NSAMP, True, True, (0,))
    stats = P.emit()
    return nc, stats


WNAMES = ["norm_mix", "norm_ffn", "norm_final", "sgu_w_in", "sgu_ln_g", "sgu_ln_b", "sgu_w_s", "sgu_b_s", "sgu_w_out",
          "gdn_w_in", "gdn_w_conv", "gdn_a_log", "gdn_dt_bias", "gdn_w_onorm", "gdn_w_out", "ffn_w_gate", "ffn_w_up", "ffn_w_down"]
SQUEEZE0 = {"sgu_w_in", "sgu_ln_g", "sgu_ln_b", "sgu_w_s", "sgu_b_s", "sgu_w_out", "gdn_w_in", "gdn_w_conv", "gdn_a_log",
            "gdn_dt_bias", "gdn_w_onorm", "gdn_w_out"}

_CACHE = {}


def kernel(**inputs):
    x_prompt = np.ascontiguousarray(inputs["x_prompt"], dtype=np.float32)
    x_sample = np.ascontiguousarray(inputs["x_sample"], dtype=np.float32)
    state_gdn = np.ascontiguousarray(inputs["state_gdn"], dtype=np.float32)
    state_conv = np.ascontiguousarray(inputs["state_conv"], dtype=np.float32)
    B, SEQ, _ = x_prompt.shape
    DB, DS, _ = x_sample.shape
    n_cores = 8
    NT = 256
    NS = DB // n_cores
    HALF = SEQ // 2
    n_half = HALF // NT
    key = (n_half, NT, NS)
    if key not in _CACHE:
        _CACHE[key] = build_program(n_half, NT, NS, n_state=n_half)[0]
    nc = _CACHE[key]
    wd = {}
    for n in WNAMES:
        a = np.ascontiguousarray(inputs[n], dtype=np.float32)
        wd[n] = a[0] if n in SQUEEZE0 else a
    in_maps = []
    for c in range(n_cores):
        m = dict(wd)
        sq, second = c % B, c // B
        m["xq"] = x_prompt[sq, :HALF]
        m["xp"] = x_prompt[sq, HALF:] if second else x_prompt[sq, :HALF]
        m["flag"] = np.array([1.0 if second else 0.0], dtype=np.float32)
        m["xs"] = x_sample[c * NS:(c + 1) * NS].reshape(NS * DS, D)
        m["sg"] = state_gdn[0, c * NS:(c + 1) * NS]
        m["sc"] = state_conv[0, c * NS:(c + 1) * NS]
        in_maps.append(m)
    res = run_bass_kernel_spmd(nc, in_maps, core_ids=list(range(n_cores)))
    r = res.results
    y_prompt = np.stack([np.concatenate([r[c]["yp"], r[c + B]["yp"]], axis=0) for c in range(B)]).astype(np.float32)
    y_sample = np.concatenate([r[c]["ys"].reshape(NS, DS, D) for c in range(n_cores)]).astype(np.float32)
    ns_gdn_p = np.stack([r[c + B]["sgp"] for c in range(B)])[None].astype(np.float32)
    ns_conv_p = np.stack([r[c + B]["scp"] for c in range(B)])[None].astype(np.float32)
    ns_gdn_s = np.concatenate([r[c]["sgs"] for c in range(n_cores)])[None].astype(np.float32)
    ns_conv_s = np.concatenate([r[c]["scs"] for c in range(n_cores)])[None].astype(np.float32)
    ns_v_s = np.concatenate([r[c]["svs"].reshape(NS, DS, DSGU) for c in range(n_cores)])[None].astype(np.float32)
    return (y_prompt, y_sample, ns_gdn_p, ns_conv_p, ns_gdn_s, ns_conv_s, ns_v_s)
```

```python
import numpy as np
import concourse.bass as bass
import concourse.mybir as mybir
from concourse.bass_utils import run_bass_kernel_spmd

F32 = mybir.dt.float32
BF16 = mybir.dt.bfloat16
AF = mybir.ActivationFunctionType
ALU = mybir.AluOpType
AX = mybir.AxisListType

COMPUTE = ("pe", "act", "dve", "pool")


class T:
    __slots__ = ("h", "name", "last_w", "readers", "dma_sem", "kind")

    def __init__(self, h, name, kind="sb"):
        self.h = h
        self.name = name
        self.kind = kind
        self.last_w = {}
        self.readers = {}
        self.dma_sem = None

    def __getitem__(self, k):
        return self.h[k]


class Op:
    __slots__ = ("eng", "fn", "deps", "idx", "signal", "sigval", "is_dma", "dsem", "dval")

    def __init__(self, eng, fn):
        self.eng = eng
        self.fn = fn
        self.deps = []
        self.signal = False
        self.sigval = None
        self.is_dma = False
        self.dsem = None
        self.dval = None


class _Rec:
    def __getattr__(self, name):
        def f(*a, **k):
            self.call = (name, a, k)
            return self
        return f


def _replay(name, a, k):
    return lambda e: getattr(e, name)(*a, **k)


class Prog:
    def __init__(self, nc):
        self.nc = nc
        self.q = {e: [] for e in ("pe", "act", "dve", "pool", "sp")}
        self.tiles = []
        self._sems = []

    def sb(self, name, shape, dtype):
        t = T(self.nc.alloc_sbuf_tensor(name, list(shape), dtype), name)
        self.tiles.append(t)
        return t

    def ps(self, name, shape, dtype=F32):
        t = T(self.nc.alloc_psum_tensor(name, list(shape), dtype), name, "ps")
        self.tiles.append(t)
        return t

    def dram(self, name, shape, dtype, kind="Internal"):
        t = T(self.nc.dram_tensor(name, list(shape), dtype, kind=kind), name, "dram")
        self.tiles.append(t)
        return t

    def ext(self, h, name):
        t = T(h, name, "dram")
        self.tiles.append(t)
        return t

    def new_sem(self, name):
        cm = self.nc.semaphore(name)
        s = cm.__enter__()
        self._sems.append(cm)
        return s

    @staticmethod
    def _conf(k1, k2):
        if k1 is None or k2 is None or k1 == k2:
            return True
        if isinstance(k1, tuple) and isinstance(k2, tuple):
            n = min(len(k1), len(k2))
            return k1[:n] == k2[:n]
        return False

    @staticmethod
    def _rk(x):
        return (x, None) if isinstance(x, T) else x

    def _add_deps(self, op, reads, writes, extra=()):
        deps = list(extra)
        for (t, k) in reads:
            for kk, w in t.last_w.items():
                if self._conf(k, kk):
                    deps.append(w)
        for (t, k) in writes:
            for kk, w in t.last_w.items():
                if self._conf(k, kk):
                    deps.append(w)
            for kk, rs in t.readers.items():
                if self._conf(k, kk):
                    deps.extend(rs)
        best = {}
        for d in deps:
            if d is op:
                continue
            if d.is_dma:
                key = ("dma", id(d.dsem))
                if key not in best or best[key].dval < d.dval:
                    best[key] = d
            else:
                if d.eng == op.eng and op.eng == "pe":
                    continue
                key = d.eng
                if key not in best or best[key].idx < d.idx:
                    best[key] = d
        for d in best.values():
            op.deps.append(d)
            if not d.is_dma:
                d.signal = True
        for (t, k) in reads:
            t.readers.setdefault(k, []).append(op)
        for (t, k) in writes:
            if k is None:
                t.last_w = {None: op}
                t.readers = {}
            else:
                t.last_w[k] = op
                t.readers[k] = []

    def op(self, eng, fn, reads=(), writes=(), extra=()):
        rec = _Rec()
        fn(rec)
        o = Op(eng, _replay(*rec.call))
        o.idx = len(self.q[eng])
        self._add_deps(o, [self._rk(r) for r in reads], [self._rk(w) for w in writes], extra)
        self.q[eng].append(o)
        return o

    def dma(self, out_ap, in_ap, reads=(), writes=(), sem_of=None, extra=(), eng="sp"):
        reads = [self._rk(r) for r in reads]
        writes = [self._rk(w) for w in writes]
        if sem_of is None:
            cands = [x for x in list(writes) + list(reads) if x[0].kind == "sb"]
            sem_of = cands[0] if cands else (list(writes) + list(reads))[0]
        st, sk = self._rk(sem_of)
        if st.dma_sem is None:
            st.dma_sem = {}
        if sk not in st.dma_sem:
            st.dma_sem[sk] = [self.new_sem("d%d" % len(self._sems)), 0]
        ent = st.dma_sem[sk]
        ent[1] += 16
        o = Op(eng, lambda e: e.dma_start(out=out_ap, in_=in_ap))
        o.is_dma = True
        o.dsem = ent[0]
        o.dval = ent[1]
        o.idx = len(self.q[eng])
        self._add_deps(o, reads, writes, extra)
        self.q[eng].append(o)
        return o

    def emit(self):
        nc = self.nc
        sems = {e: self.new_sem("s_" + e) for e in COMPUTE}
        for e in COMPUTE:
            c = 0
            for o in self.q[e]:
                if (not o.is_dma) and o.signal:
                    c += 1
                    o.sigval = c
        engmap = {"pe": "tensor", "act": "scalar", "dve": "vector", "pool": "gpsimd", "sp": "sync"}
        stats = {"waits": 0, "instr": 0}

        def emit_queue(ename, engine):
            known = {}
            for o in self.q[ename]:
                for d in o.deps:
                    if d.is_dma:
                        key, val, sem = ("dma", id(d.dsem)), d.dval, d.dsem
                    else:
                        key, val, sem = d.eng, d.sigval, sems[d.eng]
                    if known.get(key, 0) >= val:
                        continue
                    engine.wait_ge(sem, val)
                    stats["waits"] += 1
                    known[key] = val
                ins = o.fn(engine)
                stats["instr"] += 1
                if o.is_dma:
                    ins.then_inc(o.dsem, 16)
                elif o.signal:
                    ins.then_inc(sems[ename], 1)
            if ename == "sp":
                for t in self.tiles:
                    if t.dma_sem:
                        for ent in t.dma_sem.values():
                            engine.wait_ge(ent[0], ent[1])

        with nc.Block() as block:
            for ename in ["sp", "pool", "act", "dve", "pe"]:
                getattr(block, engmap[ename])(lambda engine, _n=ename: emit_queue(_n, engine))
        return stats


D = 1024
DSGU = 2048
DFF = 2816
GIN = 6176
EPS = 1e-6
NEG = -30000.0
WSLOT = 4096


DBG = []
NGROUPS = 2


def build_program(n_ptiles, NT, NS, n_state=0):
    assert NT % 128 == 0 and NS % 2 == 0
    nc = bass.Bass("TRN2", target_bir_lowering=False)
    P = Prog(nc)
    NP = n_ptiles * NT
    NSAMP = NS * 64
    NTMAX = max(NT, NSAMP) if NS else NT
    assert NSAMP <= NT or n_ptiles == 0
    NBMAX = NTMAX // 128

    def din(name, shape):
        return P.ext(nc.dram_tensor(name, list(shape), F32, kind="ExternalInput"), name)

    def dout(name, shape):
        return P.ext(nc.dram_tensor(name, list(shape), F32, kind="ExternalOutput"), name)

    xp = din("xp", [max(NP, 1), D])
    xq = din("xq", [max(n_state * NT, 1), D])
    flag = din("flag", [1])
    xs = din("xs", [max(NSAMP, 1), D])
    sg = din("sg", [max(NS, 1), 16, 128, 128])
    sc = din("sc", [max(NS, 1), 3, 4096])
    norm_mix = din("norm_mix", [2, D])
    norm_ffn = din("norm_ffn", [2, D])
    norm_final = din("norm_final", [D])
    sgu_w_in = din("sgu_w_in", [D, 4096])
    sgu_ln_g = din("sgu_ln_g", [DSGU])
    sgu_ln_b = din("sgu_ln_b", [DSGU])
    sgu_w_s = din("sgu_w_s", [8, 128, 128])
    sgu_b_s = din("sgu_b_s", [8, 128])
    sgu_w_out = din("sgu_w_out", [DSGU, D])
    gdn_w_in = din("gdn_w_in", [D, GIN])
    gdn_w_conv = din("gdn_w_conv", [4, 4096])
    gdn_a_log = din("gdn_a_log", [16])
    gdn_dt_bias = din("gdn_dt_bias", [16])
    gdn_w_onorm = din("gdn_w_onorm", [128])
    gdn_w_out = din("gdn_w_out", [DSGU, D])
    ffn_w_gate = din("ffn_w_gate", [2, D, DFF])
    ffn_w_up = din("ffn_w_up", [2, D, DFF])
    ffn_w_down = din("ffn_w_down", [2, DFF, D])

    yp = dout("yp", [max(NP, 1), D])
    ys = dout("ys", [max(NSAMP, 1), D])
    sgp = dout("sgp", [16, 128, 128])
    scp = dout("scp", [3, 4096])
    sgs = dout("sgs", [max(NS, 1), 16, 128, 128])
    scs = dout("scs", [max(NS, 1), 3, 4096])
    svs = dout("svs", [max(NSAMP, 1), DSGU])

    xT = P.sb("xT", [128, 8, NTMAX], F32)
    hT = P.sb("hT", [128, 8, NTMAX], BF16)
    A1 = P.sb("A1", [128, 22, NTMAX], BF16)
    kqT = P.sb("kqT", [128, 16, NTMAX], BF16)
    wsl = [P.sb("wsl%d" % i, [128, WSLOT], BF16) for i in range(4)]
    vtk = P.sb("vtk", [128, NBMAX, 2048], F32)
    vnb = P.sb("vnb", [128, 2048], BF16)
    vtok = P.sb("vtok", [128, NBMAX, 2048], BF16)
    ktok = P.sb("ktok", [128, NBMAX, 1024], BF16)
    grep = P.sb("grep", [128, 2048], F32)
    brep = P.sb("brep", [128, 2048], F32)
    wfin = P.sb("wfin", [128, 1024], F32)
    identF = P.sb("identF", [128, 128], F32)
    identB = P.sb("identB", [128, 128], BF16)
    onesF = P.sb("onesF", [128, 128], F32)
    onesB = P.sb("onesB", [128, 128], BF16)
    tril = P.sb("tril", [128, 128], F32)
    incl = P.sb("incl", [128, 128], F32)
    strict = P.sb("strict", [128, 128], F32)
    negm = P.sb("negm", [128, 128], BF16)
    selA = P.sb("selA", [128, 128], F32)
    selB = P.sb("selB", [128, 128], F32)
    blk1 = P.sb("blk1", [128, 128], F32)
    wsT1 = P.sb("wsT", [128, 8, 128], BF16)
    bsrow1 = P.sb("bsrow", [1, 1024], F32)
    wsT = [wsT1, wsT1]
    bsrow = [bsrow1, bsrow1]
    cw = P.sb("cw", [128, 4, 32], F32)
    nsc = P.sb("nsc", [128, 40], F32)
    alr = P.sb("alr", [128, 16], F32)
    dtr = P.sb("dtr", [128, 16], F32)
    hal = P.sb("hal", [128, 32, 4, 3], F32)
    stg = P.sb("stg", [128, 128], F32)
    flg = P.sb("flg", [128, 1], F32)
    xc = P.sb("xc", [128, 4, NTMAX + 12], F32)
    cacc = P.sb("cacc", [128, 4, NTMAX], F32)
    vTg = P.sb("vTg", [128, 4, NTMAX], BF16)
    Sf = [P.sb("Sf%d" % i, [128, 16, 128], F32) for i in range(2)]
    Sb = [P.sb("Sb%d" % i, [128, 16, 128], BF16) for i in range(2)]
    gts_ = [P.sb("gts%d" % i, [128, 32], F32) for i in range(NBMAX)]
    gsm_ = [P.sb("gsm%d" % i, [128, 12, 16], F32) for i in range(NBMAX)]
    gsm_unused = None
    Gtri = P.sb("Gtri", [128, 8, 128], F32)
    E1 = P.sb("E1", [128, 8, 128], F32)
    Es = P.sb("Es", [128, 8, 128], F32)
    PT = P.sb("PT", [128, 8, 128], BF16)
    Xa = [P.sb("Xa%d" % i, [128, 8, 128], BF16) for i in range(2)]
    Xt = [P.sb("Xt%d" % i, [128, 8, 128], BF16) for i in range(2)]
    Pm = [P.sb("Pm%d" % i, [128, 8, 128], BF16) for i in range(2)]
    P32 = P.sb("P32", [128, 8, 128], F32)
    kg = P.sb("kg", [128, 8, 128], BF16)
    kd = P.sb("kd", [128, 8, 128], BF16)
    nWT = P.sb("nWT", [128, 8, 128], BF16)
    qdT = P.sb("qdT", [128, 8, 128], BF16)
    dlt = P.sb("dlt", [128, 8, 128], BF16)
    onb = P.sb("onb", [128, 8, 128], BF16)
    oss = P.sb("oss", [128, 3, 8], F32)
    bst = P.sb("bst", [128, 4, 6], F32)
    bmv = P.sb("bmv", [128, 4], F32)

    pp = [P.ps("pp%d" % i, [128, 1024], F32) for i in range(4)]
    ppc = [0]

    def nps():
        t = pp[ppc[0] % 4]
        ppc[0] += 1
        return t

    class PH:
        def __init__(self, t, half):
            self.t, self.off, self.key = t, half * 512, (t, ("h", half))

        def ap(self, a, b, rows=slice(None)):
            return self.t.h[rows, self.off + a:self.off + b]

        def apb(self, a, b):
            return self.t.h[:, :].bitcast(BF16)[:, 2 * self.off + a:2 * self.off + b]

    p1c = [0]

    def nps1():
        i = p1c[0]
        p1c[0] += 1
        return PH(pp[(i // 2) % 4], i % 2)

    def pbf(t):
        return t.h[:, :].bitcast(BF16)

    wsc = [0]

    def nws():
        t = wsl[wsc[0] % 4]
        wsc[0] += 1
        return t

    dbg_done = set()

    def dbg(name, t, ap, shape, dtype=F32):
        if name not in DBG or name in dbg_done:
            return
        dbg_done.add(name)
        o = P.ext(nc.dram_tensor("dbg_" + name, list(shape), dtype, kind="ExternalOutput"), "dbg_" + name)
        P.dma(o.h.ap(), ap, reads=[t], writes=[o])

    def pool(fn, reads=(), writes=()):
        return P.op("pool", fn, reads, writes)

    def dve(fn, reads=(), writes=()):
        return P.op("dve", fn, reads, writes)

    def act(fn, reads=(), writes=()):
        return P.op("act", fn, reads, writes)

    def pe(fn, reads=(), writes=()):
        return P.op("pe", fn, reads, writes)

    pool(lambda e: e.memset(onesF[:], 1.0), writes=[onesF])
    pool(lambda e: e.memset(onesB[:], 1.0), writes=[onesB])
    pool(lambda e: e.memset(identF[:], 0.0), writes=[identF])
    pool(lambda e: e.affine_select(out=identF[:], in_=onesF[:], pattern=[[-1, 128]], base=0, channel_multiplier=1,
                                   compare_op=ALU.is_equal, fill=0.0), reads=[onesF], writes=[identF])
    pool(lambda e: e.tensor_copy(out=identB[:], in_=identF[:]), reads=[identF], writes=[identB])
    pool(lambda e: e.memset(tril[:], 0.0), writes=[tril])
    pool(lambda e: e.affine_select(out=tril[:], in_=onesF[:], pattern=[[1, 128]], base=0, channel_multiplier=-1,
                                   compare_op=ALU.is_ge, fill=0.0), reads=[onesF], writes=[tril])
    pool(lambda e: e.tensor_copy(out=incl[:], in_=tril[:]), reads=[tril], writes=[incl])
    pool(lambda e: e.memset(incl[0:64, 64:128], 0.0), writes=[incl])
    pool(lambda e: e.memset(strict[:], 0.0), writes=[strict])
    pool(lambda e: e.affine_select(out=strict[:], in_=onesF[:], pattern=[[1, 128]], base=0, channel_multiplier=-1,
                                   compare_op=ALU.is_gt, fill=0.0), reads=[onesF], writes=[strict])
    pool(lambda e: e.memset(strict[0:64, 64:128], 0.0), writes=[strict])
    pool(lambda e: e.tensor_scalar(out=negm[:], in0=incl[:], scalar1=-1.0, scalar2=-NEG, op0=ALU.add, op1=ALU.mult),
         reads=[incl], writes=[negm])
    pool(lambda e: e.memset(selA[:], 0.0), writes=[selA])
    pool(lambda e: e.memset(selA[0:64, :], 1.0), writes=[selA])
    pool(lambda e: e.memset(selB[:], 0.0), writes=[selB])
    pool(lambda e: e.memset(selB[64:128, :], 1.0), writes=[selB])
    pool(lambda e: e.memset(blk1[:], 0.0), writes=[blk1])
    pool(lambda e: e.memset(blk1[0:64, 0:64], 1.0), writes=[blk1])
    pool(lambda e: e.memset(blk1[64:128, 64:128], 1.0), writes=[blk1])
    pool(lambda e: e.memset(hal[:], 0.0), writes=[hal])

    def bcast_rows(ap1d, n):
        return ap1d.partition_broadcast(128)

    P.dma(grep[:], sgu_ln_g.h.ap().partition_broadcast(128), reads=[sgu_ln_g], writes=[grep])
    P.dma(brep[:], sgu_ln_b.h.ap().partition_broadcast(128), reads=[sgu_ln_b], writes=[brep])
    P.dma(wfin[:], norm_final.h.ap().partition_broadcast(128), reads=[norm_final], writes=[wfin])
    P.dma(alr[:], gdn_a_log.h.ap().partition_broadcast(128), reads=[gdn_a_log], writes=[alr])
    P.dma(dtr[:], gdn_dt_bias.h.ap().partition_broadcast(128), reads=[gdn_dt_bias], writes=[dtr])
    P.dma(flg[:], flag.h.ap().partition_broadcast(128), reads=[flag], writes=[flg])
    act(lambda e: e.activation(out=alr[:], in_=alr[:], func=AF.Exp), reads=[alr], writes=[alr])
    dve(lambda e: e.tensor_scalar(out=alr[:], in0=alr[:], scalar1=-1.0, scalar2=None, op0=ALU.mult), reads=[alr], writes=[alr])

    def small_T(dst_ap, dst_t, src_ap, src_t, R):
        P.dma(stg[0:R, :], src_ap, reads=[src_t], writes=[stg])
        pt = nps()
        pe(lambda e: e.transpose(pt.h[:, 0:R], stg[0:R, :], identF[0:R, 0:R]), reads=[stg, identF], writes=[pt])
        act(lambda e: e.copy(out=dst_ap, in_=pt.h[:, 0:R]), reads=[pt], writes=[dst_t])

    small_T(cw[:].rearrange("p i c -> p (i c)"), cw, gdn_w_conv.h.ap().rearrange("i (c p) -> (i c) p", p=128), gdn_w_conv, 128)
    small_T(nsc[:, 0:16], nsc, norm_mix.h.ap().rearrange("l (k p) -> (l k) p", p=128), norm_mix, 16)
    small_T(nsc[:, 16:32], nsc, norm_ffn.h.ap().rearrange("l (k p) -> (l k) p", p=128), norm_ffn, 16)
    small_T(nsc[:, 32:33], nsc, gdn_w_onorm.h.ap().rearrange("(o p) -> o p", o=1), gdn_w_onorm, 1)

    def setup_sgu_kind(kind):
        if kind == 0:
            P.dma(vtk[:, 0, 0:1024].rearrange("p (g s) -> p g s", g=8), sgu_w_s.h.ap().rearrange("g t s -> t g s"),
                  reads=[sgu_w_s], writes=[(vtk, 0)])
            P.dma(bsrow[0][:], sgu_b_s.h.ap().rearrange("(o g) t -> o (g t)", o=1), reads=[sgu_b_s], writes=[bsrow[0]])
        else:
            pool(lambda e: e.memset(vtk[:, 0, 0:1024], 0.0), writes=[(vtk, 0)])
            v3 = vtk[:, 0, 0:1024].rearrange("p (g s) -> p g s", g=8)
            P.dma(v3[0:64, :, 0:64], sgu_w_s.h.ap().rearrange("g t s -> t g s")[0:64, :, 0:64], reads=[sgu_w_s], writes=[(vtk, 0)])
            P.dma(v3[64:128, :, 64:128], sgu_w_s.h.ap().rearrange("g t s -> t g s")[0:64, :, 0:64], reads=[sgu_w_s], writes=[(vtk, 0)])
            b4 = bsrow[1][:].rearrange("o (g h t) -> o g h t", g=8, h=2)
            for hh in range(2):
                P.dma(b4[:, :, hh, :], sgu_b_s.h.ap().rearrange("(o g) t -> o g t", o=1)[:, :, 0:64], reads=[sgu_b_s], writes=[bsrow[1]])
        msk = tril if kind == 0 else incl
        for g4 in range(2):
            pt = nps()
            for j in range(4):
                g = g4 * 4 + j
                pe(lambda e, g=g, j=j, pt=pt: e.transpose(pt.h[:, j * 128:(j + 1) * 128], vtk[:, 0, g * 128:(g + 1) * 128], identF[:]),
                   reads=[(vtk, 0), identF], writes=[pt])
            dve(lambda e, g4=g4, pt=pt, kind=kind, msk=msk: e.tensor_tensor(
                out=wsT[kind][:, g4 * 4:(g4 + 1) * 4, :], in0=pt.h[:, 0:512].rearrange("p (g t) -> p g t", g=4),
                in1=msk[:].unsqueeze(1).to_broadcast([128, 4, 128]), op=ALU.mult), reads=[pt, msk], writes=[wsT[kind]])

    wdefs = [
        ("w_sgu_in", sgu_w_in, sgu_w_in.h.ap(), D, 4096, (0, 0)),
        ("w_sgu_out", sgu_w_out, sgu_w_out.h.ap(), DSGU, D, None),
        ("w_gate0", ffn_w_gate, ffn_w_gate.h.ap()[0], D, DFF, (16, 0)),
        ("w_up0", ffn_w_up, ffn_w_up.h.ap()[0], D, DFF, (16, 0)),
        ("w_down0", ffn_w_down, ffn_w_down.h.ap()[0], DFF, D, None),
        ("w_gdn_in", gdn_w_in, gdn_w_in.h.ap(), D, GIN, (8, 0)),
        ("w_gdn_out", gdn_w_out, gdn_w_out.h.ap(), DSGU, D, (32, 1)),
        ("w_gate1", ffn_w_gate, ffn_w_gate.h.ap()[1], D, DFF, (24, 0)),
        ("w_up1", ffn_w_up, ffn_w_up.h.ap()[1], D, DFF, (24, 0)),
        ("w_down1", ffn_w_down, ffn_w_down.h.ap()[1], DFF, D, None),
    ]
    WS = {}
    pro_ops = []
    cnt = 0
    engs = ["act", "dve"]
    stf = [(vtk, 0, lambda w: vtk[:, 0, 0:w]), (vtk, 1, lambda w: vtk[:, 1, 0:w]),
           (Sf[0], None, lambda w: Sf[0][:].rearrange("p h d -> p (h d)")[:, 0:w]), (Sf[1], None, lambda w: Sf[1][:].rearrange("p h d -> p (h d)")[:, 0:w])]
    stb = [(vtok, 0, lambda w: vtok[:, 0, 0:w]), (vtok, 1, lambda w: vtok[:, 1, 0:w]),
           (Sb[0], None, lambda w: Sb[0][:].rearrange("p h d -> p (h d)")[:, 0:w]), (Sb[1], None, lambda w: Sb[1][:].rearrange("p h d -> p (h d)")[:, 0:w])]
    for (name, srct, srcap, K, C, fold) in wdefs:
        scr = P.dram(name, [K, C], BF16)
        WS[name] = (scr, K // 128, C)
        for kc in range(K // 128):
            for c0 in range(0, C, 2048):
                cwid = min(2048, C - c0)
                i = cnt % 4
                cnt += 1
                ft, fk, fap = stf[i]
                bt, bk, bap = stb[i]
                fkey = (ft, fk) if fk is not None else ft
                bkey = (bt, bk) if bk is not None else bt
                P.dma(fap(cwid), srcap[kc * 128:(kc + 1) * 128, c0:c0 + cwid], reads=[srct], writes=[fkey])
                en = engs[cnt % 2]
                if fold is None:
                    if en == "act":
                        act(lambda e: e.copy(out=bap(cwid), in_=fap(cwid)), reads=[fkey], writes=[bkey])
                    else:
                        dve(lambda e: e.tensor_copy(out=bap(cwid), in_=fap(cwid)), reads=[fkey], writes=[bkey])
                else:
                    col = fold[0] + (kc if fold[1] == 0 else 0)
                    if en == "act":
                        act(lambda e: e.activation(out=bap(cwid), in_=fap(cwid), func=AF.Copy, scale=nsc[:, col:col + 1]), reads=[fkey, nsc], writes=[bkey])
                    else:
                        dve(lambda e: e.tensor_scalar(out=bap(cwid), in0=fap(cwid), scalar1=nsc[:, col:col + 1], scalar2=None, op0=ALU.mult),
                            reads=[fkey, nsc], writes=[bkey])
                o = P.dma(scr.h.ap()[kc * 128:(kc + 1) * 128, c0:c0 + cwid], bap(cwid), reads=[bkey], writes=[(scr, (kc, c0))], eng="act")
                pro_ops.append(o)
    fence = P.dma(stg[0:1, 0:4], onesF[0:1, 0:4], reads=[onesF], writes=[stg], extra=pro_ops)
    for name in WS:
        WS[name][0].last_w = {None: fence}
        WS[name][0].readers = {}

    def wpieces(name, c_lo, c_hi, KC):
        scr = WS[name][0]
        pcmax = (WSLOT // KC) // 128 * 128
        c0 = c_lo
        while c0 < c_hi:
            pc = min(pcmax, c_hi - c0)
            slot = nws()
            view = slot[:, 0:KC * pc].rearrange("p (k c) -> p k c", k=KC)
            P.dma(view, scr.h.ap().rearrange("(k p) c -> p k c", p=128)[:, :, c0:c0 + pc], reads=[scr], writes=[slot])
            yield slot, view, c0, pc
            c0 += pc

    def linear_fm(name, c_lo, c_hi, KC, rhs_t, rhs_ap, NTt, evac, oc_base=0, piped=False):
        gmax = min(4, max(1, 1024 // NTt))
        per_bank = max(1, 512 // NTt)
        pend = {"b1": None, "b2": None, "b2n": None}
        for slot, view, c0, pc in wpieces(name, c_lo, c_hi, KC):
            nch = pc // 128
            j0 = 0
            while j0 < nch:
                n = min(gmax, nch - j0)
                pt = nps()
                for j in range(n):
                    off = (j // per_bank) * 512 + (j % per_bank) * NTt
                    for k in range(KC):
                        pe(lambda e, pt=pt, off=off, view=view, k=k, jj=j0 + j: e.matmul(
                            pt.h[:, off:off + NTt], lhsT=view[:, k, jj * 128:(jj + 1) * 128], rhs=rhs_ap(k),
                            start=(k == 0), stop=(k == KC - 1)), reads=[slot, rhs_t], writes=[pt])
                if per_bank * NTt == 512 or n <= per_bank:
                    ap3 = pt.h[:, 0:n * NTt].rearrange("p (j t) -> p j t", j=n)
                else:
                    raise NotImplementedError
                called = [False]

                def mid():
                    called[0] = True
                    if pend["b1"] is not None:
                        pend["b1"]()
                    if pend["b2"] is not None:
                        pend["b2"]()
                if piped:
                    d_ = evac(oc_base + (c0 - c_lo) // 128 + j0, n, pt, ap3, mid)
                else:
                    d_ = evac(oc_base + (c0 - c_lo) // 128 + j0, n, pt, ap3)
                if not called[0]:
                    mid()
                pend["b2"] = pend["b2n"]
                pend["b1"], pend["b2n"] = d_ if d_ is not None else (None, None)
                j0 += n
        for k_ in ("b1", "b2"):
            if pend[k_] is not None:
                pend[k_]()
        if pend["b2n"] is not None:
            pend["b2n"]()

    def rmsnorm_to_hT(NTt):
        act(lambda e: e.activation(out=hT[:, :, 0:NTt], in_=xT[:, :, 0:NTt], func=AF.Square), reads=[xT], writes=[hT])
        pt = nps()
        for k in range(8):
            pe(lambda e, k=k, pt=pt: e.matmul(pt.h[:, 0:NTt], lhsT=onesB[:], rhs=hT[:, k, 0:NTt], start=(k == 0), stop=(k == 7)),
               reads=[onesB, hT], writes=[pt])
        act(lambda e, pt=pt: e.activation(out=cacc[:, 0, 0:NTt], in_=pt.h[:, 0:NTt], func=AF.Ln, scale=1.0 / D, bias=EPS), reads=[pt], writes=[cacc])
        act(lambda e: e.activation(out=cacc[:, 0, 0:NTt], in_=cacc[:, 0, 0:NTt], func=AF.Exp, scale=-0.5), reads=[cacc], writes=[cacc])
        dve(lambda e: e.tensor_tensor(out=hT[:, :, 0:NTt], in0=xT[:, :, 0:NTt], in1=cacc[:, 0, 0:NTt].unsqueeze(1).to_broadcast([128, 8, NTt]),
                                      op=ALU.mult), reads=[xT, cacc], writes=[hT])

    def resid_evac(NTt):
        def ev(oc0, n, pt, ap3):
            dve(lambda e: e.tensor_tensor(out=xT[:, oc0:oc0 + n, 0:NTt], in0=ap3, in1=xT[:, oc0:oc0 + n, 0:NTt], op=ALU.add),
                reads=[pt, xT], writes=[xT])
        return ev

    def ffn(layer, NTt):
        rmsnorm_to_hT(NTt)
        hrhs = lambda k: hT[:, k, 0:NTt]

        def ev_gate(oc0, n, pt, ap3):
            act(lambda e: e.activation(out=A1[:, oc0:oc0 + n, 0:NTt], in_=ap3, func=AF.Silu), reads=[pt], writes=[(A1, c) for c in range(oc0, oc0 + n)])

        def ev_up(oc0, n, pt, ap3):
            dve(lambda e: e.tensor_tensor(out=A1[:, oc0:oc0 + n, 0:NTt], in0=ap3, in1=A1[:, oc0:oc0 + n, 0:NTt], op=ALU.mult),
                reads=[pt] + [(A1, c) for c in range(oc0, oc0 + n)], writes=[(A1, c) for c in range(oc0, oc0 + n)])
        linear_fm("w_gate%d" % layer, 0, DFF, 8, hT, hrhs, NTt, ev_gate)
        linear_fm("w_up%d" % layer, 0, DFF, 8, hT, hrhs, NTt, ev_up)
        linear_fm("w_down%d" % layer, 0, D, 22, A1, lambda k: A1[:, k, 0:NTt], NTt, resid_evac(NTt))

    XW = NTMAX + 12
    convsets = [
        (xc.h[:, :, :], xc, cacc.h[:, :, :], cacc, vTg.h[:, :, :], vTg),
        (vtk.h[:, 0, 0:4 * XW].rearrange("p (j w) -> p j w", j=4), (vtk, 0),
         vnb.h[:, :].bitcast(F32)[:, 0:4 * NTMAX].rearrange("p (j t) -> p j t", j=4), vnb,
         vtk.h[:, 0, 4 * XW + 16:4 * XW + 16 + 2 * NTMAX].bitcast(BF16).rearrange("p (j t) -> p j t", j=4), (vtk, 0)),
    ]
    convc = [0]

    def run_tile(kind, tok0, NTt, first, last, seqs, mode="full", apply_flag=False, need_q_halo=False):
        full = mode == "full"
        nb = NTt // 128
        src = (xp if full else xq) if kind == 0 else xs
        dst = yp if kind == 0 else ys
        nseg = 1 if kind == 0 else NTt // 64
        L = NTt // nseg
        if apply_flag:
            dve(lambda e: e.tensor_scalar(out=Sf[0][:], in0=Sf[0][:], scalar1=flg[:, 0:1], scalar2=None, op0=ALU.mult), reads=[Sf[0], flg], writes=[Sf[0]])
            act(lambda e: e.copy(out=Sb[0][:], in_=Sf[0][:]), reads=[Sf[0]], writes=[Sb[0]])
            dve(lambda e: e.tensor_scalar(out=hal[:], in0=hal[:], scalar1=flg[:, 0:1], scalar2=None, op0=ALU.mult), reads=[hal, flg], writes=[hal])
        for b in range(nb):
            xb = b % 2
            P.dma(vtk[:, xb, 0:1024], src.h.ap()[tok0 + b * 128: tok0 + (b + 1) * 128, :], reads=[src], writes=[(vtk, xb)])
            for hf in range(2):
                pt = nps()
                for j in range(4):
                    k = hf * 4 + j
                    pe(lambda e, pt=pt, j=j, k=k, xb=xb: e.transpose(pt.h[:, j * 128:(j + 1) * 128], vtk[:, xb, k * 128:(k + 1) * 128], identF[:]),
                       reads=[(vtk, xb), identF], writes=[pt])
                act(lambda e, pt=pt, hf=hf, b=b: e.copy(out=xT[:, hf * 4:(hf + 1) * 4, b * 128:(b + 1) * 128],
                                                        in_=pt.h[:, 0:512].rearrange("p (j t) -> p j t", j=4)), reads=[pt], writes=[xT])
        rmsnorm_to_hT(NTt)
        hrhs = lambda k: hT[:, k, 0:NTt]

        def ev_u(oc0, n, pt, ap3):
            act(lambda e: e.activation(out=A1[:, oc0:oc0 + n, 0:NTt], in_=ap3, func=AF.Gelu), reads=[pt], writes=[(A1, c) for c in range(oc0, oc0 + n)])
        for slot, view, c0, pc in wpieces("w_sgu_in", 2048, 4096, 8):
            for b in range(nb):
                for q in range(pc // 512):
                    pt = nps()
                    for k in range(8):
                        pe(lambda e, pt=pt, k=k, b=b, q=q, view=view: e.matmul(pt.h[:, 0:512], lhsT=hT[:, k, b * 128:(b + 1) * 128],
                                                                               rhs=view[:, k, q * 512:(q + 1) * 512], start=(k == 0), stop=(k == 7)),
                           reads=[slot, hT], writes=[pt])
                    cc = c0 - 2048 + q * 512
                    act(lambda e, pt=pt, b=b, cc=cc: e.activation(out=vtk[:, b, cc:cc + 512], in_=pt.h[:, 0:512], func=AF.Gelu),
                        reads=[pt], writes=[(vtk, b)])
        for b in range(nb):
            for q in range(4):
                dve(lambda e, b=b, q=q: e.bn_stats(out=bst[:, q, :], in_=vtk[:, b, q * 512:(q + 1) * 512]), reads=[(vtk, b)], writes=[bst])
            dve(lambda e: e.bn_aggr(out=bmv[:, 0:2], in_=bst[:].rearrange("p q s -> p (q s)")), reads=[bst], writes=[bmv])
            act(lambda e: e.activation(out=bmv[:, 2:3], in_=bmv[:, 1:2], func=AF.Ln, bias=1e-5), reads=[bmv], writes=[bmv])
            act(lambda e: e.activation(out=bmv[:, 2:3], in_=bmv[:, 2:3], func=AF.Exp, scale=-0.5), reads=[bmv], writes=[bmv])
            dve(lambda e, b=b: e.scalar_tensor_tensor(out=vtk[:, b, :], in0=vtk[:, b, :], scalar=bmv[:, 0:1], in1=grep[:],
                                                      op0=ALU.subtract, op1=ALU.mult), reads=[(vtk, b), bmv, grep], writes=[(vtk, b)])
            dve(lambda e, b=b: e.scalar_tensor_tensor(out=vtk[:, b, :], in0=vtk[:, b, :], scalar=bmv[:, 2:3], in1=brep[:],
                                                      op0=ALU.mult, op1=ALU.add), reads=[(vtk, b), bmv, brep], writes=[(vtk, b)])
            if kind == 1:
                P.dma(svs.h.ap()[tok0 + b * 128: tok0 + (b + 1) * 128, :], vtk[:, b, :], reads=[(vtk, b)], writes=[svs])
        linear_fm("w_sgu_in", 0, 2048, 8, hT, hrhs, NTt, ev_u)
        for b in range(nb):
            act(lambda e, b=b: e.copy(out=vnb[:], in_=vtk[:, b, :]), reads=[(vtk, b)], writes=[vnb])
            for c4 in range(4):
                pt = nps()
                brow = bsrow[kind][0:1, c4 * 256:(c4 + 1) * 256].rearrange("o (g t) -> o g t", g=2).unsqueeze(2).to_broadcast([1, 2, 2, 128])
                pe(lambda e, pt=pt, brow=brow: e.matmul(pt.h[:, 0:512], lhsT=onesF[0:1, :], rhs=brow, start=True, stop=False),
                   reads=[onesF, bsrow[kind]], writes=[pt])
                for j in range(4):
                    ci = c4 * 4 + j
                    g = ci // 2
                    pe(lambda e, pt=pt, j=j, ci=ci, g=g: e.matmul(pt.h[:, j * 128:(j + 1) * 128], lhsT=vnb[:, ci * 128:(ci + 1) * 128],
                                                                 rhs=wsT[kind][:, g, :], start=False, stop=(j == 3)), reads=[vnb, wsT[kind]], writes=[pt])
                dve(lambda e, pt=pt, c4=c4, b=b: e.tensor_tensor(out=A1[:, c4 * 4:(c4 + 1) * 4, b * 128:(b + 1) * 128],
                                                                 in0=pt.h[:, 0:512].rearrange("p (j t) -> p j t", j=4),
                                                                 in1=A1[:, c4 * 4:(c4 + 1) * 4, b * 128:(b + 1) * 128], op=ALU.mult),
                    reads=[pt] + [(A1, c) for c in range(c4 * 4, c4 * 4 + 4)], writes=[(A1, c) for c in range(c4 * 4, c4 * 4 + 4)])
        linear_fm("w_sgu_out", 0, D, 16, A1, lambda k: A1[:, k, 0:NTt], NTt, resid_evac(NTt))
        ffn(0, NTt)
        rmsnorm_to_hT(NTt)
        if kind == 1:
            for c4 in range(8):
                P.dma(vtk[0:nseg * 3, 1, 0:512], sc.h.ap().rearrange("s i c -> (s i) c")[seqs[0] * 3:(seqs[0] + nseg) * 3, c4 * 512:(c4 + 1) * 512],
                      reads=[sc], writes=[(vtk, 1)])
                pt = nps()
                for j in range(4):
                    pe(lambda e, pt=pt, j=j: e.transpose(pt.h[:, j * 16:j * 16 + nseg * 3], vtk[0:nseg * 3, 1, j * 128:(j + 1) * 128],
                                                       identF[0:nseg * 3, 0:nseg * 3]), reads=[(vtk, 1), identF], writes=[pt])
                act(lambda e, pt=pt, c4=c4: e.copy(out=hal[:, c4 * 4:(c4 + 1) * 4, 0:nseg, :],
                                                   in_=pt.h[:, 0:64].rearrange("p (j x) -> p j x", j=4)[:, :, 0:nseg * 3].rearrange("p j (s i) -> p j s i", i=3)),
                    reads=[pt], writes=[hal])
        W = L + 3

        def ev_qkv(oc0, n, pt, ap3, mid):
            XC, XCK, CA, CAK, VT, VTK = convsets[convc[0] % 2]
            convc[0] += 1
            xv = XC[:, 0:n, 0:nseg * W].rearrange("p j (s w) -> p j s w", s=nseg)
            act(lambda e: e.copy(out=xv[:, :, :, 3:3 + L], in_=ap3.rearrange("p j (s l) -> p j s l", s=nseg)), reads=[pt], writes=[XCK])
            pool(lambda e: e.tensor_copy(out=xv[:, :, :, 0:3], in_=hal[:, oc0:oc0 + n, 0:nseg, :]), reads=[hal], writes=[XCK])
            pool(lambda e: e.tensor_copy(out=hal[:, oc0:oc0 + n, 0:nseg, :], in_=xv[:, :, :, L:L + 3]), reads=[XCK], writes=[hal])
            mid()
            avs = [CA[:, j, 0:NTt].rearrange("p (s l) -> p s l", s=nseg) for j in range(n)]
            CK = [(CAK if isinstance(CAK, tuple) else (CAK, None)) for j in range(n)]
            for j in range(n):
                c = oc0 + j
                dve(lambda e, j=j, c=c: e.tensor_scalar(out=avs[j], in0=xv[:, j, :, 0:L], scalar1=cw[:, 0, c:c + 1], scalar2=None, op0=ALU.mult),
                    reads=[XCK, cw], writes=[(CK[j][0], ("c", j))])
            for i in range(1, 4):
                for j in range(n):
                    c = oc0 + j
                    dve(lambda e, j=j, c=c, i=i: e.scalar_tensor_tensor(out=avs[j], in0=xv[:, j, :, i:i + L], scalar=cw[:, i, c:c + 1], in1=avs[j],
                                                                      op0=ALU.mult, op1=ALU.add), reads=[XCK, cw, (CK[j][0], ("c", j))], writes=[(CK[j][0], ("c", j))])
            KQK = [(kqT, c) for c in range(oc0, oc0 + n)]
            if oc0 < 16:
                act(lambda e: e.activation(out=kqT[:, oc0:oc0 + n, 0:NTt], in_=CA[:, 0:n, 0:NTt], func=AF.Silu), reads=[CAK], writes=KQK)
                act(lambda e: e.activation(out=VT[:, 0:n, 0:NTt], in_=kqT[:, oc0:oc0 + n, 0:NTt], func=AF.Square), reads=KQK, writes=[VTK])

                def partB1():
                    p2 = nps()
                    per_bank = max(1, 512 // NTt)
                    for j in range(n):
                        off = (j // per_bank) * 512 + (j % per_bank) * NTt
                        pe(lambda e, j=j, off=off, p2=p2: e.matmul(p2.h[:, off:off + NTt], lhsT=onesB[:], rhs=VT[:, j, 0:NTt], start=True, stop=True),
                           reads=[onesB, VTK], writes=[p2])
                    a3 = p2.h[:, 0:n * NTt].rearrange("p (j t) -> p j t", j=n)
                    act(lambda e: e.activation(out=CA[:, 0:n, 0:NTt], in_=a3, func=AF.Ln, bias=EPS), reads=[p2], writes=[CAK])
                    qb = -0.5 * float(np.log(128.0)) if oc0 < 8 else 0.0
                    act(lambda e: e.activation(out=CA[:, 0:n, 0:NTt], in_=CA[:, 0:n, 0:NTt], func=AF.Exp, scale=-0.5, bias=qb), reads=[CAK], writes=[CAK])

                def partB2():
                    dve(lambda e: e.tensor_tensor(out=kqT[:, oc0:oc0 + n, 0:NTt], in0=kqT[:, oc0:oc0 + n, 0:NTt], in1=CA[:, 0:n, 0:NTt], op=ALU.mult),
                        reads=KQK + [CAK], writes=KQK)
                    if oc0 >= 8:
                        for b in range(nb):
                            p3 = nps()
                            for j in range(n):
                                pe(lambda e, j=j, b=b, p3=p3: e.transpose(pbf(p3)[:, j * 128:(j + 1) * 128], kqT[:, oc0 + j, b * 128:(b + 1) * 128], identB[:]),
                                   reads=KQK + [identB], writes=[p3])
                            act(lambda e, b=b, p3=p3: e.copy(out=ktok[:, b, (oc0 - 8) * 128:(oc0 - 8 + n) * 128], in_=pbf(p3)[:, 0:n * 128]),
                                reads=[p3], writes=[(ktok, b)])
                return (partB1, partB2)
            else:
                act(lambda e: e.activation(out=VT[:, 0:n, 0:NTt], in_=CA[:, 0:n, 0:NTt], func=AF.Silu), reads=[CAK], writes=[VTK])

                def partB2():
                    for b in range(nb):
                        p3 = nps()
                        for j in range(n):
                            pe(lambda e, j=j, b=b, p3=p3: e.transpose(pbf(p3)[:, j * 128:(j + 1) * 128], VT[:, j, b * 128:(b + 1) * 128], identB[:]),
                               reads=[VTK, identB], writes=[p3])
                        act(lambda e, b=b, p3=p3: e.copy(out=vtok[:, b, (oc0 - 16) * 128:(oc0 - 16 + n) * 128], in_=pbf(p3)[:, 0:n * 128]),
                            reads=[p3], writes=[(vtok, b)])
                return (None, partB2)
        if full or need_q_halo:
            linear_fm("w_gdn_in", 0, 4096, 8, hT, hrhs, NTt, ev_qkv, piped=True)
        else:
            linear_fm("w_gdn_in", 1024, 4096, 8, hT, hrhs, NTt, ev_qkv, oc_base=8, piped=True)
        if (kind == 1 or last) and full:
            odst = scs if kind == 1 else scp
            orow = odst.h.ap().rearrange("s i c -> (s i) c") if kind == 1 else odst.h.ap()
            r0 = seqs[0] * 3 if kind == 1 else 0
            for c4 in range(8):
                pt = nps()
                for j in range(4):
                    pe(lambda e, pt=pt, j=j, c4=c4: e.transpose(pt.h[0:nseg * 3, j * 128:(j + 1) * 128],
                                                              hal[:, c4 * 4 + j, 0:nseg, :].rearrange("p s i -> p (s i)"), identF[:]),
                       reads=[hal, identF], writes=[pt])
                act(lambda e, pt=pt: e.copy(out=vtk[0:nseg * 3, 1, 0:512], in_=pt.h[0:nseg * 3, 0:512]), reads=[pt], writes=[(vtk, 1)])
                P.dma(orow[r0:r0 + nseg * 3, c4 * 512:(c4 + 1) * 512], vtk[0:nseg * 3, 1, 0:512], reads=[(vtk, 1)], writes=[odst])

        gslot, gview, _, _ = next(wpieces("w_gdn_in", 6144, 6176, 8))
        if kind == 0 and first and not apply_flag:
            pool(lambda e: e.memset(Sf[0][:], 0.0), writes=[Sf[0]])
            pool(lambda e: e.memset(Sb[0][:], 0.0), writes=[Sb[0]])
        for b in range(nb):
            pg = nps()
            for k in range(8):
                pe(lambda e, k=k, b=b, pg=pg: e.matmul(pg.h[:, 0:32], lhsT=hT[:, k, b * 128:(b + 1) * 128], rhs=gview[:, k, 0:32],
                                                       start=(k == 0), stop=(k == 7)), reads=[hT, gslot], writes=[pg])
            act(lambda e, pg=pg: e.copy(out=gts_[b][:], in_=pg.h[:, 0:32]), reads=[pg], writes=[gts_[b]])
            G = lambda i, b=b: gsm_[b][:, i, :]
            act(lambda e: e.activation(out=G(0), in_=gts_[b][:, 0:16], func=AF.Exp, scale=-1.0), reads=[gts_[b]], writes=[gsm_[b]])
            dve(lambda e: e.tensor_scalar(out=G(0), in0=G(0), scalar1=1.0, scalar2=None, op0=ALU.add), reads=[gsm_[b]], writes=[gsm_[b]])
            dve(lambda e: e.reciprocal(out=G(1), in_=G(0)), reads=[gsm_[b]], writes=[gsm_[b]])
            dve(lambda e: e.tensor_scalar(out=G(2), in0=G(1), scalar1=-1.0, scalar2=None, op0=ALU.mult), reads=[gsm_[b]], writes=[gsm_[b]])
            dve(lambda e: e.tensor_tensor(out=G(9), in0=gts_[b][:, 16:32], in1=dtr[:], op=ALU.add), reads=[gts_[b], dtr], writes=[gsm_[b]])
            act(lambda e: e.activation(out=G(9), in_=G(9), func=AF.Exp), reads=[gsm_[b]], writes=[gsm_[b]])
            act(lambda e: e.activation(out=G(9), in_=G(9), func=AF.Ln, bias=1.0), reads=[gsm_[b]], writes=[gsm_[b]])
            dve(lambda e: e.tensor_tensor(out=G(3), in0=G(9), in1=alr[:], op=ALU.mult), reads=[gsm_[b], alr], writes=[gsm_[b]])
            pc_ = nps()
            pe(lambda e, pc_=pc_: e.matmul(pc_.h[:, 0:16], lhsT=incl[:], rhs=G(3), start=True, stop=True), reads=[incl, gsm_[b]], writes=[pc_])
            pe(lambda e, pc_=pc_: e.matmul(pc_.h[:, 16:32], lhsT=blk1[:], rhs=G(3), start=True, stop=True), reads=[blk1, gsm_[b]], writes=[pc_])
            pe(lambda e, pc_=pc_: e.matmul(pc_.h[:, 32:48], lhsT=selA[:], rhs=G(3), start=True, stop=True), reads=[selA, gsm_[b]], writes=[pc_])
            pe(lambda e, pc_=pc_: e.matmul(pc_.h[:, 48:64], lhsT=selB[:], rhs=G(3), start=True, stop=True), reads=[selB, gsm_[b]], writes=[pc_])
            act(lambda e, pc_=pc_: e.copy(out=G(4), in_=pc_.h[:, 0:16]), reads=[pc_], writes=[gsm_[b]])
            act(lambda e, pc_=pc_: e.activation(out=G(5), in_=pc_.h[:, 0:16], func=AF.Exp), reads=[pc_], writes=[gsm_[b]])
            dve(lambda e, pc_=pc_: e.tensor_tensor(out=G(6), in0=pc_.h[:, 16:32], in1=G(4), op=ALU.subtract), reads=[pc_, gsm_[b]], writes=[gsm_[b]])
            act(lambda e: e.activation(out=G(6), in_=G(6), func=AF.Exp), reads=[gsm_[b]], writes=[gsm_[b]])
            act(lambda e, pc_=pc_: e.activation(out=G(7), in_=pc_.h[:, 32:48], func=AF.Exp), reads=[pc_], writes=[gsm_[b]])
            act(lambda e, pc_=pc_: e.activation(out=G(8), in_=pc_.h[:, 48:64], func=AF.Exp), reads=[pc_], writes=[gsm_[b]])
            dbg("gts", gts_[b], gts_[b][:], [128, 32])
            dbg("gsm", gsm_[b], gsm_[b][:], [128, 12, 16])
        def ev_z(oc0, n, pt, ap3):
            act(lambda e: e.activation(out=A1[:, oc0:oc0 + n, 0:NTt], in_=ap3, func=AF.Silu), reads=[pt], writes=[(A1, c) for c in range(oc0, oc0 + n)])
        if full:
            linear_fm("w_gdn_in", 4096, 6144, 8, hT, hrhs, NTt, ev_z)
        for b in range(nb):
            gsm = gsm_[b]
            G = lambda i, b=b: gsm_[b][:, i, :]
            if kind == 1:
                for ch in range(2):
                    sq_ = seqs[0] + b * 2 + ch
                    P.dma(Sf[ch][:], sg.h.ap()[sq_].rearrange("h k v -> k h v"), reads=[sg], writes=[Sf[ch]])
                    act(lambda e, ch=ch: e.copy(out=Sb[ch][:], in_=Sf[ch][:]), reads=[Sf[ch]], writes=[Sb[ch]])
            bs_ = slice(b * 128, (b + 1) * 128)
            NG = NGROUPS
            HG = 8 // NG
            GW = HG * 128
            grp = list(range(NG))
            for hf in range(2):
                h0 = hf * 8
                hs = [slice(g * HG, (g + 1) * HG) for g in grp]
                hgl = [slice(h0 + g * HG, h0 + (g + 1) * HG) for g in grp]
                K_ = lambda t, g: (t, ("g", g))
                X032, XT32, X132 = Gtri, E1, Es
                fl = lambda t, g: t[:, hs[g], :].rearrange("p h t -> p (h t)")
                prt_, pgr_, pkq_ = {}, {}, {}
                for g in grp:
                    dve(lambda e, g=g: e.tensor_tensor(out=Gtri[:, hs[g], :], in0=G(3)[:, hgl[g]].unsqueeze(2).to_broadcast([128, HG, 128]),
                                                       in1=incl[:].unsqueeze(1).to_broadcast([128, HG, 128]), op=ALU.mult),
                        reads=[gsm, incl], writes=[K_(Gtri, g)])
                for g in grp:
                    prt = nps1()
                    prt_[g] = prt
                    pe(lambda e, g=g, prt=prt: e.matmul(prt.ap(0, GW), lhsT=onesF[:], rhs=fl(Gtri, g), start=True, stop=False),
                       reads=[onesF, K_(Gtri, g)], writes=[prt.key])
                    pe(lambda e, g=g, prt=prt: e.matmul(prt.ap(0, GW), lhsT=identB[:], rhs=negm[:].unsqueeze(1).to_broadcast([128, HG, 128]),
                                                        start=False, stop=True), reads=[identB, negm], writes=[prt.key])
                    if full:
                        pgr = nps1()
                        pgr_[g] = pgr
                        pe(lambda e, g=g, pgr=pgr: e.matmul(pgr.ap(0, GW), lhsT=onesF[:], rhs=fl(Gtri, g), start=True, stop=True),
                           reads=[onesF, K_(Gtri, g)], writes=[pgr.key])
                v3 = lambda ph: ph.ap(0, GW).rearrange("p (h t) -> p h t", h=HG)
                for g in grp:
                    dve(lambda e, g=g: e.tensor_tensor(out=E1[:, hs[g], :], in0=v3(prt_[g]), in1=G(4)[:, hgl[g]].unsqueeze(2).to_broadcast([128, HG, 128]),
                                                       op=ALU.subtract), reads=[prt_[g].key, gsm], writes=[K_(E1, g)])
                for g in grp:
                    act(lambda e, g=g: e.activation(out=E1[:, hs[g], :], in_=E1[:, hs[g], :], func=AF.Exp), reads=[K_(E1, g)], writes=[K_(E1, g)])
                for g in grp:
                    pool(lambda e, g=g: e.tensor_tensor(out=Es[:, hs[g], :], in0=E1[:, hs[g], :], in1=strict[:].unsqueeze(1).to_broadcast([128, HG, 128]), op=ALU.mult),
                         reads=[K_(E1, g), strict], writes=[K_(Es, g)])
                for g in grp:
                    pkq = nps1()
                    pkq_[g] = pkq
                    for j in range(HG // 2):
                        hk = hf * 4 + g * (HG // 2) + j
                        if full:
                            pe(lambda e, j=j, hk=hk, pkq=pkq: e.matmul(pkq.ap(j * 256, (j + 1) * 256), lhsT=kqT[:, 8 + hk, bs_], rhs=kqT[:, hk:hk + 9:8, bs_],
                                                                       start=True, stop=True), reads=[kqT], writes=[pkq.key])
                        else:
                            pe(lambda e, j=j, hk=hk, pkq=pkq: e.matmul(pkq.ap(j * 256 + 128, (j + 1) * 256), lhsT=kqT[:, 8 + hk, bs_], rhs=kqT[:, 8 + hk, bs_],
                                                                       start=True, stop=True), reads=[(kqT, 8 + hk)], writes=[pkq.key])
                kq4 = lambda ph: ph.ap(0, GW).rearrange("p (j c t) -> p j c t", j=HG // 2, c=2)
                if full:
                    for g in grp:
                        dve(lambda e, g=g: e.tensor_tensor(out=PT[:, hs[g], :].rearrange("p (j r) t -> p j r t", r=2),
                                                           in0=kq4(pkq_[g])[:, :, 0:1, :].to_broadcast([128, HG // 2, 2, 128]),
                                                           in1=E1[:, hs[g], :].rearrange("p (j r) t -> p j r t", r=2), op=ALU.mult),
                            reads=[pkq_[g].key, K_(E1, g)], writes=[K_(PT, g)])
                for g in grp:
                    for hh in range(HG):
                        hw = g * HG + hh
                        dve(lambda e, g=g, hh=hh, hw=hw: e.scalar_tensor_tensor(out=X032[:, hw, :], in0=kq4(pkq_[g])[:, hh // 2, 1, :], scalar=G(2)[:, h0 + hw:h0 + hw + 1],
                                                                               in1=Es[:, hw, :], op0=ALU.mult, op1=ALU.mult),
                            reads=[pkq_[g].key, gsm, K_(Es, g)], writes=[(X032, ("g", g, hh))])
                if full:
                    for g in grp:
                        act(lambda e, g=g: e.activation(out=Es[:, hs[g], :], in_=v3(pgr_[g]), func=AF.Exp), reads=[pgr_[g].key], writes=[K_(Es, g)])
                    for g in grp:
                        q4 = kqT[:, hf * 4 + g * (HG // 2):hf * 4 + (g + 1) * (HG // 2), bs_].unsqueeze(2).to_broadcast([128, HG // 2, 2, 128])
                        dve(lambda e, g=g, q4=q4: e.tensor_tensor(out=qdT[:, hs[g], :].rearrange("p (j r) t -> p j r t", r=2), in0=q4,
                                                                  in1=Es[:, hs[g], :].rearrange("p (j r) t -> p j r t", r=2), op=ALU.mult),
                            reads=[kqT, K_(Es, g)], writes=[K_(qdT, g)])
                for g in grp:
                    pxt = nps1()
                    for hh in range(HG):
                        pe(lambda e, g=g, hh=hh, pxt=pxt: e.transpose(pxt.ap(hh * 128, (hh + 1) * 128), X032[:, g * HG + hh, :], identF[:]),
                           reads=[K_(X032, g), identF], writes=[pxt.key])
                    act(lambda e, g=g, pxt=pxt: e.copy(out=fl(XT32, g), in_=pxt.ap(0, GW)), reads=[pxt.key], writes=[K_(XT32, g)])
                for g in grp:
                    dve(lambda e, g=g: e.tensor_tensor(out=P32[:, hs[g], :], in0=X032[:, hs[g], :], in1=identF[:].unsqueeze(1).to_broadcast([128, HG, 128]), op=ALU.add),
                        reads=[K_(X032, g), identF], writes=[K_(P32, g)])
                for g in grp:
                    px = nps1()
                    for hh in range(HG):
                        hw = g * HG + hh
                        pe(lambda e, hh=hh, hw=hw, px=px: e.matmul(px.ap(hh * 128, (hh + 1) * 128), lhsT=XT32[:, hw, :], rhs=X032[:, hw, :], start=True, stop=True),
                           reads=[K_(XT32, g), K_(X032, g)], writes=[px.key])
                    act(lambda e, g=g, px=px: e.copy(out=fl(X132, g), in_=px.ap(0, GW)), reads=[px.key], writes=[K_(X132, g)])
                for g in grp:
                    act(lambda e, g=g: e.copy(out=Xa[1][:, hs[g], :], in_=X132[:, hs[g], :]), reads=[K_(X132, g)], writes=[K_(Xa[1], g)])
                for g in grp:
                    pxT = nps1()
                    for hh in range(HG):
                        hw = g * HG + hh
                        pe(lambda e, hh=hh, hw=hw, pxT=pxT: e.matmul(pxT.ap(hh * 128, (hh + 1) * 128), lhsT=X032[:, hw, :], rhs=XT32[:, hw, :], start=True, stop=True),
                           reads=[K_(XT32, g), K_(X032, g)], writes=[pxT.key])
                    dve(lambda e, g=g, pxT=pxT: e.tensor_copy(out=fl(Xt[1], g), in_=pxT.ap(0, GW)), reads=[pxT.key], writes=[K_(Xt[1], g)])
                for g in grp:
                    ppm = nps1()
                    for hh in range(HG):
                        hw = g * HG + hh
                        pe(lambda e, hh=hh, hw=hw, ppm=ppm: e.matmul(ppm.ap(hh * 128, (hh + 1) * 128), lhsT=XT32[:, hw, :], rhs=X132[:, hw, :], start=True, stop=True),
                           reads=[K_(XT32, g), K_(X132, g)], writes=[ppm.key])
                    dve(lambda e, g=g: e.tensor_tensor(out=P32[:, hs[g], :], in0=P32[:, hs[g], :], in1=X132[:, hs[g], :], op=ALU.add),
                        reads=[K_(P32, g), K_(X132, g)], writes=[K_(P32, g)])
                    dve(lambda e, g=g, ppm=ppm: e.tensor_tensor(out=fl(P32, g), in0=ppm.ap(0, GW), in1=fl(P32, g), op=ALU.add),
                        reads=[ppm.key, K_(P32, g)], writes=[K_(P32, g)])
                    act(lambda e, g=g: e.copy(out=Pm[1][:, hs[g], :], in_=P32[:, hs[g], :]), reads=[K_(P32, g)], writes=[K_(Pm[1], g)])
                cur = 1
                for lev in range(2, 6):
                    nxt = 1 - cur
                    if lev < 5:
                        for g in grp:
                            px = nps1()
                            for hh in range(HG):
                                hw = g * HG + hh
                                pe(lambda e, hh=hh, hw=hw, px=px, cur=cur: e.matmul(px.ap(hh * 128, (hh + 1) * 128), lhsT=Xt[cur][:, hw, :], rhs=Xa[cur][:, hw, :],
                                                                                    start=True, stop=True), reads=[K_(Xt[cur], g), K_(Xa[cur], g)], writes=[px.key])
                            act(lambda e, g=g, px=px, nxt=nxt: e.copy(out=fl(Xa[nxt], g), in_=px.ap(0, GW)), reads=[px.key], writes=[K_(Xa[nxt], g)])
                    pxTs = {}
                    for g in grp:
                        pxT = nps1()
                        for hh in range(HG):
                            hw = g * HG + hh
                            pe(lambda e, hh=hh, hw=hw, pxT=pxT, cur=cur: e.matmul(pxT.ap(hh * 128, (hh + 1) * 128), lhsT=Xa[cur][:, hw, :], rhs=Xt[cur][:, hw, :],
                                                                                  start=True, stop=True), reads=[K_(Xt[cur], g), K_(Xa[cur], g)], writes=[pxT.key])
                        dve(lambda e, g=g, pxT=pxT, nxt=nxt: e.tensor_copy(out=fl(Xt[nxt], g), in_=pxT.ap(0, GW)), reads=[pxT.key], writes=[K_(Xt[nxt], g)])
                    for g in grp:
                        ppm = nps1()
                        for hh in range(HG):
                            hw = g * HG + hh
                            pe(lambda e, hh=hh, hw=hw, ppm=ppm, nxt=nxt, cur=cur: e.matmul(ppm.ap(hh * 128, (hh + 1) * 128), lhsT=Xt[nxt][:, hw, :], rhs=Pm[cur][:, hw, :],
                                                                                           start=True, stop=True), reads=[K_(Xt[nxt], g), K_(Pm[cur], g)], writes=[ppm.key])
                        dve(lambda e, g=g, ppm=ppm: e.tensor_tensor(out=fl(P32, g), in0=ppm.ap(0, GW), in1=fl(P32, g), op=ALU.add),
                            reads=[ppm.key, K_(P32, g)], writes=[K_(P32, g)])
                        act(lambda e, g=g, nxt=nxt: e.copy(out=Pm[nxt][:, hs[g], :], in_=P32[:, hs[g], :]), reads=[K_(P32, g)], writes=[K_(Pm[nxt], g)])
                    cur = nxt
                Tt = Pm[cur]
                dbg("Tt", Tt, Tt[:], [128, 8, 128], BF16)
                for g in grp:
                    kc0 = hf * 512 + g * (HG // 2) * 128
                    k4 = ktok[:, b, kc0:kc0 + (HG // 2) * 128].rearrange("p (j d) -> p j d", j=HG // 2).unsqueeze(2).to_broadcast([128, HG // 2, 2, 128])
                    for dst_, gi in ((kg, 5), (kd, 6)):
                        pool(lambda e, g=g, k4=k4, dst_=dst_, gi=gi: e.tensor_tensor(
                            out=dst_[:, hs[g], :].rearrange("p (j r) d -> p j r d", r=2), in0=k4,
                            in1=G(gi)[:, hgl[g]].rearrange("p (j r) -> p j r", r=2).unsqueeze(3).to_broadcast([128, HG // 2, 2, 128]), op=ALU.mult),
                            reads=[(ktok, b), gsm], writes=[K_(dst_, g)])
                for g in grp:
                    pw = nps1()
                    for hh in range(HG):
                        hw = g * HG + hh
                        pe(lambda e, hh=hh, hw=hw, pw=pw, Tt=Tt: e.matmul(pw.ap(hh * 128, (hh + 1) * 128), lhsT=kg[:, hw, :], rhs=Tt[:, hw, :], start=True, stop=True),
                           reads=[K_(kg, g), K_(Tt, g)], writes=[pw.key])
                    act(lambda e, g=g, pw=pw: e.mul(out=fl(nWT, g), in_=pw.ap(0, GW), mul=-1.0), reads=[pw.key], writes=[K_(nWT, g)])
                for ch in range(2):
                    si = ch if kind == 1 else 0
                    r_ = slice(ch * 64, ch * 64 + 64)
                    SK = lambda t, g: (t, ("s", hf, g))
                    for g in grp:
                        pd = nps1()
                        for hh in range(HG):
                            hw = g * HG + hh
                            hg = h0 + hw
                            pe(lambda e, hh=hh, hw=hw, hg=hg, pd=pd, Tt=Tt, r_=r_: e.matmul(pd.ap(hh * 128, (hh + 1) * 128, r_), lhsT=Tt[r_, hw, r_], rhs=vtok[r_, b, hg * 128:(hg + 1) * 128],
                                                                                           start=True, stop=False), reads=[K_(Tt, g), (vtok, b)], writes=[pd.key])
                            pe(lambda e, hh=hh, hw=hw, hg=hg, pd=pd, r_=r_, si=si: e.matmul(pd.ap(hh * 128, (hh + 1) * 128, r_), lhsT=nWT[:, hw, r_], rhs=Sb[si][:, hg, :],
                                                                                           start=False, stop=True), reads=[K_(nWT, g), SK(Sb[si], g)], writes=[pd.key])
                        dve(lambda e, g=g, pd=pd, r_=r_: e.tensor_tensor(out=dlt[r_, hs[g], :], in0=pd.ap(0, GW, r_).rearrange("p (h d) -> p h d", h=HG),
                                                                         in1=G(1)[r_, hgl[g]].unsqueeze(2).to_broadcast([64, HG, 128]), op=ALU.mult),
                            reads=[pd.key, gsm], writes=[(dlt, (ch, g))])
                    pos = {}
                    if full:
                        for g in grp:
                            po = nps1()
                            pos[g] = po
                            for hh in range(HG):
                                hw = g * HG + hh
                                hg = h0 + hw
                                pe(lambda e, hh=hh, hw=hw, hg=hg, po=po, r_=r_, si=si: e.matmul(po.ap(hh * 128, (hh + 1) * 128, r_), lhsT=qdT[:, hw, r_], rhs=Sb[si][:, hg, :],
                                                                                               start=True, stop=False), reads=[K_(qdT, g), SK(Sb[si], g)], writes=[po.key])
                                pe(lambda e, hh=hh, hw=hw, po=po, r_=r_: e.matmul(po.ap(hh * 128, (hh + 1) * 128, r_), lhsT=PT[r_, hw, r_], rhs=dlt[r_, hw, :],
                                                                                 start=False, stop=True), reads=[K_(PT, g), (dlt, (ch, g))], writes=[po.key])
                    for g in grp:
                        psu = nps1()
                        for hh in range(HG):
                            hw = g * HG + hh
                            pe(lambda e, hh=hh, hw=hw, psu=psu, r_=r_: e.matmul(psu.ap(hh * 128, (hh + 1) * 128), lhsT=kd[r_, hw, :], rhs=dlt[r_, hw, :], start=True, stop=True),
                               reads=[K_(kd, g), (dlt, (ch, g))], writes=[psu.key])
                        for hh in range(HG):
                            hg = h0 + g * HG + hh
                            dve(lambda e, hh=hh, hg=hg, psu=psu, si=si, ch=ch: e.scalar_tensor_tensor(out=Sf[si][:, hg, :], in0=Sf[si][:, hg, :],
                                                                                                     scalar=G(7 + ch)[:, hg:hg + 1], in1=psu.ap(hh * 128, (hh + 1) * 128),
                                                                                                     op0=ALU.mult, op1=ALU.add),
                                reads=[psu.key, gsm, (Sf[si], ("s", hf, g, hh))], writes=[(Sf[si], ("s", hf, g, hh))])
                        act(lambda e, g=g, si=si: e.copy(out=Sb[si][:, hgl[g], :], in_=Sf[si][:, hgl[g], :]), reads=[SK(Sf[si], g)], writes=[SK(Sb[si], g)])
                    if full:
                        o3s = {g: pos[g].ap(0, GW, r_).rearrange("p (h d) -> p h d", h=HG) for g in grp}
                        OKs = {g: (oss, (ch, g)) for g in grp}
                        for g in grp:
                            act(lambda e, g=g: e.activation(out=E1[r_, hs[g], :], in_=o3s[g], func=AF.Square), reads=[pos[g].key], writes=[(E1, ("o", ch, g))])
                        for g in grp:
                            dve(lambda e, g=g: e.tensor_reduce(out=oss[r_, 0, hs[g]], in_=E1[r_, hs[g], :], axis=AX.X, op=ALU.add), reads=[(E1, ("o", ch, g))], writes=[OKs[g]])
                        for g in grp:
                            act(lambda e, g=g: e.activation(out=oss[r_, 1, hs[g]], in_=oss[r_, 0, hs[g]], func=AF.Ln, scale=1.0 / 128, bias=EPS), reads=[OKs[g]], writes=[OKs[g]])
                        for g in grp:
                            act(lambda e, g=g: e.activation(out=oss[r_, 2, hs[g]], in_=oss[r_, 1, hs[g]], func=AF.Exp, scale=-0.5), reads=[OKs[g]], writes=[OKs[g]])
                        for g in grp:
                            dve(lambda e, g=g: e.tensor_tensor(out=onb[r_, hs[g], :], in0=o3s[g], in1=oss[r_, 2, hs[g]].unsqueeze(2).to_broadcast([64, HG, 128]),
                                                               op=ALU.mult), reads=[pos[g].key, OKs[g]], writes=[(onb, (ch, g))])
                    if kind == 1 and hf == 1:
                        sq_ = seqs[0] + b * 2 + ch
                        P.dma(sgs.h.ap()[sq_].rearrange("h k v -> k h v"), Sf[si][:], reads=[Sf[si]], writes=[sgs])
                dbg("onb", onb, onb[:], [128, 8, 128], BF16)
                if full:
                    for g in grp:
                        pot = nps1()
                        for hh in range(HG):
                            pe(lambda e, g=g, hh=hh, pot=pot: e.transpose(pot.apb(hh * 128, (hh + 1) * 128), onb[:, g * HG + hh, :], identB[:]),
                               reads=[(onb, (0, g)), (onb, (1, g)), identB], writes=[pot.key])
                        a0 = h0 + g * HG
                        dve(lambda e, g=g, pot=pot, a0=a0: e.tensor_tensor(out=A1[:, a0:a0 + HG, bs_], in0=pot.apb(0, GW).rearrange("p (h t) -> p h t", h=HG),
                                                                           in1=A1[:, a0:a0 + HG, bs_], op=ALU.mult),
                            reads=[pot.key] + [(A1, a0 + i) for i in range(HG)], writes=[(A1, a0 + i) for i in range(HG)])
        if kind == 0 and last and full:
            P.dma(sgp.h.ap().rearrange("h k v -> k h v"), Sf[0][:], reads=[Sf[0]], writes=[sgp])
        if not full:
            return
        linear_fm("w_gdn_out", 0, D, 16, A1, lambda k: A1[:, k, 0:NTt], NTt, resid_evac(NTt))
        ffn(1, NTt)
        act(lambda e: e.activation(out=hT[:, :, 0:NTt], in_=xT[:, :, 0:NTt], func=AF.Square), reads=[xT], writes=[hT])
        pt = nps()
        for k in range(8):
            pe(lambda e, k=k, pt=pt: e.matmul(pt.h[:, 0:NTt], lhsT=onesB[:], rhs=hT[:, k, 0:NTt], start=(k == 0), stop=(k == 7)), reads=[onesB, hT], writes=[pt])
        act(lambda e, pt=pt: e.activation(out=cacc[:, 0, 0:NTt], in_=pt.h[:, 0:NTt], func=AF.Ln, scale=1.0 / D, bias=EPS), reads=[pt], writes=[cacc])
        act(lambda e: e.activation(out=cacc[:, 0, 0:NTt], in_=cacc[:, 0, 0:NTt], func=AF.Exp, scale=-0.5), reads=[cacc], writes=[cacc])
        for b in range(nb):
            prs = nps()
            pe(lambda e, b=b, prs=prs: e.transpose(prs.h[:, 0:128], cacc[:, 0, b * 128:(b + 1) * 128], identF[:]), reads=[cacc, identF], writes=[prs])
            act(lambda e, prs=prs: e.copy(out=bmv[:, 3:4], in_=prs.h[:, 0:1]), reads=[prs], writes=[bmv])
            for hf in range(2):
                pt = nps()
                for j in range(4):
                    k = hf * 4 + j
                    pe(lambda e, pt=pt, j=j, k=k, b=b: e.transpose(pt.h[:, j * 128:(j + 1) * 128], xT[:, k, b * 128:(b + 1) * 128], identF[:]),
                       reads=[xT, identF], writes=[pt])
                dve(lambda e, pt=pt, hf=hf: e.scalar_tensor_tensor(out=vtk[:, 1, hf * 512:(hf + 1) * 512], in0=pt.h[:, 0:512], scalar=bmv[:, 3:4],
                                                                   in1=wfin[:, hf * 512:(hf + 1) * 512], op0=ALU.mult, op1=ALU.mult),
                    reads=[pt, bmv, wfin], writes=[(vtk, 1)])
            P.dma(dst.h.ap()[tok0 + b * 128: tok0 + (b + 1) * 128, :], vtk[:, 1, 0:1024], reads=[(vtk, 1)], writes=[dst], eng="act")

    if n_ptiles:
        setup_sgu_kind(0)
    for ti in range(n_state):
        run_tile(0, ti * NT, NT, ti == 0, False, None, mode="state", need_q_halo=(ti == n_state - 1))
    for ti in range(n_ptiles):
        run_tile(0, ti * NT, NT, ti == 0, ti == n_ptiles - 1, None, apply_flag=(ti == 0 and n_state > 0))
    if NS:
        setup_sgu_kind(1)
        run_tile(1, 0, NSAMP, True, True, (0,))
    stats = P.emit()
    return nc, stats


WNAMES = ["norm_mix", "norm_ffn", "norm_final", "sgu_w_in", "sgu_ln_g", "sgu_ln_b", "sgu_w_s", "sgu_b_s", "sgu_w_out",
          "gdn_w_in", "gdn_w_conv", "gdn_a_log", "gdn_dt_bias", "gdn_w_onorm", "gdn_w_out", "ffn_w_gate", "ffn_w_up", "ffn_w_down"]
SQUEEZE0 = {"sgu_w_in", "sgu_ln_g", "sgu_ln_b", "sgu_w_s", "sgu_b_s", "sgu_w_out", "gdn_w_in", "gdn_w_conv", "gdn_a_log",
            "gdn_dt_bias", "gdn_w_onorm", "gdn_w_out"}

_CACHE = {}


def kernel(**inputs):
    x_prompt = np.ascontiguousarray(inputs["x_prompt"], dtype=np.float32)
    x_sample = np.ascontiguousarray(inputs["x_sample"], dtype=np.float32)
    state_gdn = np.ascontiguousarray(inputs["state_gdn"], dtype=np.float32)
    state_conv = np.ascontiguousarray(inputs["state_conv"], dtype=np.float32)
    B, SEQ, _ = x_prompt.shape
    DB, DS, _ = x_sample.shape
    n_cores = 8
    NT = 256
    NS = DB // n_cores
    HALF = SEQ // 2
    n_half = HALF // NT
    key = (n_half, NT, NS)
    if key not in _CACHE:
        _CACHE[key] = build_program(n_half, NT, NS, n_state=n_half)[0]
    nc = _CACHE[key]
    wd = {}
    for n in WNAMES:
        a = np.ascontiguousarray(inputs[n], dtype=np.float32)
        wd[n] = a[0] if n in SQUEEZE0 else a
    in_maps = []
    for c in range(n_cores):
        m = dict(wd)
        sq, second = c % B, c // B
        m["xq"] = x_prompt[sq, :HALF]
        m["xp"] = x_prompt[sq, HALF:] if second else x_prompt[sq, :HALF]
        m["flag"] = np.array([1.0 if second else 0.0], dtype=np.float32)
        m["xs"] = x_sample[c * NS:(c + 1) * NS].reshape(NS * DS, D)
        m["sg"] = state_gdn[0, c * NS:(c + 1) * NS]
        m["sc"] = state_conv[0, c * NS:(c + 1) * NS]
        in_maps.append(m)
    res = run_bass_kernel_spmd(nc, in_maps, core_ids=list(range(n_cores)))
    r = res.results
    y_prompt = np.stack([np.concatenate([r[c]["yp"], r[c + B]["yp"]], axis=0) for c in range(B)]).astype(np.float32)
    y_sample = np.concatenate([r[c]["ys"].reshape(NS, DS, D) for c in range(n_cores)]).astype(np.float32)
    ns_gdn_p = np.stack([r[c + B]["sgp"] for c in range(B)])[None].astype(np.float32)
    ns_conv_p = np.stack([r[c + B]["scp"] for c in range(B)])[None].astype(np.float32)
    ns_gdn_s = np.concatenate([r[c]["sgs"] for c in range(n_cores)])[None].astype(np.float32)
    ns_conv_s = np.concatenate([r[c]["scs"] for c in range(n_cores)])[None].astype(np.float32)
    ns_v_s = np.concatenate([r[c]["svs"].reshape(NS, DS, DSGU) for c in range(n_cores)])[None].astype(np.float32)
    return (y_prompt, y_sample, ns_gdn_p, ns_conv_p, ns_gdn_s, ns_conv_s, ns_v_s)
```
